# Optimizing a Trainium2 kernel written in Bass

```python
import jax, jax.numpy as jnp
from jax import lax
import numpy as np

D_MODEL = 2048
BATCH = 4
SEQ = 4096
DEPTH = 1

CHUNK = 64
N_LEFT_CHUNKS = 8
BAND = (N_LEFT_CHUNKS + 1) * CHUNK

ATT_WIDTH = D_MODEL // 2
ATT_HEADS = 8
ATT_HEAD_DIM = ATT_WIDTH // ATT_HEADS
MAX_REL = 128
REL_SIZE = 2 * MAX_REL + 1

GRN_WIDTH = D_MODEL - ATT_WIDTH
GRN_HEADS = 8
GRN_KEY_DIM = GRN_WIDTH // GRN_HEADS
GRN_VAL_DIM = GRN_WIDTH // GRN_HEADS

MIX_WIDTH = ATT_WIDTH + GRN_WIDTH
IN_COLS = 3 * ATT_WIDTH + 4 * GRN_WIDTH
D_FF = -(-8 * D_MODEL // (3 * 256)) * 256
EPS = 1e-6

kernel_name = "hybrid_chunk_attn_hgrn2_block"


def rmsnorm(x, gain):
    xf = x.astype(jnp.float32)
    y = xf * lax.rsqrt(jnp.mean(xf * xf, axis=-1, keepdims=True) + EPS)
    return (y * gain.astype(jnp.float32)).astype(x.dtype)


def chunk_band_attention(q, k, v, rel_bias):
    B, S, H, Dh = q.shape
    n_chunks = S // CHUNK
    qc = q.reshape(B, n_chunks, CHUNK, H, Dh)
    pad = ((0, 0), (N_LEFT_CHUNKS * CHUNK, 0), (0, 0), (0, 0))
    kp = jnp.pad(k, pad).reshape(B, n_chunks + N_LEFT_CHUNKS, CHUNK, H, Dh)
    vp = jnp.pad(v, pad).reshape(B, n_chunks + N_LEFT_CHUNKS, CHUNK, H, Dh)
    band_idx = jnp.arange(n_chunks)[:, None] + jnp.arange(N_LEFT_CHUNKS + 1)[None, :]
    kb = kp[:, band_idx].reshape(B, n_chunks, BAND, H, Dh)
    vb = vp[:, band_idx].reshape(B, n_chunks, BAND, H, Dh)
    scores = jnp.einsum('bcqhd,bckhd->bhcqk', qc, kb,
                        preferred_element_type=jnp.float32) * (Dh ** -0.5)
    dist = N_LEFT_CHUNKS * CHUNK + jnp.arange(CHUNK)[:, None] - jnp.arange(BAND)[None, :]
    rel_idx = jnp.clip(dist, -MAX_REL, MAX_REL) + MAX_REL
    bias = rel_bias.astype(jnp.float32)[:, rel_idx]
    scores = scores + bias[None, :, None]
    key_chunk = (jnp.arange(n_chunks)[:, None] - N_LEFT_CHUNKS
                 + jnp.arange(BAND)[None, :] // CHUNK)
    valid = key_chunk >= 0
    scores = jnp.where(valid[None, None, :, None, :], scores, jnp.finfo(jnp.float32).min)
    p = jax.nn.softmax(scores, axis=-1).astype(v.dtype)
    out = jnp.einsum('bhcqk,bckhd->bcqhd', p, vb)
    return out.reshape(B, S, H, Dh)


def hgrn2_mixer(q_raw, f_raw, i_val, g_raw, lb, gnorm_gain):
    B, S, _ = q_raw.shape
    n_chunks = S // CHUNK
    dtype = q_raw.dtype
    q = jax.nn.silu(q_raw.astype(jnp.float32))
    lbf = lb.astype(jnp.float32)
    f = lbf + (1.0 - lbf) * jax.nn.sigmoid(f_raw.astype(jnp.float32))
    k = 1.0 - f
    shp_k = (B, n_chunks, CHUNK, GRN_HEADS, GRN_KEY_DIM)
    q = q.reshape(shp_k)
    k = k.reshape(shp_k)
    b = jnp.cumsum(jnp.log(f).reshape(shp_k), axis=2)
    b_last = b[:, :, -1:]
    v = i_val.astype(jnp.float32).reshape(B, n_chunks, CHUNK, GRN_HEADS, GRN_VAL_DIM)
    q_dec = q * jnp.exp(b)
    a = jnp.einsum('bcthk,bcshk->bchts', q_dec, k * jnp.exp(-b))
    causal = jnp.tril(jnp.ones((CHUNK, CHUNK), dtype=bool))
    a = jnp.where(causal, a, 0.0)
    o_intra = jnp.einsum('bchts,bcshv->bcthv', a, v)
    u = jnp.einsum('bcshk,bcshv->bchkv', k * jnp.exp(b_last - b), v)
    decay = jnp.exp(b_last[:, :, 0])

    def step(state, inp):
        u_c, d_c = inp
        return d_c[..., None] * state + u_c, state

    s0 = jnp.zeros((B, GRN_HEADS, GRN_KEY_DIM, GRN_VAL_DIM), jnp.float32)
    _, s_start = lax.scan(step, s0, (jnp.swapaxes(u, 0, 1), jnp.swapaxes(decay, 0, 1)))
    s_start = jnp.swapaxes(s_start, 0, 1)
    o_inter = jnp.einsum('bcthk,bchkv->bcthv', q_dec, s_start)
    o = (o_intra + o_inter).reshape(B, S, GRN_HEADS, GRN_VAL_DIM)
    o = o * lax.rsqrt(jnp.mean(o * o, axis=-1, keepdims=True) + EPS) * gnorm_gain.astype(jnp.float32)
    o = o.reshape(B, S, GRN_HEADS * GRN_VAL_DIM) * jax.nn.silu(g_raw.astype(jnp.float32))
    return o.astype(dtype)


def setup_inputs(seed: int = 0) -> dict:
    key = jax.random.key(seed)
    ks = jax.random.split(key, 16)
    f32 = jnp.float32
    x = jax.random.normal(ks[0], (BATCH, SEQ, D_MODEL), f32)
    norm1_gain = 1.0 + 0.02 * jax.random.normal(ks[1], (DEPTH, D_MODEL), f32)
    w_in = jax.random.normal(ks[2], (DEPTH, D_MODEL, IN_COLS), f32) * D_MODEL ** -0.5
    rel_bias = 0.1 * jax.random.normal(ks[3], (DEPTH, ATT_HEADS, REL_SIZE), f32)
    lower_bounds = 0.1 * jax.random.normal(ks[4], (DEPTH + 1, GRN_WIDTH), f32)
    grn_norm_gain = 1.0 + 0.02 * jax.random.normal(ks[5], (DEPTH, GRN_VAL_DIM), f32)
    attn_out_gain = 1.0 + 0.02 * jax.random.normal(ks[6], (DEPTH, ATT_WIDTH), f32)
    w_out = jax.random.normal(ks[7], (DEPTH, MIX_WIDTH, D_MODEL), f32) * MIX_WIDTH ** -0.5
    norm2_gain = 1.0 + 0.02 * jax.random.normal(ks[8], (DEPTH, D_MODEL), f32)
    w_gate = jax.random.normal(ks[9], (DEPTH, D_MODEL, D_FF), f32) * D_MODEL ** -0.5
    w_up = jax.random.normal(ks[10], (DEPTH, D_MODEL, D_FF), f32) * D_MODEL ** -0.5
    w_down = jax.random.normal(ks[11], (DEPTH, D_FF, D_MODEL), f32) * D_FF ** -0.5
    final_gain = 1.0 + 0.02 * jax.random.normal(ks[12], (D_MODEL,), f32)
    return {"x": x, "norm1_gain": norm1_gain, "w_in": w_in, "rel_bias": rel_bias,
            "lower_bounds": lower_bounds, "grn_norm_gain": grn_norm_gain,
            "attn_out_gain": attn_out_gain, "w_out": w_out, "norm2_gain": norm2_gain,
            "w_gate": w_gate, "w_up": w_up, "w_down": w_down, "final_gain": final_gain}


def reference(x, norm1_gain, w_in, rel_bias, lower_bounds, grn_norm_gain, attn_out_gain,
              w_out, norm2_gain, w_gate, w_up, w_down, final_gain):
    B, S, _ = x.shape
    lb_all = jnp.cumsum(jax.nn.softmax(lower_bounds.astype(jnp.float32), axis=0), axis=0)
    for l in range(DEPTH):
        h = rmsnorm(x, norm1_gain[l])
        proj = h @ w_in[l]
        o1 = ATT_WIDTH
        q_a = proj[..., 0:o1].reshape(B, S, ATT_HEADS, ATT_HEAD_DIM)
        k_a = proj[..., o1:2 * o1].reshape(B, S, ATT_HEADS, ATT_HEAD_DIM)
        v_a = proj[..., 2 * o1:3 * o1].reshape(B, S, ATT_HEADS, ATT_HEAD_DIM)
        base = 3 * ATT_WIDTH
        q_r = proj[..., base:base + GRN_WIDTH]
        f_r = proj[..., base + GRN_WIDTH:base + 2 * GRN_WIDTH]
        i_r = proj[..., base + 2 * GRN_WIDTH:base + 3 * GRN_WIDTH]
        g_r = proj[..., base + 3 * GRN_WIDTH:base + 4 * GRN_WIDTH]
        attn = chunk_band_attention(q_a, k_a, v_a, rel_bias[l]).reshape(B, S, ATT_WIDTH)
        attn = rmsnorm(attn, attn_out_gain[l])
        rec = hgrn2_mixer(q_r, f_r, i_r, g_r, lb_all[l], grn_norm_gain[l])
        mixed = jnp.concatenate([attn, rec.astype(attn.dtype)], axis=-1)
        x = x + (mixed @ w_out[l]).astype(x.dtype)
        h2 = rmsnorm(x, norm2_gain[l])
        ff = jax.nn.silu(h2 @ w_gate[l]) * (h2 @ w_up[l])
        x = x + (ff @ w_down[l]).astype(x.dtype)
    return rmsnorm(x, final_gain)
```

```python
import contextlib
import os
import numpy as np
import ml_dtypes
import concourse.bass as bass
import concourse.mybir as mybir
from concourse.bass_utils import run_bass_kernel_spmd

F32 = mybir.dt.float32
BF16 = mybir.dt.bfloat16
AF = mybir.ActivationFunctionType
ALU = mybir.AluOpType
AX = mybir.AxisListType

ENGS = ("pe", "act", "dve", "pool", "sp")

D = 2048
NT = 2048
NHIST = 2048
KD = 16
DFF = 5632
NFT = 44
EPS = 1e-6
INCOLS = 7168
TB = 512
SCALE = 128 ** -0.5
SQ128 = 128 ** 0.5


class Sched:
    def __init__(self, nc, same_engine_sync=True):
        self.nc = nc
        self.ops = []
        self.last_writer = {}
        self.readers = {}
        self.same_engine_sync = same_engine_sync
        self.dma_keys = {}

    def op(self, eng, fn, reads=(), writes=(), dma_key=None):
        idx = len(self.ops)
        deps = set()
        for k in reads:
            w = self.last_writer.get(k)
            if w is not None:
                deps.add(w)
        for k in writes:
            w = self.last_writer.get(k)
            if w is not None:
                deps.add(w)
            for r in self.readers.get(k, ()):
                deps.add(r)
        if dma_key is not None:
            prev = self.dma_keys.get(dma_key)
            if prev is not None:
                deps.add(prev)
            self.dma_keys[dma_key] = idx
        deps.discard(idx)
        self.ops.append(dict(eng=eng, fn=fn, deps=deps, dma_key=dma_key, needed=False))
        for k in writes:
            self.last_writer[k] = idx
            self.readers[k] = []
        for k in reads:
            if k not in writes:
                lst = self.readers.setdefault(k, [])
                if dma_key is None:
                    lst[:] = [r for r in lst if not (self.ops[r]["eng"] == eng and self.ops[r]["dma_key"] is None)]
                lst.append(idx)
        return idx

    def barrier(self):
        last = {}
        for i, o in enumerate(self.ops):
            last[("e", o["eng"])] = i
            if o["dma_key"] is not None:
                last[("d", o["dma_key"])] = i
        deps = set(last.values())
        for e in ENGS:
            self.ops.append(dict(eng=e, fn=None, deps=set(deps), dma_key=None, needed=False))

    def emit(self, final_waits=()):
        nc = self.nc
        ops = self.ops
        for i, o in enumerate(ops):
            nd = set()
            for d in o["deps"]:
                p = ops[d]
                if p["dma_key"] is None and p["eng"] == o["eng"] and not self.same_engine_sync:
                    continue
                if p["dma_key"] is None and p["eng"] == "pe" and o["eng"] == "pe":
                    continue
                nd.add(d)
            o["deps"] = nd
            for d in nd:
                ops[d]["needed"] = True
        for d in final_waits:
            ops[d]["needed"] = True
        tick = {e: 0 for e in ENGS}
        dma_cnt = {}
        for o in ops:
            if o["dma_key"] is not None:
                dma_cnt[o["dma_key"]] = dma_cnt.get(o["dma_key"], 0) + 1
                o["sem"] = ("dma", o["dma_key"])
                o["tick"] = 16 * dma_cnt[o["dma_key"]]
            elif o["needed"]:
                tick[o["eng"]] += 1
                o["sem"] = ("eng", o["eng"])
                o["tick"] = tick[o["eng"]]
        sem_names = [("eng", e) for e in ENGS] + [("dma", k) for k in dma_cnt]
        with contextlib.ExitStack() as st:
            sems = {}
            for i, sn in enumerate(sem_names):
                sems[sn] = st.enter_context(nc.semaphore("s%d" % i))
            block = st.enter_context(nc.Block())
            per_eng = {e: [] for e in ENGS}
            for i, o in enumerate(ops):
                per_eng[o["eng"]].append(i)

            def body(ename, engine):
                waited = {}
                for i in per_eng[ename]:
                    o = ops[i]
                    need = {}
                    for d in o["deps"]:
                        p = ops[d]
                        s = p["sem"]
                        need[s] = max(need.get(s, 0), p["tick"])
                    for s, v in need.items():
                        if waited.get(s, 0) >= v:
                            continue
                        engine.wait_ge(sems[s], v)
                        waited[s] = v
                    if o["fn"] is None:
                        continue
                    ins = o["fn"](engine)
                    if o["dma_key"] is not None:
                        ins.then_inc(sems[o["sem"]], 16)
                    elif o["needed"]:
                        ins.then_inc(sems[o["sem"]], 1)
                if ename == "sp":
                    need = {}
                    for d in final_waits:
                        p = ops[d]
                        need[p["sem"]] = max(need.get(p["sem"], 0), p["tick"])
                    for s, v in need.items():
                        engine.wait_ge(sems[s], v)

            @block.tensor
            def _(e):
                body("pe", e)

            @block.scalar
            def _(e):
                body("act", e)

            @block.vector
            def _(e):
                body("dve", e)

            @block.gpsimd
            def _(e):
                body("pool", e)

            @block.sync
            def _(e):
                body("sp", e)
        return {e: len(per_eng[e]) for e in ENGS}, tick, len(sem_names)


def build_program(debug=None, stop_after=None):
    nc = bass.Bass("TRN2", target_bir_lowering=False)
    dt_in = lambda name, shape: nc.dram_tensor(name, shape, F32, kind="ExternalInput").ap()
    xw = dt_in("xw", [NHIST + NT, D])
    w_in = dt_in("w_in", [D, INCOLS])
    w_out = dt_in("w_out", [D, D])
    w_gate = dt_in("w_gate", [D, DFF])
    w_up = dt_in("w_up", [D, DFF])
    w_down = dt_in("w_down", [DFF, D])
    g1_d = dt_in("g1", [1, D])
    g2_d = dt_in("g2", [1, D])
    gf_d = dt_in("gf", [1, D])
    gat_d = dt_in("gat", [1, 1024])
    ggr_d = dt_in("ggr", [1, 128])
    lbnd_d = dt_in("lbnd", [2, 1024])
    btab_d = dt_in("btab", [8, 128, 640])
    cid_d = dt_in("c_ident", [128, 128])
    cam_d = dt_in("c_amask", [128, 640])
    ccm_d = dt_in("c_cmask", [128, 512])
    csm_d = dt_in("c_smask", [128, 512])
    chn_d = dt_in("c_hneg", [128, 1])
    y = nc.dram_tensor("y", [NT, D], F32, kind="ExternalOutput").ap()
    md = nc.dram_tensor("md", [16, 128, NT], BF16, kind="Internal").ap()
    kvd = nc.dram_tensor("kvd", [8, 128, 1024], BF16, kind="Internal").ap()
    dbg = {}
    if debug:
        for name, shape in debug.items():
            dbg[name] = nc.dram_tensor("dbg_" + name, shape, F32, kind="ExternalOutput").ap()

    S = Sched(nc)
    finals = []
    with contextlib.ExitStack() as st:
        def sb(name, shape, dt=F32):
            return st.enter_context(nc.sbuf_tensor(name, shape, dt))

        def ps(name, shape, dt=F32):
            return st.enter_context(nc.psum_tensor(name, shape, dt))

        R2 = sb("R2", [128, 16, 2048], BF16)
        R1 = sb("R1", [128, 28672], BF16)
        NSLOT = 3
        wpool = sb("wpool", [128, NSLOT, 8192], BF16)
        ident = sb("ident", [128, 128], BF16)
        identF = sb("identF", [128, 128], F32)
        onesF = sb("onesF", [128, 128], F32)
        amask = sb("amask", [128, 640], F32)
        cmask = sb("cmask", [128, 512], F32)
        smask = sb("smask", [128, 512], F32)
        hneg = sb("hneg", [128, 1], F32)
        g1c = sb("g1c", [128, 16], F32)
        g2c = sb("g2c", [128, 16], F32)
        gfb = sb("gfb", [128, 2048], F32)
        gatc = sb("gatc", [128, 8], F32)
        ggrc = sb("ggrc", [128, 1], F32)
        lbc = sb("lbc", [128, 8], F32)
        lb1 = sb("lb1", [128, 8], F32)
        omlc = sb("omlc", [128, 8], F32)
        nomlc = sb("nomlc", [128, 8], F32)
        junk = sb("junk", [128, 2048], BF16)
        hb = sb("hb", [128, 2048], BF16)
        silb = sb("silb", [128, 2, 512], F32)
        ss = sb("ss", [128, 64], F32)
        sd = sb("sd", [128, 64], F32)
        rstd = sb("rstd", [128, 64], F32)
        ssa = sb("ssa", [128, 16, 8], F32)
        rsa = sb("rsa", [128, 16], F32)

        def carve(off, shape, dt):
            n = 1
            for s_ in shape[1:]:
                n *= s_
            esz = 2 if dt == BF16 else 4
            a = R1[:, off // 2: off // 2 + n * esz // 2]
            if dt == F32:
                a = a.bitcast(F32)
            if len(shape) == 3:
                a = a.rearrange("p (a b) -> p a b", a=shape[1])
            elif len(shape) == 4:
                a = a.rearrange("p (a b c) -> p a b c", a=shape[1], b=shape[2])
            return a

        xin = carve(0, [128, 2048], F32)
        Sst = carve(8192, [128, 2, 8, 128], F32)
        Sb = carve(16384, [128, 8, 128], BF16)
        tab = carve(18432, [128, 8, 640], BF16)
        WK = 28672
        KB2 = 2048
        w_sgn = carve(WK + 0 * KB2, [128, 512], F32)
        w_f = carve(WK + 1 * KB2, [128, 512], F32)
        w_b = carve(WK + 2 * KB2, [128, 512], F32)
        w_e0 = carve(WK + 3 * KB2, [128, 512], F32)
        w_e1 = carve(WK + 4 * KB2, [128, 512], F32)
        w_e = carve(WK + 5 * KB2, [128, 512], F32)
        w_qs = carve(WK + 6 * KB2, [128, 512], F32)
        w_gs0 = carve(WK + 7 * KB2, [128, 512], F32)
        o = WK + 8 * KB2
        w_qd0 = carve(o, [128, 512], BF16); o += 1024
        w_kd0 = carve(o, [128, 512], BF16); o += 1024
        w_kk0 = carve(o, [128, 512], BF16); o += 1024
        w_kk1 = carve(o, [128, 512], BF16); o += 1024
        w_vi = carve(o, [128, 512], BF16); o += 1024
        w_tm0 = carve(o, [128, 12, 128], BF16); o += 3072
        w_am = carve(o, [128, 512], BF16); o += 1024
        mst = carve(o, [128, 2, 512], BF16); o += 2048
        w_eb8 = carve(o, [128, 2, 8], F32); o += 64
        assert o <= 57344, o
        o = 18432
        w_gs1 = carve(o, [128, 512], F32); o += 2048
        w_X = carve(o, [128, 512], F32); o += 2048
        w_qd1 = carve(o, [128, 512], BF16); o += 1024
        w_kd1 = carve(o, [128, 512], BF16); o += 1024
        w_tm1 = carve(o, [128, 12, 128], BF16); o += 3072
        assert o <= 28672, o
        w_gsP = (w_gs0, w_gs1)
        w_qdP = (w_qd0, w_qd1)
        w_kdP = (w_kd0, w_kd1)
        w_tmP = (w_tm0, w_tm1)
        o = WK
        a_q = carve(o, [128, 2048], BF16); o += 4096
        a_k = carve(o, [128, 2560], BF16); o += 5120
        a_vT = carve(o, [128, 2, 512], BF16); o += 2048
        a_v = carve(o, [128, 20, 132], BF16); o += 5280
        a_p = carve(o, [128, 2, 640], BF16); o += 2560
        a_ab = carve(o, [128, 2, 128], BF16); o += 512
        a_jk = carve(o, [128, 128], BF16); o += 256
        a_rd = carve(o, [128, 2], F32); o += 8
        o = (o + 63) // 64 * 64
        mst_a = carve(o, [128, 2, 512], BF16); o += 2048
        assert o <= 57344, o
        mb = carve(0, [128, 16, 512], BF16)
        x1 = carve(16384, [128, 4, 2048], F32)
        yo = carve(49152, [128, 2048], F32)
        r2f = R2[:].rearrange("p k t -> p (k t)")
        ffT = r2f[:, 0:22528].rearrange("p (f t) -> p f t", f=NFT)
        h2T = r2f[:, 22528:30720].rearrange("p (k t) -> p k t", k=16)

        p_acc = [ps("pacc%d" % i, [128, 512]) for i in range(4)]
        p_gu = [ps("pgu%d" % i, [128, 512]) for i in range(3)]
        p_tr = ps("ptr", [128, 8, 128], BF16)

        def dma(eng, out, in_, reads, writes, key, slow=False):
            if slow:
                return S.op(eng, lambda e: e.dma_start(out=out, in_=in_, allow_slow_non_contiguous=True), reads=reads, writes=writes, dma_key=key)
            return S.op(eng, lambda e: e.dma_start(out=out, in_=in_), reads=reads, writes=writes, dma_key=key)

        dma("sp", identF[:], cid_d[:, :], [], ["identF"], "c0")
        dma("sp", amask[:], cam_d[:, :], [], ["amask"], "c1")
        dma("sp", cmask[:], ccm_d[:, :], [], ["cmask"], "c2")
        dma("sp", smask[:], csm_d[:, :], [], ["smask"], "c3")
        dma("sp", hneg[:], chn_d[:, :], [], ["hneg"], "c4")
        dma("sp", gfb[:], gf_d.broadcast_to([128, D]), [], ["gfb"], "c5")
        cvec = lambda d_: d_.rearrange("o (k p) -> p (o k)", p=128)
        dma("sp", g1c[:], cvec(g1_d), [], ["g1c"], "c6", slow=True)
        dma("sp", g2c[:], cvec(g2_d), [], ["g2c"], "c7", slow=True)
        dma("sp", gatc[:], cvec(gat_d), [], ["gatc"], "c8", slow=True)
        dma("sp", ggrc[:], cvec(ggr_d), [], ["ggrc"], "c9", slow=True)
        dma("sp", lbc[:], cvec(lbnd_d[0:1, :]), [], ["lbc"], "c10", slow=True)
        dma("sp", lb1[:], cvec(lbnd_d[1:2, :]), [], ["lb1"], "c11", slow=True)
        S.op("dve", lambda e: e.tensor_copy(out=ident[:], in_=identF[:]), ["identF"], ["ident"])
        S.op("dve", lambda e: e.memset(onesF[:], 1.0), [], ["onesF"])
        S.op("dve", lambda e: e.memset(Sst, 0.0), [], [("S", 0, h) for h in range(8)] + [("S", 1, h) for h in range(8)])
        S.op("dve", lambda e: e.memset(w_e0, 0.0), [], ["e0"])
        S.op("dve", lambda e: e.memset(w_e1, 0.0), [], ["e1"])
        S.op("dve", lambda e: e.tensor_tensor(out=lbc[:], in0=lbc[:], in1=lb1[:], op=ALU.subtract), ["lbc", "lb1"], ["lbc"])
        S.op("act", lambda e: e.activation(out=lbc[:], in_=lbc[:], func=AF.Sigmoid), ["lbc"], ["lbc"])
        S.op("dve", lambda e: e.tensor_scalar(out=omlc[:], in0=lbc[:], scalar1=-1.0, scalar2=1.0, op0=ALU.mult, op1=ALU.add), ["lbc"], ["omlc"])
        S.op("dve", lambda e: e.tensor_scalar(out=nomlc[:], in0=lbc[:], scalar1=1.0, scalar2=-1.0, op0=ALU.mult, op1=ALU.add), ["lbc"], ["nomlc"])
        def build_tab():
            for h in range(8):
                dma("sp", xin[:, 0:640], btab_d[h], [], ["xin"], "xin")
                S.op("dve", lambda e, h=h: e.scalar_tensor_tensor(out=tab[:, h, :], in0=xin[:, 0:640], scalar=SQ128, in1=amask[:],
                                                                  op0=ALU.mult, op1=ALU.add), ["xin", "amask"], [("tab", h)])

        wstate = {"n": 0}

        def wslot():
            s = wstate["n"] % NSLOT
            wstate["n"] += 1
            return s

        def wview(slot):
            return wpool[:, slot, :].rearrange("p (k c) -> p k c", k=16)

        def wkeys(slot, q0=0, q1=4):
            return [("w", slot, q) for q in range(q0, q1)]

        def wload_cols(slot, dst_off, src, col0, ncols):
            q0, q1 = dst_off // 128, (dst_off + ncols) // 128
            dma("pool", wview(slot)[:, :, dst_off:dst_off + ncols],
                src[:, col0:col0 + ncols].rearrange("(k p) c -> p k c", p=128),
                [], wkeys(slot, q0, q1), ("w", slot, q0))

        ncount = {"n": 0}

        def norm_to_T(src_ap, src_reads, dstT, dname, col0, gain_c, gkey):
            i = ncount["n"] % 32
            ncount["n"] += 1
            S.op("act", lambda e: e.activation(out=junk[:], in_=src_ap, func=AF.Square, accum_out=ss[:, i:i + 1]),
                 src_reads, ["junk", ("ss", i)])
            S.op("act", lambda e: e.activation(out=sd[:, i:i + 1], in_=ss[:, i:i + 1], func=AF.Sqrt, scale=1.0 / D, bias=EPS),
                 [("ss", i)], [("sd", i)])
            S.op("dve", lambda e: e.reciprocal(out=rstd[:, i:i + 1], in_=sd[:, i:i + 1]), [("sd", i)], [("rstd", i)])
            S.op("dve", lambda e: e.tensor_scalar(out=hb[:], in0=src_ap, scalar1=rstd[:, i:i + 1], scalar2=None, op0=ALU.mult),
                 src_reads + [("rstd", i)], ["hb"])
            for half in range(2):
                for j in range(8):
                    k = half * 8 + j
                    S.op("pe", lambda e, k=k, j=j: e.transpose(out=p_tr[:, j, :], in_=hb[:, k * 128:(k + 1) * 128], identity=ident[:]),
                         ["hb", "ident"], ["ptr"])
                S.op("dve", lambda e, half=half: e.tensor_tensor(
                    out=dstT[:, half * 8:half * 8 + 8, col0:col0 + 128], in0=p_tr[:],
                    in1=gain_c[:, half * 8:half * 8 + 8].unsqueeze(2).broadcast_to([128, 8, 128]), op=ALU.mult),
                    ["ptr", gkey], [("T", dname, col0, half)])

        def tkeys(dname, tok0, ntok):
            return [("T", dname, c, half) for c in range(tok0, tok0 + ntok, 128) for half in range(2)]

        def load_norm_x(ti, col0):
            dma("sp", xin, xw[ti * 128:(ti + 1) * 128, :], [], ["xin"], "xin")
            norm_to_T(xin, ["xin"], R2, "R2", col0, g1c, "g1c")

        gu_rr = {"n": 0}

        def gu_bank():
            b = gu_rr["n"] % 3
            gu_rr["n"] += 1
            return b

        def proj(slot, coff, tok0, bank):
            v = wview(slot)
            for k in range(KD):
                S.op("pe", lambda e, k=k: e.matmul(p_gu[bank][:], lhsT=v[:, k, coff:coff + 128], rhs=R2[:, k, tok0:tok0 + 512],
                                                   start=(k == 0), stop=(k == KD - 1)),
                     [("w", slot, coff // 128)] + tkeys("R2", tok0, 512), [("gu", bank)])

        for ti in range(16):
            load_norm_x(ti, ti * 128)

        def hgrn_front(h, tb, slot, main, p):
            tok0 = tb * 512
            cf, ci = (128, 256) if main else (0, 128)
            w_qd, w_kd, w_tm, w_gs = w_qdP[p], w_kdP[p], w_tmP[p], w_gsP[p]
            if main:
                bq = gu_bank()
                proj(slot, 0, tok0, bq)
                S.op("act", lambda e: e.activation(out=w_qs, in_=p_gu[bq][:], func=AF.Sigmoid), [("gu", bq)], ["qs"])
                S.op("dve", lambda e: e.tensor_tensor(out=w_qs, in0=p_gu[bq][:], in1=w_qs, op=ALU.mult), [("gu", bq), "qs"], ["qs"])
            bf = gu_bank()
            proj(slot, cf, tok0, bf)
            S.op("act", lambda e: e.activation(out=w_sgn, in_=p_gu[bf][:], func=AF.Sigmoid, scale=-1.0), [("gu", bf)], ["sgn"])
            if main:
                bg = gu_bank()
                proj(slot, 384, tok0, bg)
                S.op("act", lambda e: e.activation(out=w_gs, in_=p_gu[bg][:], func=AF.Sigmoid), [("gu", bg)], [("gs", p)])
                S.op("dve", lambda e: e.tensor_tensor(out=w_gs, in0=p_gu[bg][:], in1=w_gs, op=ALU.mult), [("gu", bg), ("gs", p)], [("gs", p)])
            S.op("dve", lambda e: e.tensor_scalar(out=w_f, in0=w_sgn, scalar1=nomlc[:, h:h + 1], scalar2=1.0, op0=ALU.mult, op1=ALU.add),
                 ["sgn", "nomlc"], ["f"])
            S.op("act", lambda e: e.activation(out=w_f, in_=w_f, func=AF.Ln), ["f"], ["f"])
            S.op("dve", lambda e: e.tensor_tensor_scan(out=w_b, data0=smask[:], data1=w_f, initial=0.0, op0=ALU.mult, op1=ALU.add),
                 ["f", "smask"], ["b"])
            bi = gu_bank()
            proj(slot, ci, tok0, bi)
            S.op("act", lambda e: e.copy(out=w_vi, in_=p_gu[bi][:]), [("gu", bi)], ["vi"])
            bview = w_b.rearrange("p (c t) -> p c t", t=64)
            S.op("act", lambda e: e.activation(out=w_eb8[:, p, :].unsqueeze(2), in_=bview[:, :, 63:64], func=AF.Exp), ["b"], [("eb8", p)])
            for c in range(8):
                dst, dk = (w_e0, "e0") if c % 2 == 0 else (w_e1, "e1")
                S.op("act", lambda e, c=c, dst=dst: e.activation(out=dst[:, c * 64:(c + 1) * 64], in_=w_b[:, c * 64:(c + 1) * 64], func=AF.Exp,
                                                                 scale=-1.0, bias=w_b[:, c * 64 + 63:c * 64 + 64]), ["b"], [dk])
            S.op("dve", lambda e: e.scalar_tensor_tensor(out=w_kk0, in0=w_sgn, scalar=omlc[:, h:h + 1], in1=w_e0, op0=ALU.mult, op1=ALU.mult),
                 ["sgn", "e0", "omlc"], ["kk0"])
            S.op("dve", lambda e: e.scalar_tensor_tensor(out=w_kk1, in0=w_sgn, scalar=omlc[:, h:h + 1], in1=w_e1, op0=ALU.mult, op1=ALU.mult),
                 ["sgn", "e1", "omlc"], ["kk1"])
            if main:
                S.op("act", lambda e: e.activation(out=w_e, in_=w_b, func=AF.Exp), ["b"], ["e"])
                S.op("dve", lambda e: e.tensor_tensor(out=w_qd, in0=w_qs, in1=w_e, op=ALU.mult), ["qs", "e"], [("qd", p)])
                S.op("act", lambda e: e.activation(out=w_e, in_=w_b, func=AF.Exp, scale=-1.0), ["b"], ["e"])
                S.op("dve", lambda e: e.scalar_tensor_tensor(out=w_kd, in0=w_sgn, scalar=omlc[:, h:h + 1], in1=w_e, op0=ALU.mult, op1=ALU.mult),
                     ["sgn", "e", "omlc"], [("kd", p)])

        def hgrn_front_b(h, tb, main, p):
            w_tm = w_tmP[p]
            for j in range(4):
                S.op("pe", lambda e, j=j: e.transpose(out=p_tr[:, j, :], in_=w_vi[:, j * 128:(j + 1) * 128], identity=ident[:]), ["vi", "ident"], ["ptr"])
            for j in range(4):
                S.op("pe", lambda e, j=j: e.transpose(out=p_tr[:, 4 + j, :], in_=w_kk0[:, j * 128:(j + 1) * 128], identity=ident[:]), ["kk0", "ident"], ["ptr"])
            S.op("dve", lambda e: e.tensor_copy(out=w_tm[:, 0:8, :], in_=p_tr[:]), ["ptr"], [("tmA", p)])
            for j in range(4):
                S.op("pe", lambda e, j=j: e.transpose(out=p_tr[:, j, :], in_=w_kk1[:, j * 128:(j + 1) * 128], identity=ident[:]), ["kk1", "ident"], ["ptr"])
            S.op("dve", lambda e: e.tensor_copy(out=w_tm[:, 8:12, :], in_=p_tr[:, 0:4, :]), ["ptr"], [("tmB", p)])

        def hgrn_back(h, tb, main, p):
            tok0 = tb * 512
            w_qd, w_kd, w_tm, w_gs = w_qdP[p], w_kdP[p], w_tmP[p], w_gsP[p]
            for c in range(8):
                j, r = c // 2, c % 2
                bank = 1 + c // 4
                S.op("pe", lambda e, c=c, j=j, r=r, bank=bank: e.matmul(
                    p_acc[bank][:, (c % 4) * 128:(c % 4) * 128 + 128], lhsT=w_tm[:, 4 + 4 * r + j, :], rhs=w_tm[:, j, :],
                    start=True, stop=True), [("tmA", p), ("tmB", p)], [("acc", bank)])
            if main:
                for j in range(4):
                    S.op("pe", lambda e, j=j: e.matmul(p_acc[0][:, j * 128:(j + 1) * 128], lhsT=w_kd[:, j * 128:(j + 1) * 128], rhs=w_qd[:, j * 128:(j + 1) * 128],
                                                       start=True, stop=True), [("kd", p), ("qd", p)], [("acc", 0)])
                S.op("dve", lambda e: e.tensor_tensor(out=w_am, in0=p_acc[0][:], in1=cmask[:], op=ALU.mult), [("acc", 0), "cmask"], ["am"])
            for c in range(8):
                pp = c % 2
                bank = 1 + c // 4
                if main:
                    S.op("pool", lambda e, c=c, pp=pp: e.tensor_copy(out=Sb[:, c, :], in_=Sst[:, pp, h, :]), [("S", pp, h)], [("Sb", c)])
                S.op("dve", lambda e, c=c, pp=pp, bank=bank: e.scalar_tensor_tensor(
                    out=Sst[:, 1 - pp, h, :], in0=Sst[:, pp, h, :], scalar=w_eb8[:, p, c:c + 1], in1=p_acc[bank][:, (c % 4) * 128:(c % 4) * 128 + 128],
                    op0=ALU.mult, op1=ALU.add), [("S", pp, h), ("eb8", p), ("acc", bank)], [("S", 1 - pp, h)])

        def hgrn_back_b(h, tb, main, p):
            tok0 = tb * 512
            w_qd, w_kd, w_tm, w_gs = w_qdP[p], w_kdP[p], w_tmP[p], w_gsP[p]
            if not main:
                return
            for j in range(4):
                S.op("pe", lambda e, j=j: e.matmul(p_acc[3][:, j * 128:(j + 1) * 128], lhsT=w_tm[:, j, :], rhs=w_am[:, j * 128:(j + 1) * 128],
                                                   start=True, stop=False), [("tmA", p), "am"], [("acc", 3)])
                for r in range(2):
                    c = 2 * j + r
                    S.op("pe", lambda e, c=c, r=r: e.matmul(p_acc[3][:, c * 64:(c + 1) * 64], lhsT=Sb[:, c, :], rhs=w_qd[:, c * 64:(c + 1) * 64],
                                                            start=False, stop=(r == 1)), [("Sb", c), ("qd", p)], [("acc", 3)])
            S.op("act", lambda e: e.activation(out=w_X, in_=p_acc[3][:], func=AF.Square), [("acc", 3)], ["X"])
            S.op("pe", lambda e: e.matmul(p_acc[0][:], lhsT=onesF[:], rhs=w_X, start=True, stop=True), ["X", "onesF"], [("acc", 0)])
            S.op("act", lambda e: e.activation(out=w_X, in_=p_acc[0][:], func=AF.Ln, scale=1.0 / 128, bias=EPS), [("acc", 0)], ["X"])
            S.op("act", lambda e: e.activation(out=w_X, in_=w_X, func=AF.Exp, scale=-0.5), ["X"], ["X"])
            S.op("dve", lambda e: e.scalar_tensor_tensor(out=w_X, in0=p_acc[3][:], scalar=ggrc[:, 0:1], in1=w_X, op0=ALU.mult, op1=ALU.mult),
                 [("acc", 3), "X", "ggrc"], ["X"])
            ms = (h * 4 + tb) % 2
            S.op("dve", lambda e: e.tensor_tensor(out=mst[:, ms, :], in0=w_X, in1=w_gs, op=ALU.mult), ["X", ("gs", p)], [("mst", ms)])
            dma("sp", md[8 + h, :, tok0:tok0 + 512], mst[:, ms, :], [("mst", ms)], [("md", 8 + h, tb)], ("mst", ms))

        def hist_kv(h, slot):
            bk = gu_bank()
            proj(slot, 256, 1536, bk)
            S.op("act", lambda e: e.copy(out=w_kk0, in_=p_gu[bk][:]), [("gu", bk)], ["kk0"])
            dma("sp", kvd[h, :, 0:512], w_kk0, ["kk0"], [("kvd", h)], "kvd_k")
            bv = gu_bank()
            proj(slot, 384, 1536, bv)
            S.op("act", lambda e: e.copy(out=w_kk1, in_=p_gu[bv][:]), [("gu", bv)], ["kk1"])
            for j in range(4):
                S.op("pe", lambda e, j=j: e.transpose(out=p_tr[:, j, :], in_=w_kk1[:, j * 128:(j + 1) * 128], identity=ident[:]), ["kk1", "ident"], ["ptr"])
            S.op("dve", lambda e: e.tensor_copy(out=w_vi.rearrange("p (a b) -> p a b", a=4), in_=p_tr[:, 0:4, :]), ["ptr"], ["vi"])
            dma("sp", kvd[h, :, 512:1024], w_vi, ["vi"], [("kvd", h)], "kvd_v")

        def hgrn_phase(main):
            slots = {}

            def load_w(h):
                slot = wslot()
                slots[h] = slot
                if main:
                    for gi in range(4):
                        wload_cols(slot, gi * 128, w_in, 3072 + gi * 1024 + h * 128, 128)
                else:
                    wload_cols(slot, 0, w_in, 3072 + 1024 + h * 128, 128)
                    wload_cols(slot, 128, w_in, 3072 + 2048 + h * 128, 128)
                    wload_cols(slot, 256, w_in, 1024 + h * 128, 128)
                    wload_cols(slot, 384, w_in, 2048 + h * 128, 128)
            units = [(h, tb) for h in range(8) for tb in range(4)]
            load_w(0)
            for n, (h, tb) in enumerate(units):
                if tb == 0 and h + 1 < 8:
                    load_w(h + 1)
                hgrn_front(h, tb, slots[h], main, n % 2)
                if n > 0:
                    ph, ptb = units[n - 1]
                    hgrn_back(ph, ptb, main, (n - 1) % 2)
                hgrn_front_b(h, tb, main, n % 2)
                if n > 0:
                    hgrn_back_b(ph, ptb, main, (n - 1) % 2)
                if (not main) and tb == 3:
                    hist_kv(h, slots[h])
            ph, ptb = units[-1]
            hgrn_back(ph, ptb, main, (len(units) - 1) % 2)
            hgrn_back_b(ph, ptb, main, (len(units) - 1) % 2)

        hgrn_phase(main=False)
        if debug and "S" in debug:
            dma("sp", dbg["S"], Sst[:, 0, :, :], [("S", 0, h) for h in range(8)], [], "dbgS")
            finals.append(len(S.ops) - 1)
        S.barrier()

        build_tab()
        for ti in range(16):
            load_norm_x(16 + ti, ti * 128)
        S.op("dve", lambda e: e.memset(a_v, 1.0), [], ["a_v_init"])

        def attention_head(h):
            slot = wslot()
            wload_cols(slot, 0, w_in, h * 128, 128)
            wload_cols(slot, 128, w_in, 1024 + h * 128, 128)
            wload_cols(slot, 256, w_in, 2048 + h * 128, 128)
            dma("sp", a_k[:, 0:512], kvd[h, :, 0:512], [("kvd", h)], [("a_k", 0)], "akh")
            dma("sp", a_v[:, 0:4, 0:128], kvd[h, :, 512:1024].rearrange("p (a b) -> p a b", a=4), [("kvd", h), "a_v_init"], [("a_v", 0)], "avh")
            for tb in range(4):
                b = gu_bank()
                proj(slot, 0, tb * 512, b)
                S.op("act", lambda e, b=b, tb=tb: e.copy(out=a_q[:, tb * 512:(tb + 1) * 512], in_=p_gu[b][:]), [("gu", b)], [("a_q", tb)])
                b = gu_bank()
                proj(slot, 128, tb * 512, b)
                S.op("act", lambda e, b=b, tb=tb: e.copy(out=a_k[:, 512 + tb * 512:512 + (tb + 1) * 512], in_=p_gu[b][:]), [("gu", b)], [("a_k", 1 + tb)])
                b = gu_bank()
                proj(slot, 256, tb * 512, b)
                vs = tb % 2
                S.op("act", lambda e, b=b, vs=vs: e.copy(out=a_vT[:, vs, :], in_=p_gu[b][:]), [("gu", b)], [("a_vT", vs)])
                for j in range(4):
                    S.op("pe", lambda e, j=j, vs=vs: e.transpose(out=p_tr[:, j, :], in_=a_vT[:, vs, j * 128:(j + 1) * 128], identity=ident[:]),
                         [("a_vT", vs), "ident"], ["ptr"])
                S.op("dve", lambda e, tb=tb: e.tensor_copy(out=a_v[:, 4 + tb * 4:8 + tb * 4, 0:128], in_=p_tr[:, 0:4, :]), ["ptr", "a_v_init"], [("a_v", 1 + tb)])
            def do_qt(qt):
                pp = qt % 2
                kreads = sorted(set(("a_k", (qt + kt) // 4) for kt in range(5)))
                vreads = sorted(set(("a_v", (qt + kt) // 4) for kt in range(5)))
                for kt in range(5):
                    bank = 0 if kt < 4 else 1
                    col = (kt % 4) * 128
                    S.op("pe", lambda e, kt=kt, bank=bank, col=col: e.matmul(
                        p_acc[bank][:, col:col + 128], lhsT=a_k[:, (qt + kt) * 128:(qt + kt + 1) * 128], rhs=a_q[:, qt * 128:(qt + 1) * 128],
                        start=True, stop=False), kreads + [("a_q", qt // 4)], [("acc", bank)])
                    S.op("pe", lambda e, kt=kt, bank=bank, col=col: e.matmul(
                        p_acc[bank][:, col:col + 128], lhsT=ident[:], rhs=tab[:, h, kt * 128:(kt + 1) * 128],
                        start=False, stop=True), [("tab", h), "ident"], [("acc", bank)])
                if int(os.environ.get('ATT_STAGE', '9')) < 2:
                    return
                nh = max(0, min(4, 4 - qt))
                if nh > 0:
                    S.op("act", lambda e, nh=nh: e.activation(out=a_p[:, pp, 0:nh * 128], in_=p_acc[0][:, 0:nh * 128], func=AF.Exp, scale=SCALE, bias=hneg[:, 0:1]),
                         [("acc", 0), "hneg"], [("a_p", pp)])
                if nh < 4:
                    S.op("act", lambda e, nh=nh: e.activation(out=a_p[:, pp, nh * 128:512], in_=p_acc[0][:, nh * 128:512], func=AF.Exp, scale=SCALE),
                         [("acc", 0)], [("a_p", pp)])
                S.op("act", lambda e: e.activation(out=a_p[:, pp, 512:640], in_=p_acc[1][:, 0:128], func=AF.Exp, scale=SCALE),
                     [("acc", 1)], [("a_p", pp)])
                if int(os.environ.get('ATT_STAGE', '9')) < 3:
                    return
                for kt in range(5):
                    S.op("pe", lambda e, kt=kt: e.matmul(p_acc[2][:, 0:129], lhsT=a_p[:, pp, kt * 128:(kt + 1) * 128], rhs=a_v[:, qt + kt, 0:129],
                                                         start=(kt == 0), stop=(kt == 4)), [("a_p", pp)] + vreads, [("acc", 2)])
                if int(os.environ.get('ATT_STAGE', '9')) < 4:
                    return
                S.op("dve", lambda e: e.reciprocal(out=a_rd[:, pp:pp + 1], in_=p_acc[2][:, 128:129]), [("acc", 2)], [("a_rd", pp)])
                S.op("dve", lambda e: e.tensor_scalar(out=a_ab[:, pp, :], in0=p_acc[2][:, 0:128], scalar1=a_rd[:, pp:pp + 1], scalar2=None, op0=ALU.mult),
                     [("acc", 2), ("a_rd", pp)], [("a_ab", pp)])
                S.op("act", lambda e: e.activation(out=a_jk, in_=a_ab[:, pp, :], func=AF.Square, accum_out=ssa[:, qt, h:h + 1]),
                     [("a_ab", pp)], ["a_jk", ("ssa", qt, h)])
                S.op("pe", lambda e: e.transpose(out=p_tr[:, qt % 4, :], in_=a_ab[:, pp, :], identity=ident[:]), [("a_ab", pp), "ident"], ["ptr"])
                if qt % 4 == 3:
                    g = qt // 4
                    ms = (h * 4 + g) % 2
                    S.op("dve", lambda e, ms=ms: e.tensor_scalar(out=mst_a[:, ms, :].rearrange("p (a b) -> p a b", a=4), in0=p_tr[:, 0:4, :], scalar1=gatc[:, h:h + 1], scalar2=None, op0=ALU.mult),
                         ["ptr", "gatc"], [("mst", ms)])
                    dma("sp", md[h, :, g * 512:(g + 1) * 512], mst_a[:, ms, :], [("mst", ms)], [("md", h, g)], ("mst", ms))
            for qt_ in range(int(os.environ.get('ATT_QTS', '16'))):
                do_qt(qt_)

        if stop_after != "H":
            for h in range(int(os.environ.get('ATT_HEADS', '8'))):
                attention_head(h)
            S.barrier()
        if stop_after not in ("H", "att"):
            S.op("dve", lambda e: e.memset(w_e0, 0.0), [], ["e0"])
            S.op("dve", lambda e: e.memset(w_e1, 0.0), [], ["e1"])
            hgrn_phase(main=True)
            S.barrier()

        if stop_after is None:
            S.op("dve", lambda e: e.tensor_reduce(out=rsa[:], in_=ssa[:], axis=AX.X, op=ALU.add), [], ["rsa"])
            S.op("act", lambda e: e.activation(out=rsa[:], in_=rsa[:], func=AF.Sqrt, scale=1.0 / 1024, bias=EPS), ["rsa"], ["rsa"])
            S.op("dve", lambda e: e.reciprocal(out=rsa[:], in_=rsa[:]), ["rsa"], ["rsa"])
            for blk in range(NT // TB):
                t0 = blk * TB
                dma("sp", mb, md[:, :, t0:t0 + TB].rearrange("k p t -> p k t"), [], ["mb"], "mb")
                for tt in range(4):
                    dma("sp", x1[:, tt, :], xw[NHIST + t0 + tt * 128:NHIST + t0 + (tt + 1) * 128, :], [], [("x1", tt)], ("x1", tt))
                for dblk in range(4):
                    slot = wslot()
                    wload_cols(slot, 0, w_out, dblk * 512, 512)
                    wv = wview(slot)
                    for tt in range(4):
                        qt = blk * 4 + tt
                        bA, bR = (0, 1) if tt % 2 == 0 else (2, 3)
                        for k in range(16):
                            bank = bA if k < 8 else bR
                            S.op("pe", lambda e, k=k, tt=tt, bank=bank, wv=wv: e.matmul(
                                p_acc[bank][:], lhsT=mb[:, k, tt * 128:(tt + 1) * 128], rhs=wv[:, k, :], start=(k % 8 == 0), stop=(k % 8 == 7)),
                                ["mb"] + wkeys(slot), [("acc", bank)])
                        S.op("dve", lambda e, tt=tt, qt=qt, bA=bA, dblk=dblk: e.scalar_tensor_tensor(
                            out=x1[:, tt, dblk * 512:(dblk + 1) * 512], in0=p_acc[bA][:], scalar=rsa[:, qt:qt + 1], in1=x1[:, tt, dblk * 512:(dblk + 1) * 512],
                            op0=ALU.mult, op1=ALU.add), [("acc", bA), "rsa", ("x1", tt)], [("x1", tt)])
                        S.op("dve", lambda e, tt=tt, bR=bR, dblk=dblk: e.tensor_tensor(
                            out=x1[:, tt, dblk * 512:(dblk + 1) * 512], in0=x1[:, tt, dblk * 512:(dblk + 1) * 512], in1=p_acc[bR][:], op=ALU.add),
                            [("acc", bR), ("x1", tt)], [("x1", tt)])
                if debug and "x1" in debug:
                    for tt in range(4):
                        dma("sp", dbg["x1"][t0 + tt * 128:t0 + (tt + 1) * 128, :], x1[:, tt, :], [("x1", tt)], [], ("dbgx1", tt))
                        finals.append(len(S.ops) - 1)
                for tt in range(4):
                    norm_to_T(x1[:, tt, :], [("x1", tt)], h2T, "h2T", tt * 128, g2c, "g2c")
                h2keys = tkeys("h2T", 0, 512)
                for fp in range(NFT // 2):
                    slot = wslot()
                    wload_cols(slot, 0, w_gate, fp * 256, 256)
                    wload_cols(slot, 256, w_up, fp * 256, 256)
                    wv = wview(slot)
                    for fi in range(2):
                        ft = fp * 2 + fi
                        bg = gu_bank()
                        for k in range(16):
                            S.op("pe", lambda e, k=k, fi=fi, bg=bg, wv=wv: e.matmul(p_gu[bg][:], lhsT=wv[:, k, fi * 128:(fi + 1) * 128], rhs=h2T[:, k, :],
                                                                                  start=(k == 0), stop=(k == 15)), [("w", slot, fi)] + h2keys, [("gu", bg)])
                        bu = gu_bank()
                        for k in range(16):
                            S.op("pe", lambda e, k=k, fi=fi, bu=bu, wv=wv: e.matmul(p_gu[bu][:], lhsT=wv[:, k, 256 + fi * 128:256 + (fi + 1) * 128], rhs=h2T[:, k, :],
                                                                                  start=(k == 0), stop=(k == 15)), [("w", slot, 2 + fi)] + h2keys, [("gu", bu)])
                        sl = ft % 2
                        S.op("act", lambda e, bg=bg, sl=sl: e.activation(out=silb[:, sl, :], in_=p_gu[bg][:], func=AF.Silu), [("gu", bg)], [("sil", sl)])
                        S.op("dve", lambda e, bu=bu, sl=sl, ft=ft: e.tensor_tensor(out=ffT[:, ft, :], in0=p_gu[bu][:], in1=silb[:, sl, :], op=ALU.mult),
                             [("gu", bu), ("sil", sl)], [("ffT", ft)])
                for dblk in range(4):
                    for fg, (f0, nf) in enumerate(((0, 16), (16, 16), (32, 12))):
                        slot = wslot()
                        wv = wview(slot)
                        dma("pool", wv[:, 0:nf, :], w_down[f0 * 128:(f0 + nf) * 128, dblk * 512:(dblk + 1) * 512].rearrange("(k p) c -> p k c", p=128),
                            [], wkeys(slot), ("w", slot, 0))
                        for fl in range(nf):
                            fc = f0 + fl
                            for tt in range(4):
                                S.op("pe", lambda e, fl=fl, fc=fc, tt=tt, wv=wv: e.matmul(p_acc[tt][:], lhsT=ffT[:, fc, tt * 128:(tt + 1) * 128], rhs=wv[:, fl, :],
                                                                                        start=(fc == 0), stop=(fc == NFT - 1)), wkeys(slot) + [("ffT", fc)], [("acc", tt)])
                    for tt in range(4):
                        S.op("dve", lambda e, tt=tt, dblk=dblk: e.tensor_tensor(
                            out=x1[:, tt, dblk * 512:(dblk + 1) * 512], in0=x1[:, tt, dblk * 512:(dblk + 1) * 512], in1=p_acc[tt][:], op=ALU.add),
                            [("acc", tt), ("x1", tt)], [("x1", tt)])
                for tt in range(4):
                    i = 32 + (blk * 4 + tt) % 32
                    S.op("act", lambda e, tt=tt, i=i: e.activation(out=junk[:], in_=x1[:, tt, :], func=AF.Square, accum_out=ss[:, i:i + 1]),
                         [("x1", tt)], ["junk", ("ss", i)])
                    S.op("act", lambda e, i=i: e.activation(out=sd[:, i:i + 1], in_=ss[:, i:i + 1], func=AF.Sqrt, scale=1.0 / D, bias=EPS), [("ss", i)], [("sd", i)])
                    S.op("dve", lambda e, i=i: e.reciprocal(out=rstd[:, i:i + 1], in_=sd[:, i:i + 1]), [("sd", i)], [("rstd", i)])
                    S.op("dve", lambda e, tt=tt, i=i: e.scalar_tensor_tensor(out=yo, in0=x1[:, tt, :], scalar=rstd[:, i:i + 1], in1=gfb[:],
                                                                           op0=ALU.mult, op1=ALU.mult), [("x1", tt), ("rstd", i), "gfb"], ["yo"])
                    dma("sp", y[t0 + tt * 128:t0 + (tt + 1) * 128, :], yo, ["yo"], [], "yo")
                    finals.append(len(S.ops) - 1)

        if debug and "md" in debug:
            S.barrier()
            for m in range(16):
                for g in range(4):
                    dma("sp", hb[:, 0:512], md[m, :, g * 512:(g + 1) * 512], [], ["hb"], "hbd")
                    S.op("dve", lambda e: e.tensor_copy(out=gfb[:, 0:512], in_=hb[:, 0:512]), ["hb"], ["gfb"])
                    dma("sp", dbg["md"][m, :, g * 512:(g + 1) * 512], gfb[:, 0:512], ["gfb"], [], "gfbd")
                    finals.append(len(S.ops) - 1)
        stats = S.emit(final_waits=finals)
    return nc, stats


def _consts():
    ident = np.eye(128, dtype=np.float32)
    kk = np.arange(128)[:, None]
    amask = np.zeros((128, 640), np.float32)
    iq = np.arange(128)[None, :]
    amask[:, 0:128] = np.where((iq >= 64) & (kk < 64), -1e5, 0.0)
    amask[:, 512:640] = np.where((iq < 64) & (kk >= 64), -1e5, 0.0)
    s = np.arange(128)[:, None]
    t = np.arange(128)[None, :]
    tri = ((s // 64 == t // 64) & (s <= t)).astype(np.float32)
    cmask = np.tile(tri, (1, 4))
    smask = np.ones((128, 512), np.float32)
    smask[:, 0::64] = 0.0
    return ident, amask, cmask, smask


def _bias_index():
    kt = np.arange(5)[None, :, None]
    kk = np.arange(128)[:, None, None]
    iq = np.arange(128)[None, None, :]
    dist = 512 - 128 * kt + iq - kk
    return (np.clip(dist, -128, 128) + 128).reshape(128, 640)


_PROG = {}


def kernel(x, norm1_gain, w_in, rel_bias, lower_bounds, grn_norm_gain, attn_out_gain, w_out, norm2_gain,
           w_gate, w_up, w_down, final_gain):
    x = np.asarray(x, np.float32)
    if "nc" not in _PROG:
        _PROG["nc"] = build_program()[0]
    nc = _PROG["nc"]
    ident, amask, cmask, smask = _consts()
    idx = _bias_index()
    btab = np.ascontiguousarray(np.asarray(rel_bias, np.float32)[0][:, idx])
    shared = {
        "w_in": np.ascontiguousarray(np.asarray(w_in, np.float32)[0]),
        "w_out": np.ascontiguousarray(np.asarray(w_out, np.float32)[0]),
        "w_gate": np.ascontiguousarray(np.asarray(w_gate, np.float32)[0]),
        "w_up": np.ascontiguousarray(np.asarray(w_up, np.float32)[0]),
        "w_down": np.ascontiguousarray(np.asarray(w_down, np.float32)[0]),
        "g1": np.asarray(norm1_gain, np.float32).reshape(1, D),
        "g2": np.asarray(norm2_gain, np.float32).reshape(1, D),
        "gf": np.asarray(final_gain, np.float32).reshape(1, D),
        "gat": np.asarray(attn_out_gain, np.float32).reshape(1, 1024),
        "ggr": np.asarray(grn_norm_gain, np.float32).reshape(1, 128),
        "lbnd": np.ascontiguousarray(np.asarray(lower_bounds, np.float32)),
        "btab": btab, "c_ident": ident, "c_amask": amask, "c_cmask": cmask, "c_smask": smask,
    }
    in_maps = []
    for c in range(8):
        b, half = c // 2, c % 2
        xwc = np.zeros((NHIST + NT, D), np.float32)
        if half == 1:
            xwc[:] = x[b]
        else:
            xwc[NHIST:] = x[b, :NT]
        m = dict(shared)
        m["xw"] = xwc
        m["c_hneg"] = np.full((128, 1), 0.0 if half == 1 else -30000.0, np.float32)
        in_maps.append(m)
    res = run_bass_kernel_spmd(nc, in_maps, core_ids=list(range(8)))
    out = np.empty((4, 4096, D), np.float32)
    for c in range(8):
        b, half = c // 2, c % 2
        out[b, half * NT:(half + 1) * NT] = res.results[c]["y"]
    return out
```

```python
import contextlib
import os
import numpy as np
import ml_dtypes
import concourse.bass as bass
import concourse.mybir as mybir
from concourse.bass_utils import run_bass_kernel_spmd

F32 = mybir.dt.float32
BF16 = mybir.dt.bfloat16
AF = mybir.ActivationFunctionType
ALU = mybir.AluOpType
AX = mybir.AxisListType

ENGS = ("pe", "act", "dve", "pool", "sp")

D = 2048
NT = 2048
NHIST = 2048
KD = 16
DFF = 5632
NFT = 44
EPS = 1e-6
INCOLS = 7168
TB = 512
SCALE = 128 ** -0.5
SQ128 = 128 ** 0.5


class Sched:
    def __init__(self, nc, same_engine_sync=True):
        self.nc = nc
        self.ops = []
        self.last_writer = {}
        self.readers = {}
        self.same_engine_sync = same_engine_sync
        self.dma_keys = {}

    def op(self, eng, fn, reads=(), writes=(), dma_key=None):
        idx = len(self.ops)
        deps = set()
        for k in reads:
            w = self.last_writer.get(k)
            if w is not None:
                deps.add(w)
        for k in writes:
            w = self.last_writer.get(k)
            if w is not None:
                deps.add(w)
            for r in self.readers.get(k, ()):
                deps.add(r)
        if dma_key is not None:
            prev = self.dma_keys.get(dma_key)
            if prev is not None:
                deps.add(prev)
            self.dma_keys[dma_key] = idx
        deps.discard(idx)
        self.ops.append(dict(eng=eng, fn=fn, deps=deps, dma_key=dma_key, needed=False))
        for k in writes:
            self.last_writer[k] = idx
            self.readers[k] = []
        for k in reads:
            if k not in writes:
                lst = self.readers.setdefault(k, [])
                if dma_key is None:
                    lst[:] = [r for r in lst if not (self.ops[r]["eng"] == eng and self.ops[r]["dma_key"] is None)]
                lst.append(idx)
        return idx

    def barrier(self):
        last = {}
        for i, o in enumerate(self.ops):
            last[("e", o["eng"])] = i
            if o["dma_key"] is not None:
                last[("d", o["dma_key"])] = i
        deps = set(last.values())
        for e in ENGS:
            self.ops.append(dict(eng=e, fn=None, deps=set(deps), dma_key=None, needed=False))

    def emit(self, final_waits=()):
        nc = self.nc
        ops = self.ops
        for i, o in enumerate(ops):
            nd = set()
            for d in o["deps"]:
                p = ops[d]
                if p["dma_key"] is None and p["eng"] == o["eng"] and not self.same_engine_sync:
                    continue
                if p["dma_key"] is None and p["eng"] == "pe" and o["eng"] == "pe":
                    continue
                nd.add(d)
            o["deps"] = nd
            for d in nd:
                ops[d]["needed"] = True
        for d in final_waits:
            ops[d]["needed"] = True
        tick = {e: 0 for e in ENGS}
        dma_cnt = {}
        for o in ops:
            if o["dma_key"] is not None:
                dma_cnt[o["dma_key"]] = dma_cnt.get(o["dma_key"], 0) + 1
                o["sem"] = ("dma", o["dma_key"])
                o["tick"] = 16 * dma_cnt[o["dma_key"]]
            elif o["needed"]:
                tick[o["eng"]] += 1
                o["sem"] = ("eng", o["eng"])
                o["tick"] = tick[o["eng"]]
        sem_names = [("eng", e) for e in ENGS] + [("dma", k) for k in dma_cnt]
        with contextlib.ExitStack() as st:
            sems = {}
            for i, sn in enumerate(sem_names):
                sems[sn] = st.enter_context(nc.semaphore("s%d" % i))
            block = st.enter_context(nc.Block())
            per_eng = {e: [] for e in ENGS}
            for i, o in enumerate(ops):
                per_eng[o["eng"]].append(i)

            def body(ename, engine):
                waited = {}
                for i in per_eng[ename]:
                    o = ops[i]
                    need = {}
                    for d in o["deps"]:
                        p = ops[d]
                        s = p["sem"]
                        need[s] = max(need.get(s, 0), p["tick"])
                    for s, v in need.items():
                        if waited.get(s, 0) >= v:
                            continue
                        engine.wait_ge(sems[s], v)
                        waited[s] = v
                    if o["fn"] is None:
                        continue
                    ins = o["fn"](engine)
                    if o["dma_key"] is not None:
                        ins.then_inc(sems[o["sem"]], 16)
                    elif o["needed"]:
                        ins.then_inc(sems[o["sem"]], 1)
                if ename == "sp":
                    need = {}
                    for d in final_waits:
                        p = ops[d]
                        need[p["sem"]] = max(need.get(p["sem"], 0), p["tick"])
                    for s, v in need.items():
                        engine.wait_ge(sems[s], v)

            @block.tensor
            def _(e):
                body("pe", e)

            @block.scalar
            def _(e):
                body("act", e)

            @block.vector
            def _(e):
                body("dve", e)

            @block.gpsimd
            def _(e):
                body("pool", e)

            @block.sync
            def _(e):
                body("sp", e)
        return {e: len(per_eng[e]) for e in ENGS}, tick, len(sem_names)


def build_program(debug=None, stop_after=None):
    nc = bass.Bass("TRN2", target_bir_lowering=False)
    dt_in = lambda name, shape: nc.dram_tensor(name, shape, F32, kind="ExternalInput").ap()
    xw = dt_in("xw", [NHIST + NT, D])
    w_in = dt_in("w_in", [D, INCOLS])
    w_out = dt_in("w_out", [D, D])
    w_gate = dt_in("w_gate", [D, DFF])
    w_up = dt_in("w_up", [D, DFF])
    w_down = dt_in("w_down", [DFF, D])
    g1_d = dt_in("g1", [1, D])
    g2_d = dt_in("g2", [1, D])
    gf_d = dt_in("gf", [1, D])
    gat_d = dt_in("gat", [1, 1024])
    ggr_d = dt_in("ggr", [1, 128])
    lbnd_d = dt_in("lbnd", [2, 1024])
    btab_d = dt_in("btab", [8, 128, 640])
    cid_d = dt_in("c_ident", [128, 128])
    cam_d = dt_in("c_amask", [128, 640])
    ccm_d = dt_in("c_cmask", [128, 512])
    csm_d = dt_in("c_smask", [128, 512])
    chn_d = dt_in("c_hneg", [128, 1])
    y = nc.dram_tensor("y", [NT, D], F32, kind="ExternalOutput").ap()
    md = nc.dram_tensor("md", [16, 128, NT], BF16, kind="Internal").ap()
    kvd = nc.dram_tensor("kvd", [8, 128, 1024], BF16, kind="Internal").ap()
    dbg = {}
    if debug:
        for name, shape in debug.items():
            dbg[name] = nc.dram_tensor("dbg_" + name, shape, F32, kind="ExternalOutput").ap()

    S = Sched(nc)
    finals = []
    with contextlib.ExitStack() as st:
        def sb(name, shape, dt=F32):
            return st.enter_context(nc.sbuf_tensor(name, shape, dt))

        def ps(name, shape, dt=F32):
            return st.enter_context(nc.psum_tensor(name, shape, dt))

        R2 = sb("R2", [128, 16, 2048], BF16)
        R1 = sb("R1", [128, 28672], BF16)
        NSLOT = 3
        wpool = sb("wpool", [128, NSLOT, 8192], BF16)
        ident = sb("ident", [128, 128], BF16)
        identF = sb("identF", [128, 128], F32)
        onesF = sb("onesF", [128, 128], F32)
        amask = sb("amask", [128, 640], F32)
        cmask = sb("cmask", [128, 512], F32)
        smask = sb("smask", [128, 512], F32)
        hneg = sb("hneg", [128, 1], F32)
        g1c = sb("g1c", [128, 16], F32)
        g2c = sb("g2c", [128, 16], F32)
        gfb = sb("gfb", [128, 2048], F32)
        gatc = sb("gatc", [128, 8], F32)
        ggrc = sb("ggrc", [128, 1], F32)
        lbc = sb("lbc", [128, 8], F32)
        lb1 = sb("lb1", [128, 8], F32)
        omlc = sb("omlc", [128, 8], F32)
        nomlc = sb("nomlc", [128, 8], F32)
        junk = sb("junk", [128, 2048], BF16)
        hb = sb("hb", [128, 2048], BF16)
        silb = sb("silb", [128, 2, 512], F32)
        ss = sb("ss", [128, 64], F32)
        sd = sb("sd", [128, 64], F32)
        rstd = sb("rstd", [128, 64], F32)
        ssa = sb("ssa", [128, 16, 8], F32)
        rsa = sb("rsa", [128, 16], F32)

        def carve(off, shape, dt):
            n = 1
            for s_ in shape[1:]:
                n *= s_
            esz = 2 if dt == BF16 else 4
            a = R1[:, off // 2: off // 2 + n * esz // 2]
            if dt == F32:
                a = a.bitcast(F32)
            if len(shape) == 3:
                a = a.rearrange("p (a b) -> p a b", a=shape[1])
            elif len(shape) == 4:
                a = a.rearrange("p (a b c) -> p a b c", a=shape[1], b=shape[2])
            return a

        xin = carve(0, [128, 2048], F32)
        Sfin = carve(8192, [128, 8, 128], F32)
        Sall = carve(12288, [128, 8, 128], F32)
        Sb = carve(16384, [128, 8, 128], BF16)
        tab = carve(18432, [128, 8, 640], BF16)
        WK = 28672
        KB2 = 2048
        w_sgn = carve(WK + 0 * KB2, [128, 512], F32)
        w_f = carve(WK + 1 * KB2, [128, 512], F32)
        w_b = carve(WK + 2 * KB2, [128, 512], F32)
        w_e0 = carve(WK + 3 * KB2, [128, 512], F32)
        w_e1 = carve(WK + 4 * KB2, [128, 512], F32)
        w_e = carve(WK + 5 * KB2, [128, 512], F32)
        w_qs = carve(WK + 6 * KB2, [128, 512], F32)
        w_gs0 = carve(WK + 7 * KB2, [128, 512], F32)
        o = WK + 8 * KB2
        w_qd0 = carve(o, [128, 512], BF16); o += 1024
        w_kd0 = carve(o, [128, 512], BF16); o += 1024
        w_kk0 = carve(o, [128, 512], BF16); o += 1024
        w_kk1 = carve(o, [128, 512], BF16); o += 1024
        w_vi = carve(o, [128, 512], BF16); o += 1024
        w_tm0 = carve(o, [128, 12, 128], BF16); o += 3072
        w_am = carve(o, [128, 512], BF16); o += 1024
        mst = carve(o, [128, 2, 512], BF16); o += 2048
        w_eb8 = carve(o, [128, 2, 8], F32); o += 64
        assert o <= 57344, o
        o = 18432
        w_gs1 = carve(o, [128, 512], F32); o += 2048
        w_X = carve(o, [128, 512], F32); o += 2048
        w_qd1 = carve(o, [128, 512], BF16); o += 1024
        w_kd1 = carve(o, [128, 512], BF16); o += 1024
        w_tm1 = carve(o, [128, 12, 128], BF16); o += 3072
        assert o <= 28672, o
        w_gsP = (w_gs0, w_gs1)
        w_qdP = (w_qd0, w_qd1)
        w_kdP = (w_kd0, w_kd1)
        w_tmP = (w_tm0, w_tm1)
        o = WK
        a_q = carve(o, [128, 2048], BF16); o += 4096
        a_k = carve(o, [128, 2560], BF16); o += 5120
        a_vT = carve(o, [128, 2, 512], BF16); o += 2048
        a_v = carve(o, [128, 20, 132], BF16); o += 5280
        a_p = carve(o, [128, 2, 640], BF16); o += 2560
        a_ab = carve(o, [128, 2, 128], BF16); o += 512
        a_jk = carve(o, [128, 128], BF16); o += 256
        a_rd = carve(o, [128, 2], F32); o += 8
        o = (o + 63) // 64 * 64
        mst_a = carve(o, [128, 2, 512], BF16); o += 2048
        assert o <= 57344, o
        mb = carve(0, [128, 16, 512], BF16)
        x1 = carve(16384, [128, 4, 2048], F32)
        yo = carve(49152, [128, 2048], F32)
        r2f = R2[:].rearrange("p k t -> p (k t)")
        ffT = r2f[:, 0:22528].rearrange("p (f t) -> p f t", f=NFT)
        h2T = r2f[:, 22528:30720].rearrange("p (k t) -> p k t", k=16)

        p_acc = [ps("pacc%d" % i, [128, 512]) for i in range(4)]
        p_gu = [ps("pgu%d" % i, [128, 512]) for i in range(3)]
        p_tr = ps("ptr", [128, 8, 128], BF16)

        def dma(eng, out, in_, reads, writes, key, slow=False):
            if slow:
                return S.op(eng, lambda e: e.dma_start(out=out, in_=in_, allow_slow_non_contiguous=True), reads=reads, writes=writes, dma_key=key)
            return S.op(eng, lambda e: e.dma_start(out=out, in_=in_), reads=reads, writes=writes, dma_key=key)

        dma("sp", identF[:], cid_d[:, :], [], ["identF"], "c0")
        dma("sp", amask[:], cam_d[:, :], [], ["amask"], "c1")
        dma("sp", cmask[:], ccm_d[:, :], [], ["cmask"], "c2")
        dma("sp", smask[:], csm_d[:, :], [], ["smask"], "c3")
        dma("sp", hneg[:], chn_d[:, :], [], ["hneg"], "c4")
        dma("sp", gfb[:], gf_d.broadcast_to([128, D]), [], ["gfb"], "c5")
        cvec = lambda d_: d_.rearrange("o (k p) -> p (o k)", p=128)
        dma("sp", g1c[:], cvec(g1_d), [], ["g1c"], "c6", slow=True)
        dma("sp", g2c[:], cvec(g2_d), [], ["g2c"], "c7", slow=True)
        dma("sp", gatc[:], cvec(gat_d), [], ["gatc"], "c8", slow=True)
        dma("sp", ggrc[:], cvec(ggr_d), [], ["ggrc"], "c9", slow=True)
        dma("sp", lbc[:], cvec(lbnd_d[0:1, :]), [], ["lbc"], "c10", slow=True)
        dma("sp", lb1[:], cvec(lbnd_d[1:2, :]), [], ["lb1"], "c11", slow=True)
        S.op("dve", lambda e: e.tensor_copy(out=ident[:], in_=identF[:]), ["identF"], ["ident"])
        S.op("dve", lambda e: e.memset(onesF[:], 1.0), [], ["onesF"])
        S.op("dve", lambda e: e.memset(Sfin, 0.0), [], [("S", h) for h in range(8)])
        S.op("dve", lambda e: e.memset(w_e0, 0.0), [], ["e0"])
        S.op("dve", lambda e: e.memset(w_e1, 0.0), [], ["e1"])
        S.op("dve", lambda e: e.tensor_tensor(out=lbc[:], in0=lbc[:], in1=lb1[:], op=ALU.subtract), ["lbc", "lb1"], ["lbc"])
        S.op("act", lambda e: e.activation(out=lbc[:], in_=lbc[:], func=AF.Sigmoid), ["lbc"], ["lbc"])
        S.op("dve", lambda e: e.tensor_scalar(out=omlc[:], in0=lbc[:], scalar1=-1.0, scalar2=1.0, op0=ALU.mult, op1=ALU.add), ["lbc"], ["omlc"])
        S.op("dve", lambda e: e.tensor_scalar(out=nomlc[:], in0=lbc[:], scalar1=1.0, scalar2=-1.0, op0=ALU.mult, op1=ALU.add), ["lbc"], ["nomlc"])
        def build_tab():
            for h in range(8):
                dma("sp", xin[:, 0:640], btab_d[h], [], ["xin"], "xin")
                S.op("dve", lambda e, h=h: e.scalar_tensor_tensor(out=tab[:, h, :], in0=xin[:, 0:640], scalar=SQ128, in1=amask[:],
                                                                  op0=ALU.mult, op1=ALU.add), ["xin", "amask"], [("tab", h)])

        wstate = {"n": 0}

        def wslot():
            s = wstate["n"] % NSLOT
            wstate["n"] += 1
            return s

        def wview(slot):
            return wpool[:, slot, :].rearrange("p (k c) -> p k c", k=16)

        def wkeys(slot, q0=0, q1=4):
            return [("w", slot, q) for q in range(q0, q1)]

        def wload_cols(slot, dst_off, src, col0, ncols):
            q0, q1 = dst_off // 128, (dst_off + ncols) // 128
            dma("pool", wview(slot)[:, :, dst_off:dst_off + ncols],
                src[:, col0:col0 + ncols].rearrange("(k p) c -> p k c", p=128),
                [], wkeys(slot, q0, q1), ("w", slot, q0))

        ncount = {"n": 0}

        def norm_to_T(src_ap, src_reads, dstT, dname, col0, gain_c, gkey):
            i = ncount["n"] % 32
            ncount["n"] += 1
            S.op("act", lambda e: e.activation(out=junk[:], in_=src_ap, func=AF.Square, accum_out=ss[:, i:i + 1]),
                 src_reads, ["junk", ("ss", i)])
            S.op("act", lambda e: e.activation(out=sd[:, i:i + 1], in_=ss[:, i:i + 1], func=AF.Sqrt, scale=1.0 / D, bias=EPS),
                 [("ss", i)], [("sd", i)])
            S.op("dve", lambda e: e.reciprocal(out=rstd[:, i:i + 1], in_=sd[:, i:i + 1]), [("sd", i)], [("rstd", i)])
            S.op("dve", lambda e: e.tensor_scalar(out=hb[:], in0=src_ap, scalar1=rstd[:, i:i + 1], scalar2=None, op0=ALU.mult),
                 src_reads + [("rstd", i)], ["hb"])
            for half in range(2):
                for j in range(8):
                    k = half * 8 + j
                    S.op("pe", lambda e, k=k, j=j: e.transpose(out=p_tr[:, j, :], in_=hb[:, k * 128:(k + 1) * 128], identity=ident[:]),
                         ["hb", "ident"], ["ptr"])
                S.op("dve", lambda e, half=half: e.tensor_tensor(
                    out=dstT[:, half * 8:half * 8 + 8, col0:col0 + 128], in0=p_tr[:],
                    in1=gain_c[:, half * 8:half * 8 + 8].unsqueeze(2).broadcast_to([128, 8, 128]), op=ALU.mult),
                    ["ptr", gkey], [("T", dname, col0, half)])

        def tkeys(dname, tok0, ntok):
            return [("T", dname, c, half) for c in range(tok0, tok0 + ntok, 128) for half in range(2)]

        def load_norm_x(ti, col0):
            dma("sp", xin, xw[ti * 128:(ti + 1) * 128, :], [], ["xin"], "xin")
            norm_to_T(xin, ["xin"], R2, "R2", col0, g1c, "g1c")

        gu_rr = {"n": 0}

        def gu_bank():
            b = gu_rr["n"] % 3
            gu_rr["n"] += 1
            return b

        def proj(slot, coff, tok0, bank):
            v = wview(slot)
            for k in range(KD):
                S.op("pe", lambda e, k=k: e.matmul(p_gu[bank][:], lhsT=v[:, k, coff:coff + 128], rhs=R2[:, k, tok0:tok0 + 512],
                                                   start=(k == 0), stop=(k == KD - 1)),
                     [("w", slot, coff // 128)] + tkeys("R2", tok0, 512), [("gu", bank)])

        for ti in range(16):
            load_norm_x(ti, ti * 128)

        def hgrn_front(h, tb, slot, main, p):
            tok0 = tb * 512
            cf, ci = (128, 256) if main else (0, 128)
            w_qd, w_kd, w_tm, w_gs = w_qdP[p], w_kdP[p], w_tmP[p], w_gsP[p]
            if main:
                bq = gu_bank()
                proj(slot, 0, tok0, bq)
                S.op("act", lambda e: e.activation(out=w_qs, in_=p_gu[bq][:], func=AF.Sigmoid), [("gu", bq)], ["qs"])
                S.op("dve", lambda e: e.tensor_tensor(out=w_qs, in0=p_gu[bq][:], in1=w_qs, op=ALU.mult), [("gu", bq), "qs"], ["qs"])
            bf = gu_bank()
            proj(slot, cf, tok0, bf)
            S.op("act", lambda e: e.activation(out=w_sgn, in_=p_gu[bf][:], func=AF.Sigmoid, scale=-1.0), [("gu", bf)], ["sgn"])
            if main:
                bg = gu_bank()
                proj(slot, 384, tok0, bg)
                S.op("act", lambda e: e.activation(out=w_gs, in_=p_gu[bg][:], func=AF.Sigmoid), [("gu", bg)], [("gs", p)])
                S.op("dve", lambda e: e.tensor_tensor(out=w_gs, in0=p_gu[bg][:], in1=w_gs, op=ALU.mult), [("gu", bg), ("gs", p)], [("gs", p)])
            S.op("dve", lambda e: e.tensor_scalar(out=w_f, in0=w_sgn, scalar1=nomlc[:, h:h + 1], scalar2=1.0, op0=ALU.mult, op1=ALU.add),
                 ["sgn", "nomlc"], ["f"])
            S.op("act", lambda e: e.activation(out=w_f, in_=w_f, func=AF.Ln), ["f"], ["f"])
            S.op("dve", lambda e: e.tensor_tensor_scan(out=w_b, data0=smask[:], data1=w_f, initial=0.0, op0=ALU.mult, op1=ALU.add),
                 ["f", "smask"], ["b"])
            bi = gu_bank()
            proj(slot, ci, tok0, bi)
            S.op("act", lambda e: e.copy(out=w_vi, in_=p_gu[bi][:]), [("gu", bi)], ["vi"])
            bview = w_b.rearrange("p (c t) -> p c t", t=64)
            S.op("act", lambda e: e.activation(out=w_eb8[:, p, :].unsqueeze(2), in_=bview[:, :, 63:64], func=AF.Exp), ["b"], [("eb8", p)])
            for c in range(8):
                dst, dk = (w_e0, "e0") if c % 2 == 0 else (w_e1, "e1")
                S.op("act", lambda e, c=c, dst=dst: e.activation(out=dst[:, c * 64:(c + 1) * 64], in_=w_b[:, c * 64:(c + 1) * 64], func=AF.Exp,
                                                                 scale=-1.0, bias=w_b[:, c * 64 + 63:c * 64 + 64]), ["b"], [dk])
            S.op("dve", lambda e: e.scalar_tensor_tensor(out=w_kk0, in0=w_sgn, scalar=omlc[:, h:h + 1], in1=w_e0, op0=ALU.mult, op1=ALU.mult),
                 ["sgn", "e0", "omlc"], ["kk0"])
            S.op("dve", lambda e: e.scalar_tensor_tensor(out=w_kk1, in0=w_sgn, scalar=omlc[:, h:h + 1], in1=w_e1, op0=ALU.mult, op1=ALU.mult),
                 ["sgn", "e1", "omlc"], ["kk1"])
            if main:
                S.op("act", lambda e: e.activation(out=w_e, in_=w_b, func=AF.Exp), ["b"], ["e"])
                S.op("dve", lambda e: e.tensor_tensor(out=w_qd, in0=w_qs, in1=w_e, op=ALU.mult), ["qs", "e"], [("qd", p)])
                S.op("act", lambda e: e.activation(out=w_e, in_=w_b, func=AF.Exp, scale=-1.0), ["b"], ["e"])
                S.op("dve", lambda e: e.scalar_tensor_tensor(out=w_kd, in0=w_sgn, scalar=omlc[:, h:h + 1], in1=w_e, op0=ALU.mult, op1=ALU.mult),
                     ["sgn", "e", "omlc"], [("kd", p)])

        def hgrn_front_b(h, tb, main, p):
            w_tm = w_tmP[p]
            for j in range(4):
                S.op("pe", lambda e, j=j: e.transpose(out=p_tr[:, j, :], in_=w_vi[:, j * 128:(j + 1) * 128], identity=ident[:]), ["vi", "ident"], ["ptr"])
            for j in range(4):
                S.op("pe", lambda e, j=j: e.transpose(out=p_tr[:, 4 + j, :], in_=w_kk0[:, j * 128:(j + 1) * 128], identity=ident[:]), ["kk0", "ident"], ["ptr"])
            S.op("dve", lambda e: e.tensor_copy(out=w_tm[:, 0:8, :], in_=p_tr[:]), ["ptr"], [("tmA", p)])
            for j in range(4):
                S.op("pe", lambda e, j=j: e.transpose(out=p_tr[:, j, :], in_=w_kk1[:, j * 128:(j + 1) * 128], identity=ident[:]), ["kk1", "ident"], ["ptr"])
            S.op("dve", lambda e: e.tensor_copy(out=w_tm[:, 8:12, :], in_=p_tr[:, 0:4, :]), ["ptr"], [("tmB", p)])

        def hgrn_back(h, tb, main, p):
            tok0 = tb * 512
            w_qd, w_kd, w_tm, w_gs = w_qdP[p], w_kdP[p], w_tmP[p], w_gsP[p]
            for c in range(8):
                j, r = c // 2, c % 2
                bank = 1 + c // 4
                S.op("pe", lambda e, c=c, j=j, r=r, bank=bank: e.matmul(
                    p_acc[bank][:, (c % 4) * 128:(c % 4) * 128 + 128], lhsT=w_tm[:, 4 + 4 * r + j, :], rhs=w_tm[:, j, :],
                    start=True, stop=True), [("tmA", p), ("tmB", p)], [("acc", bank)])
            if main:
                for j in range(4):
                    S.op("pe", lambda e, j=j: e.matmul(p_acc[0][:, j * 128:(j + 1) * 128], lhsT=w_kd[:, j * 128:(j + 1) * 128], rhs=w_qd[:, j * 128:(j + 1) * 128],
                                                       start=True, stop=True), [("kd", p), ("qd", p)], [("acc", 0)])
                S.op("dve", lambda e: e.tensor_tensor(out=w_am, in0=p_acc[0][:], in1=cmask[:], op=ALU.mult), [("acc", 0), "cmask"], ["am"])
            if main:
                S.op("pool", lambda e: e.tensor_copy(out=Sb[:, 0, :], in_=Sfin[:, h, :]), [("S", h)], [("Sb", 0)])
            for c in range(8):
                bank = 1 + c // 4
                src, skey = (Sfin[:, h, :], ("S", h)) if c == 0 else (Sall[:, c, :], ("Sall", c))
                dst, dkey = (Sfin[:, h, :], ("S", h)) if c == 7 else (Sall[:, c + 1, :], ("Sall", c + 1))
                S.op("dve", lambda e, c=c, bank=bank, src=src, dst=dst: e.scalar_tensor_tensor(
                    out=dst, in0=src, scalar=w_eb8[:, p, c:c + 1], in1=p_acc[bank][:, (c % 4) * 128:(c % 4) * 128 + 128],
                    op0=ALU.mult, op1=ALU.add), [skey, ("eb8", p), ("acc", bank)], [dkey])
                if main and c == 3:
                    S.op("act", lambda e: e.copy(out=Sb[:, 1:5, :], in_=Sall[:, 1:5, :]), [("Sall", i) for i in range(1, 5)], [("Sb", i) for i in range(1, 5)])
                if main and c == 6:
                    S.op("act", lambda e: e.copy(out=Sb[:, 5:8, :], in_=Sall[:, 5:8, :]), [("Sall", i) for i in range(5, 8)], [("Sb", i) for i in range(5, 8)])

        def hgrn_back_b(h, tb, main, p):
            tok0 = tb * 512
            w_qd, w_kd, w_tm, w_gs = w_qdP[p], w_kdP[p], w_tmP[p], w_gsP[p]
            if not main:
                return
            for j in range(4):
                S.op("pe", lambda e, j=j: e.matmul(p_acc[3][:, j * 128:(j + 1) * 128], lhsT=w_tm[:, j, :], rhs=w_am[:, j * 128:(j + 1) * 128],
                                                   start=True, stop=False), [("tmA", p), "am"], [("acc", 3)])
                for r in range(2):
                    c = 2 * j + r
                    S.op("pe", lambda e, c=c, r=r: e.matmul(p_acc[3][:, c * 64:(c + 1) * 64], lhsT=Sb[:, c, :], rhs=w_qd[:, c * 64:(c + 1) * 64],
                                                            start=False, stop=(r == 1)), [("Sb", c), ("qd", p)], [("acc", 3)])
            S.op("act", lambda e: e.activation(out=w_X, in_=p_acc[3][:], func=AF.Square), [("acc", 3)], ["X"])
            S.op("pe", lambda e: e.matmul(p_acc[0][:], lhsT=onesF[:], rhs=w_X, start=True, stop=True), ["X", "onesF"], [("acc", 0)])
            S.op("act", lambda e: e.activation(out=w_X, in_=p_acc[0][:], func=AF.Ln, scale=1.0 / 128, bias=EPS), [("acc", 0)], ["X"])
            S.op("act", lambda e: e.activation(out=w_X, in_=w_X, func=AF.Exp, scale=-0.5), ["X"], ["X"])
            S.op("dve", lambda e: e.scalar_tensor_tensor(out=w_X, in0=p_acc[3][:], scalar=ggrc[:, 0:1], in1=w_X, op0=ALU.mult, op1=ALU.mult),
                 [("acc", 3), "X", "ggrc"], ["X"])
            ms = (h * 4 + tb) % 2
            S.op("dve", lambda e: e.tensor_tensor(out=mst[:, ms, :], in0=w_X, in1=w_gs, op=ALU.mult), ["X", ("gs", p)], [("mst", ms)])
            dma("sp", md[8 + h, :, tok0:tok0 + 512], mst[:, ms, :], [("mst", ms)], [("md", 8 + h, tb)], ("mst", ms))

        def hist_kv(h, slot):
            bk = gu_bank()
            proj(slot, 256, 1536, bk)
            S.op("act", lambda e: e.copy(out=w_kk0, in_=p_gu[bk][:]), [("gu", bk)], ["kk0"])
            dma("sp", kvd[h, :, 0:512], w_kk0, ["kk0"], [("kvd", h)], "kvd_k")
            bv = gu_bank()
            proj(slot, 384, 1536, bv)
            S.op("act", lambda e: e.copy(out=w_kk1, in_=p_gu[bv][:]), [("gu", bv)], ["kk1"])
            for j in range(4):
                S.op("pe", lambda e, j=j: e.transpose(out=p_tr[:, j, :], in_=w_kk1[:, j * 128:(j + 1) * 128], identity=ident[:]), ["kk1", "ident"], ["ptr"])
            S.op("dve", lambda e: e.tensor_copy(out=w_vi.rearrange("p (a b) -> p a b", a=4), in_=p_tr[:, 0:4, :]), ["ptr"], ["vi"])
            dma("sp", kvd[h, :, 512:1024], w_vi, ["vi"], [("kvd", h)], "kvd_v")

        def hgrn_phase(main):
            slots = {}

            def load_w(h):
                slot = wslot()
                slots[h] = slot
                if main:
                    for gi in range(4):
                        wload_cols(slot, gi * 128, w_in, 3072 + gi * 1024 + h * 128, 128)
                else:
                    wload_cols(slot, 0, w_in, 3072 + 1024 + h * 128, 128)
                    wload_cols(slot, 128, w_in, 3072 + 2048 + h * 128, 128)
                    wload_cols(slot, 256, w_in, 1024 + h * 128, 128)
                    wload_cols(slot, 384, w_in, 2048 + h * 128, 128)
            units = [(h, tb) for h in range(8) for tb in range(4)]
            load_w(0)
            for n, (h, tb) in enumerate(units):
                if tb == 0 and h + 1 < 8:
                    load_w(h + 1)
                hgrn_front(h, tb, slots[h], main, n % 2)
                if n > 0:
                    ph, ptb = units[n - 1]
                    hgrn_back(ph, ptb, main, (n - 1) % 2)
                hgrn_front_b(h, tb, main, n % 2)
                if n > 0:
                    hgrn_back_b(ph, ptb, main, (n - 1) % 2)
                if (not main) and tb == 3:
                    hist_kv(h, slots[h])
            ph, ptb = units[-1]
            hgrn_back(ph, ptb, main, (len(units) - 1) % 2)
            hgrn_back_b(ph, ptb, main, (len(units) - 1) % 2)

        hgrn_phase(main=False)
        if debug and "S" in debug:
            dma("sp", dbg["S"], Sfin, [("S", h) for h in range(8)], [], "dbgS")
            finals.append(len(S.ops) - 1)
        S.barrier()

        build_tab()
        for ti in range(16):
            load_norm_x(16 + ti, ti * 128)
        S.op("dve", lambda e: e.memset(a_v, 1.0), [], ["a_v_init"])

        def attention_head(h):
            slot = wslot()
            wload_cols(slot, 0, w_in, h * 128, 128)
            wload_cols(slot, 128, w_in, 1024 + h * 128, 128)
            wload_cols(slot, 256, w_in, 2048 + h * 128, 128)
            dma("sp", a_k[:, 0:512], kvd[h, :, 0:512], [("kvd", h)], [("a_k", 0)], "akh")
            dma("sp", a_v[:, 0:4, 0:128], kvd[h, :, 512:1024].rearrange("p (a b) -> p a b", a=4), [("kvd", h), "a_v_init"], [("a_v", 0)], "avh")
            for tb in range(4):
                b = gu_bank()
                proj(slot, 0, tb * 512, b)
                S.op("act", lambda e, b=b, tb=tb: e.copy(out=a_q[:, tb * 512:(tb + 1) * 512], in_=p_gu[b][:]), [("gu", b)], [("a_q", tb)])
                b = gu_bank()
                proj(slot, 128, tb * 512, b)
                S.op("act", lambda e, b=b, tb=tb: e.copy(out=a_k[:, 512 + tb * 512:512 + (tb + 1) * 512], in_=p_gu[b][:]), [("gu", b)], [("a_k", 1 + tb)])
                b = gu_bank()
                proj(slot, 256, tb * 512, b)
                vs = tb % 2
                S.op("act", lambda e, b=b, vs=vs: e.copy(out=a_vT[:, vs, :], in_=p_gu[b][:]), [("gu", b)], [("a_vT", vs)])
                for j in range(4):
                    S.op("pe", lambda e, j=j, vs=vs: e.transpose(out=p_tr[:, j, :], in_=a_vT[:, vs, j * 128:(j + 1) * 128], identity=ident[:]),
                         [("a_vT", vs), "ident"], ["ptr"])
                S.op("dve", lambda e, tb=tb: e.tensor_copy(out=a_v[:, 4 + tb * 4:8 + tb * 4, 0:128], in_=p_tr[:, 0:4, :]), ["ptr", "a_v_init"], [("a_v", 1 + tb)])
            def sbanks(qt):
                if qt % 2 == 0:
                    return (p_acc[0], ("acc", 0)), (p_acc[1], ("acc", 1)), (p_acc[2], ("acc", 2))
                return (p_gu[0], ("gu", 0)), (p_gu[1], ("gu", 1)), (p_acc[3], ("acc", 3))

            def att_S(qt):
                (bA, kA), (bB, kB), _ = sbanks(qt)
                kreads = sorted(set(("a_k", (qt + kt) // 4) for kt in range(5)))
                for kt in range(5):
                    bank, bkey = (bA, kA) if kt < 4 else (bB, kB)
                    col = (kt % 4) * 128
                    S.op("pe", lambda e, kt=kt, bank=bank, col=col: e.matmul(
                        bank[:, col:col + 128], lhsT=a_k[:, (qt + kt) * 128:(qt + kt + 1) * 128], rhs=a_q[:, qt * 128:(qt + 1) * 128],
                        start=True, stop=False), kreads + [("a_q", qt // 4)], [bkey])
                    S.op("pe", lambda e, kt=kt, bank=bank, col=col: e.matmul(
                        bank[:, col:col + 128], lhsT=ident[:], rhs=tab[:, h, kt * 128:(kt + 1) * 128],
                        start=False, stop=True), [("tab", h), "ident"], [bkey])

            def att_rest(qt):
                pp = qt % 2
                (bA, kA), (bB, kB), (bP, kP) = sbanks(qt)
                vreads = sorted(set(("a_v", (qt + kt) // 4) for kt in range(5)))
                nh = max(0, min(4, 4 - qt))
                if nh > 0:
                    S.op("act", lambda e: e.activation(out=a_p[:, pp, 0:nh * 128], in_=bA[:, 0:nh * 128], func=AF.Exp, scale=SCALE, bias=hneg[:, 0:1]),
                         [kA, "hneg"], [("a_p", pp)])
                if nh < 4:
                    S.op("act", lambda e: e.activation(out=a_p[:, pp, nh * 128:512], in_=bA[:, nh * 128:512], func=AF.Exp, scale=SCALE),
                         [kA], [("a_p", pp)])
                S.op("act", lambda e: e.activation(out=a_p[:, pp, 512:640], in_=bB[:, 0:128], func=AF.Exp, scale=SCALE),
                     [kB], [("a_p", pp)])
                for kt in range(5):
                    S.op("pe", lambda e, kt=kt: e.matmul(bP[:, 0:129], lhsT=a_p[:, pp, kt * 128:(kt + 1) * 128], rhs=a_v[:, qt + kt, 0:129],
                                                         start=(kt == 0), stop=(kt == 4)), [("a_p", pp)] + vreads, [kP])
                S.op("dve", lambda e: e.reciprocal(out=a_rd[:, pp:pp + 1], in_=bP[:, 128:129]), [kP], [("a_rd", pp)])
                S.op("dve", lambda e: e.tensor_scalar(out=a_ab[:, pp, :], in0=bP[:, 0:128], scalar1=a_rd[:, pp:pp + 1], scalar2=None, op0=ALU.mult),
                     [kP, ("a_rd", pp)], [("a_ab", pp)])
                S.op("act", lambda e: e.activation(out=a_jk, in_=a_ab[:, pp, :], func=AF.Square, accum_out=ssa[:, qt, h:h + 1]),
                     [("a_ab", pp)], ["a_jk", ("ssa", qt, h)])
                S.op("pe", lambda e: e.transpose(out=p_tr[:, qt % 4, :], in_=a_ab[:, pp, :], identity=ident[:]), [("a_ab", pp), "ident"], ["ptr"])
                if qt % 4 == 3:
                    g = qt // 4
                    ms = (h * 4 + g) % 2
                    S.op("dve", lambda e: e.tensor_scalar(out=mst_a[:, ms, :].rearrange("p (a b) -> p a b", a=4), in0=p_tr[:, 0:4, :], scalar1=gatc[:, h:h + 1], scalar2=None, op0=ALU.mult),
                         ["ptr", "gatc"], [("mst", ms)])
                    dma("sp", md[h, :, g * 512:(g + 1) * 512], mst_a[:, ms, :], [("mst", ms)], [("md", h, g)], ("mst", ms))

            att_S(0)
            for qt_ in range(16):
                if qt_ + 1 < 16:
                    att_S(qt_ + 1)
                att_rest(qt_)


        if stop_after != "H":
            for h in range(int(os.environ.get('ATT_HEADS', '8'))):
                attention_head(h)
            S.barrier()
        if stop_after not in ("H", "att"):
            S.op("dve", lambda e: e.memset(w_e0, 0.0), [], ["e0"])
            S.op("dve", lambda e: e.memset(w_e1, 0.0), [], ["e1"])
            hgrn_phase(main=True)
            S.barrier()

        if stop_after is None:
            S.op("dve", lambda e: e.tensor_reduce(out=rsa[:], in_=ssa[:], axis=AX.X, op=ALU.add), [], ["rsa"])
            S.op("act", lambda e: e.activation(out=rsa[:], in_=rsa[:], func=AF.Sqrt, scale=1.0 / 1024, bias=EPS), ["rsa"], ["rsa"])
            S.op("dve", lambda e: e.reciprocal(out=rsa[:], in_=rsa[:]), ["rsa"], ["rsa"])
            for blk in range(NT // TB):
                t0 = blk * TB
                dma("sp", mb, md[:, :, t0:t0 + TB].rearrange("k p t -> p k t"), [], ["mb"], "mb")
                for tt in range(4):
                    dma("sp", x1[:, tt, :], xw[NHIST + t0 + tt * 128:NHIST + t0 + (tt + 1) * 128, :], [], [("x1", tt)], ("x1", tt))
                for dblk in range(4):
                    slot = wslot()
                    wload_cols(slot, 0, w_out, dblk * 512, 512)
                    wv = wview(slot)
                    for tt in range(4):
                        qt = blk * 4 + tt
                        bA, bR = (0, 1) if tt % 2 == 0 else (2, 3)
                        for k in range(16):
                            bank = bA if k < 8 else bR
                            S.op("pe", lambda e, k=k, tt=tt, bank=bank, wv=wv: e.matmul(
                                p_acc[bank][:], lhsT=mb[:, k, tt * 128:(tt + 1) * 128], rhs=wv[:, k, :], start=(k % 8 == 0), stop=(k % 8 == 7)),
                                ["mb"] + wkeys(slot), [("acc", bank)])
                        S.op("dve", lambda e, tt=tt, qt=qt, bA=bA, dblk=dblk: e.scalar_tensor_tensor(
                            out=x1[:, tt, dblk * 512:(dblk + 1) * 512], in0=p_acc[bA][:], scalar=rsa[:, qt:qt + 1], in1=x1[:, tt, dblk * 512:(dblk + 1) * 512],
                            op0=ALU.mult, op1=ALU.add), [("acc", bA), "rsa", ("x1", tt)], [("x1", tt)])
                        S.op("dve", lambda e, tt=tt, bR=bR, dblk=dblk: e.tensor_tensor(
                            out=x1[:, tt, dblk * 512:(dblk + 1) * 512], in0=x1[:, tt, dblk * 512:(dblk + 1) * 512], in1=p_acc[bR][:], op=ALU.add),
                            [("acc", bR), ("x1", tt)], [("x1", tt)])
                if debug and "x1" in debug:
                    for tt in range(4):
                        dma("sp", dbg["x1"][t0 + tt * 128:t0 + (tt + 1) * 128, :], x1[:, tt, :], [("x1", tt)], [], ("dbgx1", tt))
                        finals.append(len(S.ops) - 1)
                for tt in range(4):
                    norm_to_T(x1[:, tt, :], [("x1", tt)], h2T, "h2T", tt * 128, g2c, "g2c")
                h2keys = tkeys("h2T", 0, 512)
                for fp in range(NFT // 2):
                    slot = wslot()
                    wload_cols(slot, 0, w_gate, fp * 256, 256)
                    wload_cols(slot, 256, w_up, fp * 256, 256)
                    wv = wview(slot)
                    for fi in range(2):
                        ft = fp * 2 + fi
                        bg = gu_bank()
                        for k in range(16):
                            S.op("pe", lambda e, k=k, fi=fi, bg=bg, wv=wv: e.matmul(p_gu[bg][:], lhsT=wv[:, k, fi * 128:(fi + 1) * 128], rhs=h2T[:, k, :],
                                                                                  start=(k == 0), stop=(k == 15)), [("w", slot, fi)] + h2keys, [("gu", bg)])
                        bu = gu_bank()
                        for k in range(16):
                            S.op("pe", lambda e, k=k, fi=fi, bu=bu, wv=wv: e.matmul(p_gu[bu][:], lhsT=wv[:, k, 256 + fi * 128:256 + (fi + 1) * 128], rhs=h2T[:, k, :],
                                                                                  start=(k == 0), stop=(k == 15)), [("w", slot, 2 + fi)] + h2keys, [("gu", bu)])
                        sl = ft % 2
                        S.op("act", lambda e, bg=bg, sl=sl: e.activation(out=silb[:, sl, :], in_=p_gu[bg][:], func=AF.Silu), [("gu", bg)], [("sil", sl)])
                        S.op("dve", lambda e, bu=bu, sl=sl, ft=ft: e.tensor_tensor(out=ffT[:, ft, :], in0=p_gu[bu][:], in1=silb[:, sl, :], op=ALU.mult),
                             [("gu", bu), ("sil", sl)], [("ffT", ft)])
                for dblk in range(4):
                    for fg, (f0, nf) in enumerate(((0, 16), (16, 16), (32, 12))):
                        slot = wslot()
                        wv = wview(slot)
                        dma("pool", wv[:, 0:nf, :], w_down[f0 * 128:(f0 + nf) * 128, dblk * 512:(dblk + 1) * 512].rearrange("(k p) c -> p k c", p=128),
                            [], wkeys(slot), ("w", slot, 0))
                        for fl in range(nf):
                            fc = f0 + fl
                            for tt in range(4):
                                S.op("pe", lambda e, fl=fl, fc=fc, tt=tt, wv=wv: e.matmul(p_acc[tt][:], lhsT=ffT[:, fc, tt * 128:(tt + 1) * 128], rhs=wv[:, fl, :],
                                                                                        start=(fc == 0), stop=(fc == NFT - 1)), wkeys(slot) + [("ffT", fc)], [("acc", tt)])
                    for tt in range(4):
                        S.op("dve", lambda e, tt=tt, dblk=dblk: e.tensor_tensor(
                            out=x1[:, tt, dblk * 512:(dblk + 1) * 512], in0=x1[:, tt, dblk * 512:(dblk + 1) * 512], in1=p_acc[tt][:], op=ALU.add),
                            [("acc", tt), ("x1", tt)], [("x1", tt)])
                for tt in range(4):
                    i = 32 + (blk * 4 + tt) % 32
                    S.op("act", lambda e, tt=tt, i=i: e.activation(out=junk[:], in_=x1[:, tt, :], func=AF.Square, accum_out=ss[:, i:i + 1]),
                         [("x1", tt)], ["junk", ("ss", i)])
                    S.op("act", lambda e, i=i: e.activation(out=sd[:, i:i + 1], in_=ss[:, i:i + 1], func=AF.Sqrt, scale=1.0 / D, bias=EPS), [("ss", i)], [("sd", i)])
                    S.op("dve", lambda e, i=i: e.reciprocal(out=rstd[:, i:i + 1], in_=sd[:, i:i + 1]), [("sd", i)], [("rstd", i)])
                    S.op("dve", lambda e, tt=tt, i=i: e.scalar_tensor_tensor(out=yo, in0=x1[:, tt, :], scalar=rstd[:, i:i + 1], in1=gfb[:],
                                                                           op0=ALU.mult, op1=ALU.mult), [("x1", tt), ("rstd", i), "gfb"], ["yo"])
                    dma("sp", y[t0 + tt * 128:t0 + (tt + 1) * 128, :], yo, ["yo"], [], "yo")
                    finals.append(len(S.ops) - 1)

        if debug and "md" in debug:
            S.barrier()
            for m in range(16):
                for g in range(4):
                    dma("sp", hb[:, 0:512], md[m, :, g * 512:(g + 1) * 512], [], ["hb"], "hbd")
                    S.op("dve", lambda e: e.tensor_copy(out=gfb[:, 0:512], in_=hb[:, 0:512]), ["hb"], ["gfb"])
                    dma("sp", dbg["md"][m, :, g * 512:(g + 1) * 512], gfb[:, 0:512], ["gfb"], [], "gfbd")
                    finals.append(len(S.ops) - 1)
        stats = S.emit(final_waits=finals)
    return nc, stats


def _consts():
    ident = np.eye(128, dtype=np.float32)
    kk = np.arange(128)[:, None]
    amask = np.zeros((128, 640), np.float32)
    iq = np.arange(128)[None, :]
    amask[:, 0:128] = np.where((iq >= 64) & (kk < 64), -1e5, 0.0)
    amask[:, 512:640] = np.where((iq < 64) & (kk >= 64), -1e5, 0.0)
    s = np.arange(128)[:, None]
    t = np.arange(128)[None, :]
    tri = ((s // 64 == t // 64) & (s <= t)).astype(np.float32)
    cmask = np.tile(tri, (1, 4))
    smask = np.ones((128, 512), np.float32)
    smask[:, 0::64] = 0.0
    return ident, amask, cmask, smask


def _bias_index():
    kt = np.arange(5)[None, :, None]
    kk = np.arange(128)[:, None, None]
    iq = np.arange(128)[None, None, :]
    dist = 512 - 128 * kt + iq - kk
    return (np.clip(dist, -128, 128) + 128).reshape(128, 640)


_PROG = {}


def kernel(x, norm1_gain, w_in, rel_bias, lower_bounds, grn_norm_gain, attn_out_gain, w_out, norm2_gain,
           w_gate, w_up, w_down, final_gain):
    x = np.asarray(x, np.float32)
    if "nc" not in _PROG:
        _PROG["nc"] = build_program()[0]
    nc = _PROG["nc"]
    ident, amask, cmask, smask = _consts()
    idx = _bias_index()
    btab = np.ascontiguousarray(np.asarray(rel_bias, np.float32)[0][:, idx])
    shared = {
        "w_in": np.ascontiguousarray(np.asarray(w_in, np.float32)[0]),
        "w_out": np.ascontiguousarray(np.asarray(w_out, np.float32)[0]),
        "w_gate": np.ascontiguousarray(np.asarray(w_gate, np.float32)[0]),
        "w_up": np.ascontiguousarray(np.asarray(w_up, np.float32)[0]),
        "w_down": np.ascontiguousarray(np.asarray(w_down, np.float32)[0]),
        "g1": np.asarray(norm1_gain, np.float32).reshape(1, D),
        "g2": np.asarray(norm2_gain, np.float32).reshape(1, D),
        "gf": np.asarray(final_gain, np.float32).reshape(1, D),
        "gat": np.asarray(attn_out_gain, np.float32).reshape(1, 1024),
        "ggr": np.asarray(grn_norm_gain, np.float32).reshape(1, 128),
        "lbnd": np.ascontiguousarray(np.asarray(lower_bounds, np.float32)),
        "btab": btab, "c_ident": ident, "c_amask": amask, "c_cmask": cmask, "c_smask": smask,
    }
    in_maps = []
    for c in range(8):
        b, half = c // 2, c % 2
        xwc = np.zeros((NHIST + NT, D), np.float32)
        if half == 1:
            xwc[:] = x[b]
        else:
            xwc[NHIST:] = x[b, :NT]
        m = dict(shared)
        m["xw"] = xwc
        m["c_hneg"] = np.full((128, 1), 0.0 if half == 1 else -30000.0, np.float32)
        in_maps.append(m)
    res = run_bass_kernel_spmd(nc, in_maps, core_ids=list(range(8)))
    out = np.empty((4, 4096, D), np.float32)
    for c in range(8):
        b, half = c // 2, c % 2
        out[b, half * NT:(half + 1) * NT] = res.results[c]["y"]
    return out
```

```python
import contextlib
import os
import numpy as np
import ml_dtypes
import concourse.bass as bass
import concourse.mybir as mybir
from concourse.bass_utils import run_bass_kernel_spmd

F32 = mybir.dt.float32
BF16 = mybir.dt.bfloat16
AF = mybir.ActivationFunctionType
ALU = mybir.AluOpType
AX = mybir.AxisListType

ENGS = ("pe", "act", "dve", "pool", "sp")

D = 2048
NT = 2048
NHIST = 2048
KD = 16
DFF = 5632
NFT = 44
EPS = 1e-6
INCOLS = 7168
TB = 512
SCALE = 128 ** -0.5
SQ128 = 128 ** 0.5


class Sched:
    def __init__(self, nc, same_engine_sync=True):
        self.nc = nc
        self.ops = []
        self.last_writer = {}
        self.readers = {}
        self.same_engine_sync = same_engine_sync
        self.dma_keys = {}

    def op(self, eng, fn, reads=(), writes=(), dma_key=None):
        idx = len(self.ops)
        deps = set()
        for k in reads:
            w = self.last_writer.get(k)
            if w is not None:
                deps.add(w)
        for k in writes:
            w = self.last_writer.get(k)
            if w is not None:
                deps.add(w)
            for r in self.readers.get(k, ()):
                deps.add(r)
        if dma_key is not None:
            prev = self.dma_keys.get(dma_key)
            if prev is not None:
                deps.add(prev)
            self.dma_keys[dma_key] = idx
        deps.discard(idx)
        self.ops.append(dict(eng=eng, fn=fn, deps=deps, dma_key=dma_key, needed=False))
        for k in writes:
            self.last_writer[k] = idx
            self.readers[k] = []
        for k in reads:
            if k not in writes:
                lst = self.readers.setdefault(k, [])
                if dma_key is None:
                    lst[:] = [r for r in lst if not (self.ops[r]["eng"] == eng and self.ops[r]["dma_key"] is None)]
                lst.append(idx)
        return idx

    def barrier(self):
        last = {}
        for i, o in enumerate(self.ops):
            if o["fn"] is None:
                continue
            last[("e", o["eng"])] = i
            if o["dma_key"] is not None:
                last[("d", o["dma_key"])] = i
        deps = set(last.values())
        for e in ENGS:
            self.ops.append(dict(eng=e, fn=None, deps=set(deps), dma_key=None, needed=False))

    def emit(self, final_waits=()):
        nc = self.nc
        ops = self.ops
        for i, o in enumerate(ops):
            nd = set()
            for d in o["deps"]:
                p = ops[d]
                if p["dma_key"] is None and p["eng"] == o["eng"] and not self.same_engine_sync:
                    continue
                if p["dma_key"] is None and p["eng"] == "pe" and o["eng"] == "pe":
                    continue
                nd.add(d)
            o["deps"] = nd
            for d in nd:
                ops[d]["needed"] = True
        for d in final_waits:
            ops[d]["needed"] = True
        tick = {e: 0 for e in ENGS}
        dma_cnt = {}
        for o in ops:
            if o["dma_key"] is not None:
                dma_cnt[o["dma_key"]] = dma_cnt.get(o["dma_key"], 0) + 1
                o["sem"] = ("dma", o["dma_key"])
                o["tick"] = 16 * dma_cnt[o["dma_key"]]
            elif o["needed"]:
                tick[o["eng"]] += 1
                o["sem"] = ("eng", o["eng"])
                o["tick"] = tick[o["eng"]]
        sem_names = [("eng", e) for e in ENGS] + [("dma", k) for k in dma_cnt]
        with contextlib.ExitStack() as st:
            sems = {}
            for i, sn in enumerate(sem_names):
                sems[sn] = st.enter_context(nc.semaphore("s%d" % i))
            block = st.enter_context(nc.Block())
            per_eng = {e: [] for e in ENGS}
            for i, o in enumerate(ops):
                per_eng[o["eng"]].append(i)

            def body(ename, engine):
                waited = {}
                for i in per_eng[ename]:
                    o = ops[i]
                    need = {}
                    for d in o["deps"]:
                        p = ops[d]
                        s = p["sem"]
                        need[s] = max(need.get(s, 0), p["tick"])
                    for s, v in need.items():
                        if waited.get(s, 0) >= v:
                            continue
                        engine.wait_ge(sems[s], v)
                        waited[s] = v
                    if o["fn"] is None:
                        continue
                    ins = o["fn"](engine)
                    if o["dma_key"] is not None:
                        ins.then_inc(sems[o["sem"]], 16)
                    elif o["needed"]:
                        ins.then_inc(sems[o["sem"]], 1)
                if ename == "sp":
                    need = {}
                    for d in final_waits:
                        p = ops[d]
                        need[p["sem"]] = max(need.get(p["sem"], 0), p["tick"])
                    for s, v in need.items():
                        engine.wait_ge(sems[s], v)

            @block.tensor
            def _(e):
                body("pe", e)

            @block.scalar
            def _(e):
                body("act", e)

            @block.vector
            def _(e):
                body("dve", e)

            @block.gpsimd
            def _(e):
                body("pool", e)

            @block.sync
            def _(e):
                body("sp", e)
        return {e: len(per_eng[e]) for e in ENGS}, tick, len(sem_names)


def build_program(debug=None, stop_after=None):
    nc = bass.Bass("TRN2", target_bir_lowering=False)
    dt_in = lambda name, shape: nc.dram_tensor(name, shape, F32, kind="ExternalInput").ap()
    xw = dt_in("xw", [NHIST + NT, D])
    w_in = dt_in("w_in", [D, INCOLS])
    w_out = dt_in("w_out", [D, D])
    w_gate = dt_in("w_gate", [D, DFF])
    w_up = dt_in("w_up", [D, DFF])
    w_down = dt_in("w_down", [DFF, D])
    g1_d = dt_in("g1", [1, D])
    g2_d = dt_in("g2", [1, D])
    gf_d = dt_in("gf", [1, D])
    gat_d = dt_in("gat", [1, 1024])
    ggr_d = dt_in("ggr", [1, 128])
    lbnd_d = dt_in("lbnd", [2, 1024])
    btab_d = dt_in("btab", [8, 128, 640])
    cid_d = dt_in("c_ident", [128, 128])
    cam_d = dt_in("c_amask", [128, 640])
    ccm_d = dt_in("c_cmask", [128, 512])
    csm_d = dt_in("c_smask", [128, 512])
    chn_d = dt_in("c_hneg", [128, 1])
    y = nc.dram_tensor("y", [NT, D], F32, kind="ExternalOutput").ap()
    md = nc.dram_tensor("md", [16, 128, NT], BF16, kind="Internal").ap()
    kvd = nc.dram_tensor("kvd", [8, 128, 1024], BF16, kind="Internal").ap()
    dbg = {}
    if debug:
        for name, shape in debug.items():
            dbg[name] = nc.dram_tensor("dbg_" + name, shape, F32, kind="ExternalOutput").ap()

    S = Sched(nc)
    finals = []
    with contextlib.ExitStack() as st:
        def sb(name, shape, dt=F32):
            return st.enter_context(nc.sbuf_tensor(name, shape, dt))

        def ps(name, shape, dt=F32):
            return st.enter_context(nc.psum_tensor(name, shape, dt))

        R2 = sb("R2", [128, 16, 2048], BF16)
        R1 = sb("R1", [128, 28672], BF16)
        NSLOT = 3
        wpool = sb("wpool", [128, NSLOT, 8192], BF16)
        ident = sb("ident", [128, 128], BF16)
        identF = sb("identF", [128, 128], F32)
        onesF = sb("onesF", [128, 128], F32)
        amask = sb("amask", [128, 640], F32)
        cmask = sb("cmask", [128, 512], F32)
        smask = sb("smask", [128, 512], F32)
        hneg = sb("hneg", [128, 1], F32)
        g1c = sb("g1c", [128, 16], F32)
        g2c = sb("g2c", [128, 16], F32)
        gfb = sb("gfb", [128, 2048], F32)
        gatc = sb("gatc", [128, 8], F32)
        ggrc = sb("ggrc", [128, 1], F32)
        lbc = sb("lbc", [128, 8], F32)
        lb1 = sb("lb1", [128, 8], F32)
        omlc = sb("omlc", [128, 8], F32)
        nomlc = sb("nomlc", [128, 8], F32)
        junk = sb("junk", [128, 2048], BF16)
        hb = sb("hb", [128, 2048], BF16)
        silb = sb("silb", [128, 2, 512], F32)
        ss = sb("ss", [128, 64], F32)
        sd = sb("sd", [128, 64], F32)
        rstd = sb("rstd", [128, 64], F32)
        ssa = sb("ssa", [128, 16, 8], F32)
        rsa = sb("rsa", [128, 16], F32)

        def carve(off, shape, dt):
            n = 1
            for s_ in shape[1:]:
                n *= s_
            esz = 2 if dt == BF16 else 4
            a = R1[:, off // 2: off // 2 + n * esz // 2]
            if dt == F32:
                a = a.bitcast(F32)
            if len(shape) == 3:
                a = a.rearrange("p (a b) -> p a b", a=shape[1])
            elif len(shape) == 4:
                a = a.rearrange("p (a b c) -> p a b c", a=shape[1], b=shape[2])
            return a

        xin = carve(0, [128, 2048], F32)
        Sfin = carve(8192, [128, 8, 128], F32)
        Sall = carve(12288, [128, 8, 128], F32)
        Sb = carve(16384, [128, 8, 128], BF16)
        tab = carve(18432, [128, 8, 640], BF16)
        WK = 28672
        KB2 = 2048
        w_sgn = carve(WK + 0 * KB2, [128, 512], F32)
        w_f = carve(WK + 1 * KB2, [128, 512], F32)
        w_b = carve(WK + 2 * KB2, [128, 512], F32)
        w_e0 = carve(WK + 3 * KB2, [128, 512], F32)
        w_e1 = carve(WK + 4 * KB2, [128, 512], F32)
        w_e = carve(WK + 5 * KB2, [128, 512], F32)
        w_qs = carve(WK + 6 * KB2, [128, 512], F32)
        w_gs0 = carve(WK + 7 * KB2, [128, 512], F32)
        o = WK + 8 * KB2
        w_qd0 = carve(o, [128, 512], BF16); o += 1024
        w_kd0 = carve(o, [128, 512], BF16); o += 1024
        w_kk0 = carve(o, [128, 512], BF16); o += 1024
        w_kk1 = carve(o, [128, 512], BF16); o += 1024
        w_vi = carve(o, [128, 512], BF16); o += 1024
        w_tm0 = carve(o, [128, 12, 128], BF16); o += 3072
        w_am = carve(o, [128, 512], BF16); o += 1024
        mst = carve(o, [128, 2, 512], BF16); o += 2048
        w_eb8 = carve(o, [128, 2, 8], F32); o += 64
        assert o <= 57344, o
        o = 18432
        w_gs1 = carve(o, [128, 512], F32); o += 2048
        w_X = carve(o, [128, 512], F32); o += 2048
        w_qd1 = carve(o, [128, 512], BF16); o += 1024
        w_kd1 = carve(o, [128, 512], BF16); o += 1024
        w_tm1 = carve(o, [128, 12, 128], BF16); o += 3072
        assert o <= 28672, o
        w_gsP = (w_gs0, w_gs1)
        w_qdP = (w_qd0, w_qd1)
        w_kdP = (w_kd0, w_kd1)
        w_tmP = (w_tm0, w_tm1)
        o = WK
        a_q = carve(o, [128, 2048], BF16); o += 4096
        a_k = carve(o, [128, 2560], BF16); o += 5120
        a_vT = carve(o, [128, 2, 512], BF16); o += 2048
        a_v = carve(o, [128, 20, 132], BF16); o += 5280
        a_p = carve(o, [128, 2, 640], BF16); o += 2560
        a_ab = carve(o, [128, 2, 128], BF16); o += 512
        a_jk = carve(o, [128, 128], BF16); o += 256
        a_rd = carve(o, [128, 2], F32); o += 8
        o = (o + 63) // 64 * 64
        mst_a = carve(o, [128, 2, 512], BF16); o += 2048
        assert o <= 57344, o
        mb = carve(0, [128, 16, 512], BF16)
        x1 = carve(16384, [128, 4, 2048], F32)
        yo = carve(49152, [128, 2048], F32)
        r2f = R2[:].rearrange("p k t -> p (k t)")
        ffT = r2f[:, 0:22528].rearrange("p (f t) -> p f t", f=NFT)
        h2T = r2f[:, 22528:30720].rearrange("p (k t) -> p k t", k=16)

        p_acc = [ps("pacc%d" % i, [128, 512]) for i in range(4)]
        p_gu = [ps("pgu%d" % i, [128, 512]) for i in range(3)]
        p_tr = ps("ptr", [128, 8, 128], BF16)

        def dma(eng, out, in_, reads, writes, key, slow=False):
            if slow:
                return S.op(eng, lambda e: e.dma_start(out=out, in_=in_, allow_slow_non_contiguous=True), reads=reads, writes=writes, dma_key=key)
            return S.op(eng, lambda e: e.dma_start(out=out, in_=in_), reads=reads, writes=writes, dma_key=key)

        dma("sp", identF[:], cid_d[:, :], [], ["identF"], "c0")
        dma("sp", amask[:], cam_d[:, :], [], ["amask"], "c1")
        dma("sp", cmask[:], ccm_d[:, :], [], ["cmask"], "c2")
        dma("sp", smask[:], csm_d[:, :], [], ["smask"], "c3")
        dma("sp", hneg[:], chn_d[:, :], [], ["hneg"], "c4")
        dma("sp", gfb[:], gf_d.broadcast_to([128, D]), [], ["gfb"], "c5")
        cvec = lambda d_: d_.rearrange("o (k p) -> p (o k)", p=128)
        dma("sp", g1c[:], cvec(g1_d), [], ["g1c"], "c6", slow=True)
        dma("sp", g2c[:], cvec(g2_d), [], ["g2c"], "c7", slow=True)
        dma("sp", gatc[:], cvec(gat_d), [], ["gatc"], "c8", slow=True)
        dma("sp", ggrc[:], cvec(ggr_d), [], ["ggrc"], "c9", slow=True)
        dma("sp", lbc[:], cvec(lbnd_d[0:1, :]), [], ["lbc"], "c10", slow=True)
        dma("sp", lb1[:], cvec(lbnd_d[1:2, :]), [], ["lb1"], "c11", slow=True)
        S.op("dve", lambda e: e.tensor_copy(out=ident[:], in_=identF[:]), ["identF"], ["ident"])
        S.op("dve", lambda e: e.memset(onesF[:], 1.0), [], ["onesF"])
        S.op("dve", lambda e: e.memset(Sfin, 0.0), [], [("S", h) for h in range(8)])
        S.op("dve", lambda e: e.tensor_tensor(out=lbc[:], in0=lbc[:], in1=lb1[:], op=ALU.subtract), ["lbc", "lb1"], ["lbc"])
        S.op("act", lambda e: e.activation(out=lbc[:], in_=lbc[:], func=AF.Sigmoid), ["lbc"], ["lbc"])
        S.op("dve", lambda e: e.tensor_scalar(out=omlc[:], in0=lbc[:], scalar1=-1.0, scalar2=1.0, op0=ALU.mult, op1=ALU.add), ["lbc"], ["omlc"])
        S.op("dve", lambda e: e.tensor_scalar(out=nomlc[:], in0=lbc[:], scalar1=1.0, scalar2=-1.0, op0=ALU.mult, op1=ALU.add), ["lbc"], ["nomlc"])
        def build_tab():
            for h in range(8):
                dma("sp", xin[:, 0:640], btab_d[h], [], ["xin"], "xin")
                S.op("dve", lambda e, h=h: e.scalar_tensor_tensor(out=tab[:, h, :], in0=xin[:, 0:640], scalar=SQ128, in1=amask[:],
                                                                  op0=ALU.mult, op1=ALU.add), ["xin", "amask"], [("tab", h)])

        wstate = {"n": 0}

        def wslot():
            s = wstate["n"] % NSLOT
            wstate["n"] += 1
            return s

        def wview(slot):
            return wpool[:, slot, :].rearrange("p (k c) -> p k c", k=16)

        def wkeys(slot, q0=0, q1=4):
            return [("w", slot, q) for q in range(q0, q1)]

        def wload_cols(slot, dst_off, src, col0, ncols):
            q0, q1 = dst_off // 128, (dst_off + ncols) // 128
            dma("pool", wview(slot)[:, :, dst_off:dst_off + ncols],
                src[:, col0:col0 + ncols].rearrange("(k p) c -> p k c", p=128),
                [], wkeys(slot, q0, q1), ("w", slot, q0))

        ncount = {"n": 0}

        def norm_to_T(src_ap, src_reads, dstT, dname, col0, gain_c, gkey, hbuf=None, hkey="hb"):
            hbv = hb[:] if hbuf is None else hbuf
            i = ncount["n"] % 32
            ncount["n"] += 1
            S.op("act", lambda e: e.activation(out=junk[:], in_=src_ap, func=AF.Square, accum_out=ss[:, i:i + 1]),
                 src_reads, ["junk", ("ss", i)])
            S.op("act", lambda e: e.activation(out=sd[:, i:i + 1], in_=ss[:, i:i + 1], func=AF.Sqrt, scale=1.0 / D, bias=EPS),
                 [("ss", i)], [("sd", i)])
            S.op("dve", lambda e: e.reciprocal(out=rstd[:, i:i + 1], in_=sd[:, i:i + 1]), [("sd", i)], [("rstd", i)])
            S.op("dve", lambda e: e.tensor_scalar(out=hbv, in0=src_ap, scalar1=rstd[:, i:i + 1], scalar2=None, op0=ALU.mult),
                 src_reads + [("rstd", i)], [hkey])
            for half in range(2):
                for j in range(8):
                    k = half * 8 + j
                    S.op("pe", lambda e, k=k, j=j: e.transpose(out=p_tr[:, j, :], in_=hbv[:, k * 128:(k + 1) * 128], identity=ident[:]),
                         [hkey, "ident"], ["ptr"])
                S.op("dve", lambda e, half=half: e.tensor_tensor(
                    out=dstT[:, half * 8:half * 8 + 8, col0:col0 + 128], in0=p_tr[:],
                    in1=gain_c[:, half * 8:half * 8 + 8].unsqueeze(2).broadcast_to([128, 8, 128]), op=ALU.mult),
                    ["ptr", gkey], [("T", dname, col0, half)])

        def tkeys(dname, tok0, ntok):
            return [("T", dname, c, half) for c in range(tok0, tok0 + ntok, 128) for half in range(2)]

        xin2 = carve(WK, [128, 2048], F32)
        hb2 = carve(WK + 8192, [128, 2048], BF16)

        def load_norm_x(ti, col0):
            if ti % 2 == 0:
                dma("sp", xin, xw[ti * 128:(ti + 1) * 128, :], [], ["xin"], "xin")
                norm_to_T(xin, ["xin"], R2, "R2", col0, g1c, "g1c")
            else:
                dma("sp", xin2, xw[ti * 128:(ti + 1) * 128, :], [], ["xin2"], "xin2")
                norm_to_T(xin2, ["xin2"], R2, "R2", col0, g1c, "g1c", hbuf=hb2, hkey="hb2")

        gu_rr = {"n": 0}

        def gu_bank():
            b = gu_rr["n"] % 3
            gu_rr["n"] += 1
            return b

        def proj(slot, coff, tok0, bank):
            v = wview(slot)
            for k in range(KD):
                S.op("pe", lambda e, k=k: e.matmul(p_gu[bank][:], lhsT=v[:, k, coff:coff + 128], rhs=R2[:, k, tok0:tok0 + 512],
                                                   start=(k == 0), stop=(k == KD - 1)),
                     [("w", slot, coff // 128)] + tkeys("R2", tok0, 512), [("gu", bank)])

        for ti in range(16):
            load_norm_x(ti, ti * 128)
        S.barrier()
        S.op("dve", lambda e: e.memset(w_e0, 0.0), [], ["e0"])
        S.op("dve", lambda e: e.memset(w_e1, 0.0), [], ["e1"])

        def hgrn_front(h, tb, slot, main, p):
            tok0 = tb * 512
            cf, ci = (128, 256) if main else (0, 128)
            w_qd, w_kd, w_tm, w_gs = w_qdP[p], w_kdP[p], w_tmP[p], w_gsP[p]
            if main:
                bq = gu_bank()
                proj(slot, 0, tok0, bq)
                S.op("act", lambda e: e.activation(out=w_qs, in_=p_gu[bq][:], func=AF.Sigmoid), [("gu", bq)], ["qs"])
                S.op("dve", lambda e: e.tensor_tensor(out=w_qs, in0=p_gu[bq][:], in1=w_qs, op=ALU.mult), [("gu", bq), "qs"], ["qs"])
            bf = gu_bank()
            proj(slot, cf, tok0, bf)
            S.op("act", lambda e: e.activation(out=w_sgn, in_=p_gu[bf][:], func=AF.Sigmoid, scale=-1.0), [("gu", bf)], ["sgn"])
            if main:
                bg = gu_bank()
                proj(slot, 384, tok0, bg)
                S.op("act", lambda e: e.activation(out=w_gs, in_=p_gu[bg][:], func=AF.Sigmoid), [("gu", bg)], [("gs", p)])
                S.op("dve", lambda e: e.tensor_tensor(out=w_gs, in0=p_gu[bg][:], in1=w_gs, op=ALU.mult), [("gu", bg), ("gs", p)], [("gs", p)])
            S.op("dve", lambda e: e.tensor_scalar(out=w_f, in0=w_sgn, scalar1=nomlc[:, h:h + 1], scalar2=1.0, op0=ALU.mult, op1=ALU.add),
                 ["sgn", "nomlc"], ["f"])
            S.op("act", lambda e: e.activation(out=w_f, in_=w_f, func=AF.Ln), ["f"], ["f"])
            S.op("dve", lambda e: e.tensor_tensor_scan(out=w_b, data0=smask[:], data1=w_f, initial=0.0, op0=ALU.mult, op1=ALU.add),
                 ["f", "smask"], ["b"])
            bi = gu_bank()
            proj(slot, ci, tok0, bi)
            S.op("act", lambda e: e.copy(out=w_vi, in_=p_gu[bi][:]), [("gu", bi)], ["vi"])
            bview = w_b.rearrange("p (c t) -> p c t", t=64)
            S.op("act", lambda e: e.activation(out=w_eb8[:, p, :].unsqueeze(2), in_=bview[:, :, 63:64], func=AF.Exp), ["b"], [("eb8", p)])
            for c in range(8):
                dst, dk = (w_e0, "e0") if c % 2 == 0 else (w_e1, "e1")
                S.op("act", lambda e, c=c, dst=dst: e.activation(out=dst[:, c * 64:(c + 1) * 64], in_=w_b[:, c * 64:(c + 1) * 64], func=AF.Exp,
                                                                 scale=-1.0, bias=w_b[:, c * 64 + 63:c * 64 + 64]), ["b"], [dk])
            S.op("dve", lambda e: e.scalar_tensor_tensor(out=w_kk0, in0=w_sgn, scalar=omlc[:, h:h + 1], in1=w_e0, op0=ALU.mult, op1=ALU.mult),
                 ["sgn", "e0", "omlc"], ["kk0"])
            S.op("dve", lambda e: e.scalar_tensor_tensor(out=w_kk1, in0=w_sgn, scalar=omlc[:, h:h + 1], in1=w_e1, op0=ALU.mult, op1=ALU.mult),
                 ["sgn", "e1", "omlc"], ["kk1"])
            if main:
                S.op("act", lambda e: e.activation(out=w_e, in_=w_b, func=AF.Exp), ["b"], ["e"])
                S.op("dve", lambda e: e.tensor_tensor(out=w_qd, in0=w_qs, in1=w_e, op=ALU.mult), ["qs", "e"], [("qd", p)])
                S.op("act", lambda e: e.activation(out=w_e, in_=w_b, func=AF.Exp, scale=-1.0), ["b"], ["e"])
                S.op("dve", lambda e: e.scalar_tensor_tensor(out=w_kd, in0=w_sgn, scalar=omlc[:, h:h + 1], in1=w_e, op0=ALU.mult, op1=ALU.mult),
                     ["sgn", "e", "omlc"], [("kd", p)])

        def hgrn_front_b(h, tb, main, p):
            w_tm = w_tmP[p]
            for j in range(4):
                S.op("pe", lambda e, j=j: e.transpose(out=p_tr[:, j, :], in_=w_vi[:, j * 128:(j + 1) * 128], identity=ident[:]), ["vi", "ident"], ["ptr"])
            for j in range(4):
                S.op("pe", lambda e, j=j: e.transpose(out=p_tr[:, 4 + j, :], in_=w_kk0[:, j * 128:(j + 1) * 128], identity=ident[:]), ["kk0", "ident"], ["ptr"])
            S.op("dve", lambda e: e.tensor_copy(out=w_tm[:, 0:8, :], in_=p_tr[:]), ["ptr"], [("tmA", p)])
            for j in range(4):
                S.op("pe", lambda e, j=j: e.transpose(out=p_tr[:, j, :], in_=w_kk1[:, j * 128:(j + 1) * 128], identity=ident[:]), ["kk1", "ident"], ["ptr"])
            S.op("dve", lambda e: e.tensor_copy(out=w_tm[:, 8:12, :], in_=p_tr[:, 0:4, :]), ["ptr"], [("tmB", p)])

        def hgrn_back(h, tb, main, p):
            tok0 = tb * 512
            w_qd, w_kd, w_tm, w_gs = w_qdP[p], w_kdP[p], w_tmP[p], w_gsP[p]
            for c in range(8):
                j, r = c // 2, c % 2
                bank = 1 + c // 4
                S.op("pe", lambda e, c=c, j=j, r=r, bank=bank: e.matmul(
                    p_acc[bank][:, (c % 4) * 128:(c % 4) * 128 + 128], lhsT=w_tm[:, 4 + 4 * r + j, :], rhs=w_tm[:, j, :],
                    start=True, stop=True), [("tmA", p), ("tmB", p)], [("acc", bank)])
            if main:
                for j in range(4):
                    S.op("pe", lambda e, j=j: e.matmul(p_acc[0][:, j * 128:(j + 1) * 128], lhsT=w_kd[:, j * 128:(j + 1) * 128], rhs=w_qd[:, j * 128:(j + 1) * 128],
                                                       start=True, stop=True), [("kd", p), ("qd", p)], [("acc", 0)])
                S.op("dve", lambda e: e.tensor_tensor(out=w_am, in0=p_acc[0][:], in1=cmask[:], op=ALU.mult), [("acc", 0), "cmask"], ["am"])
            if main:
                S.op("pool", lambda e: e.tensor_copy(out=Sb[:, 0, :], in_=Sfin[:, h, :]), [("S", h)], [("Sb", 0)])
            for c in range(8):
                bank = 1 + c // 4
                src, skey = (Sfin[:, h, :], ("S", h)) if c == 0 else (Sall[:, c, :], ("Sall", c))
                dst, dkey = (Sfin[:, h, :], ("S", h)) if c == 7 else (Sall[:, c + 1, :], ("Sall", c + 1))
                S.op("dve", lambda e, c=c, bank=bank, src=src, dst=dst: e.scalar_tensor_tensor(
                    out=dst, in0=src, scalar=w_eb8[:, p, c:c + 1], in1=p_acc[bank][:, (c % 4) * 128:(c % 4) * 128 + 128],
                    op0=ALU.mult, op1=ALU.add), [skey, ("eb8", p), ("acc", bank)], [dkey])
                if main and c == 3:
                    S.op("act", lambda e: e.copy(out=Sb[:, 1:5, :], in_=Sall[:, 1:5, :]), [("Sall", i) for i in range(1, 5)], [("Sb", i) for i in range(1, 5)])
                if main and c == 6:
                    S.op("act", lambda e: e.copy(out=Sb[:, 5:8, :], in_=Sall[:, 5:8, :]), [("Sall", i) for i in range(5, 8)], [("Sb", i) for i in range(5, 8)])

        def hgrn_back_b(h, tb, main, p):
            tok0 = tb * 512
            w_qd, w_kd, w_tm, w_gs = w_qdP[p], w_kdP[p], w_tmP[p], w_gsP[p]
            if not main:
                return
            for j in range(4):
                S.op("pe", lambda e, j=j: e.matmul(p_acc[3][:, j * 128:(j + 1) * 128], lhsT=w_tm[:, j, :], rhs=w_am[:, j * 128:(j + 1) * 128],
                                                   start=True, stop=False), [("tmA", p), "am"], [("acc", 3)])
                for r in range(2):
                    c = 2 * j + r
                    S.op("pe", lambda e, c=c, r=r: e.matmul(p_acc[3][:, c * 64:(c + 1) * 64], lhsT=Sb[:, c, :], rhs=w_qd[:, c * 64:(c + 1) * 64],
                                                            start=False, stop=(r == 1)), [("Sb", c), ("qd", p)], [("acc", 3)])
            S.op("act", lambda e: e.activation(out=w_X, in_=p_acc[3][:], func=AF.Square), [("acc", 3)], ["X"])
            S.op("pe", lambda e: e.matmul(p_acc[0][:], lhsT=onesF[:], rhs=w_X, start=True, stop=True), ["X", "onesF"], [("acc", 0)])
            S.op("act", lambda e: e.activation(out=w_X, in_=p_acc[0][:], func=AF.Ln, scale=1.0 / 128, bias=EPS), [("acc", 0)], ["X"])
            S.op("act", lambda e: e.activation(out=w_X, in_=w_X, func=AF.Exp, scale=-0.5), ["X"], ["X"])
            S.op("dve", lambda e: e.scalar_tensor_tensor(out=w_X, in0=p_acc[3][:], scalar=ggrc[:, 0:1], in1=w_X, op0=ALU.mult, op1=ALU.mult),
                 [("acc", 3), "X", "ggrc"], ["X"])
            ms = (h * 4 + tb) % 2
            S.op("dve", lambda e: e.tensor_tensor(out=mst[:, ms, :], in0=w_X, in1=w_gs, op=ALU.mult), ["X", ("gs", p)], [("mst", ms)])
            dma("sp", md[8 + h, :, tok0:tok0 + 512], mst[:, ms, :], [("mst", ms)], [("md", 8 + h, tb)], ("mst", ms))

        def hist_kv(h, slot):
            bk = gu_bank()
            proj(slot, 256, 1536, bk)
            S.op("act", lambda e: e.copy(out=w_kk0, in_=p_gu[bk][:]), [("gu", bk)], ["kk0"])
            dma("sp", kvd[h, :, 0:512], w_kk0, ["kk0"], [("kvd", h)], "kvd_k")
            bv = gu_bank()
            proj(slot, 384, 1536, bv)
            S.op("act", lambda e: e.copy(out=w_kk1, in_=p_gu[bv][:]), [("gu", bv)], ["kk1"])
            for j in range(4):
                S.op("pe", lambda e, j=j: e.transpose(out=p_tr[:, j, :], in_=w_kk1[:, j * 128:(j + 1) * 128], identity=ident[:]), ["kk1", "ident"], ["ptr"])
            S.op("dve", lambda e: e.tensor_copy(out=w_vi.rearrange("p (a b) -> p a b", a=4), in_=p_tr[:, 0:4, :]), ["ptr"], ["vi"])
            dma("sp", kvd[h, :, 512:1024], w_vi, ["vi"], [("kvd", h)], "kvd_v")

        def hgrn_phase(main):
            slots = {}

            def load_w(h):
                slot = wslot()
                slots[h] = slot
                if main:
                    for gi in range(4):
                        wload_cols(slot, gi * 128, w_in, 3072 + gi * 1024 + h * 128, 128)
                else:
                    wload_cols(slot, 0, w_in, 3072 + 1024 + h * 128, 128)
                    wload_cols(slot, 128, w_in, 3072 + 2048 + h * 128, 128)
                    wload_cols(slot, 256, w_in, 1024 + h * 128, 128)
                    wload_cols(slot, 384, w_in, 2048 + h * 128, 128)
            units = [(h, tb) for h in range(8) for tb in range(4)]
            load_w(0)
            for n, (h, tb) in enumerate(units):
                if tb == 0 and h + 1 < 8:
                    load_w(h + 1)
                hgrn_front(h, tb, slots[h], main, n % 2)
                if n > 0:
                    ph, ptb = units[n - 1]
                    hgrn_back(ph, ptb, main, (n - 1) % 2)
                hgrn_front_b(h, tb, main, n % 2)
                if n > 0:
                    hgrn_back_b(ph, ptb, main, (n - 1) % 2)
                if (not main) and tb == 3:
                    hist_kv(h, slots[h])
            ph, ptb = units[-1]
            hgrn_back(ph, ptb, main, (len(units) - 1) % 2)
            hgrn_back_b(ph, ptb, main, (len(units) - 1) % 2)

        hgrn_phase(main=False)
        if debug and "S" in debug:
            dma("sp", dbg["S"], Sfin, [("S", h) for h in range(8)], [], "dbgS")
            finals.append(len(S.ops) - 1)
        S.barrier()

        build_tab()
        for ti in range(16):
            load_norm_x(16 + ti, ti * 128)
        S.barrier()
        S.op("dve", lambda e: e.memset(a_v, 1.0), [], ["a_v_init"])

        def attention_head(h):
            slot = wslot()
            wload_cols(slot, 0, w_in, h * 128, 128)
            wload_cols(slot, 128, w_in, 1024 + h * 128, 128)
            wload_cols(slot, 256, w_in, 2048 + h * 128, 128)
            dma("sp", a_k[:, 0:512], kvd[h, :, 0:512], [("kvd", h)], [("a_k", 0)], "akh")
            dma("sp", a_v[:, 0:4, 0:128], kvd[h, :, 512:1024].rearrange("p (a b) -> p a b", a=4), [("kvd", h), "a_v_init"], [("a_v", 0)], "avh")
            for tb in range(4):
                b = gu_bank()
                proj(slot, 0, tb * 512, b)
                S.op("act", lambda e, b=b, tb=tb: e.copy(out=a_q[:, tb * 512:(tb + 1) * 512], in_=p_gu[b][:]), [("gu", b)], [("a_q", tb)])
                b = gu_bank()
                proj(slot, 128, tb * 512, b)
                S.op("act", lambda e, b=b, tb=tb: e.copy(out=a_k[:, 512 + tb * 512:512 + (tb + 1) * 512], in_=p_gu[b][:]), [("gu", b)], [("a_k", 1 + tb)])
                b = gu_bank()
                proj(slot, 256, tb * 512, b)
                vs = tb % 2
                S.op("act", lambda e, b=b, vs=vs: e.copy(out=a_vT[:, vs, :], in_=p_gu[b][:]), [("gu", b)], [("a_vT", vs)])
                for j in range(4):
                    S.op("pe", lambda e, j=j, vs=vs: e.transpose(out=p_tr[:, j, :], in_=a_vT[:, vs, j * 128:(j + 1) * 128], identity=ident[:]),
                         [("a_vT", vs), "ident"], ["ptr"])
                S.op("dve", lambda e, tb=tb: e.tensor_copy(out=a_v[:, 4 + tb * 4:8 + tb * 4, 0:128], in_=p_tr[:, 0:4, :]), ["ptr", "a_v_init"], [("a_v", 1 + tb)])
            def sbanks(qt):
                if qt % 2 == 0:
                    return (p_acc[0], ("acc", 0)), (p_acc[1], ("acc", 1)), (p_acc[2], ("acc", 2))
                return (p_gu[0], ("gu", 0)), (p_gu[1], ("gu", 1)), (p_acc[3], ("acc", 3))

            def att_S(qt):
                (bA, kA), (bB, kB), _ = sbanks(qt)
                kreads = sorted(set(("a_k", (qt + kt) // 4) for kt in range(5)))
                for kt in range(5):
                    bank, bkey = (bA, kA) if kt < 4 else (bB, kB)
                    col = (kt % 4) * 128
                    S.op("pe", lambda e, kt=kt, bank=bank, col=col: e.matmul(
                        bank[:, col:col + 128], lhsT=a_k[:, (qt + kt) * 128:(qt + kt + 1) * 128], rhs=a_q[:, qt * 128:(qt + 1) * 128],
                        start=True, stop=False), kreads + [("a_q", qt // 4)], [bkey])
                    S.op("pe", lambda e, kt=kt, bank=bank, col=col: e.matmul(
                        bank[:, col:col + 128], lhsT=ident[:], rhs=tab[:, h, kt * 128:(kt + 1) * 128],
                        start=False, stop=True), [("tab", h), "ident"], [bkey])

            def att_rest(qt):
                pp = qt % 2
                (bA, kA), (bB, kB), (bP, kP) = sbanks(qt)
                vreads = sorted(set(("a_v", (qt + kt) // 4) for kt in range(5)))
                nh = max(0, min(4, 4 - qt))
                if nh > 0:
                    S.op("act", lambda e: e.activation(out=a_p[:, pp, 0:nh * 128], in_=bA[:, 0:nh * 128], func=AF.Exp, scale=SCALE, bias=hneg[:, 0:1]),
                         [kA, "hneg"], [("a_p", pp)])
                if nh < 4:
                    S.op("act", lambda e: e.activation(out=a_p[:, pp, nh * 128:512], in_=bA[:, nh * 128:512], func=AF.Exp, scale=SCALE),
                         [kA], [("a_p", pp)])
                S.op("act", lambda e: e.activation(out=a_p[:, pp, 512:640], in_=bB[:, 0:128], func=AF.Exp, scale=SCALE),
                     [kB], [("a_p", pp)])
                for kt in range(5):
                    S.op("pe", lambda e, kt=kt: e.matmul(bP[:, 0:129], lhsT=a_p[:, pp, kt * 128:(kt + 1) * 128], rhs=a_v[:, qt + kt, 0:129],
                                                         start=(kt == 0), stop=(kt == 4)), [("a_p", pp)] + vreads, [kP])
                S.op("dve", lambda e: e.reciprocal(out=a_rd[:, pp:pp + 1], in_=bP[:, 128:129]), [kP], [("a_rd", pp)])
                S.op("dve", lambda e: e.tensor_scalar(out=a_ab[:, pp, :], in0=bP[:, 0:128], scalar1=a_rd[:, pp:pp + 1], scalar2=None, op0=ALU.mult),
                     [kP, ("a_rd", pp)], [("a_ab", pp)])
                S.op("act", lambda e: e.activation(out=a_jk, in_=a_ab[:, pp, :], func=AF.Square, accum_out=ssa[:, qt, h:h + 1]),
                     [("a_ab", pp)], ["a_jk", ("ssa", qt, h)])
                S.op("pe", lambda e: e.transpose(out=p_tr[:, qt % 4, :], in_=a_ab[:, pp, :], identity=ident[:]), [("a_ab", pp), "ident"], ["ptr"])
                if qt % 4 == 3:
                    g = qt // 4
                    ms = (h * 4 + g) % 2
                    S.op("dve", lambda e: e.tensor_scalar(out=mst_a[:, ms, :].rearrange("p (a b) -> p a b", a=4), in0=p_tr[:, 0:4, :], scalar1=gatc[:, h:h + 1], scalar2=None, op0=ALU.mult),
                         ["ptr", "gatc"], [("mst", ms)])
                    dma("sp", md[h, :, g * 512:(g + 1) * 512], mst_a[:, ms, :], [("mst", ms)], [("md", h, g)], ("mst", ms))

            att_S(0)
            for qt_ in range(16):
                if qt_ + 1 < 16:
                    att_S(qt_ + 1)
                att_rest(qt_)


        if stop_after != "H":
            for h in range(int(os.environ.get('ATT_HEADS', '8'))):
                attention_head(h)
            S.barrier()
        if stop_after not in ("H", "att"):
            S.op("dve", lambda e: e.memset(w_e0, 0.0), [], ["e0"])
            S.op("dve", lambda e: e.memset(w_e1, 0.0), [], ["e1"])
            hgrn_phase(main=True)
            S.barrier()

        if stop_after is None:
            S.op("dve", lambda e: e.tensor_reduce(out=rsa[:], in_=ssa[:], axis=AX.X, op=ALU.add), [], ["rsa"])
            S.op("act", lambda e: e.activation(out=rsa[:], in_=rsa[:], func=AF.Sqrt, scale=1.0 / 1024, bias=EPS), ["rsa"], ["rsa"])
            S.op("dve", lambda e: e.reciprocal(out=rsa[:], in_=rsa[:]), ["rsa"], ["rsa"])
            for blk in range(NT // TB):
                t0 = blk * TB
                dma("sp", mb, md[:, :, t0:t0 + TB].rearrange("k p t -> p k t"), [], ["mb"], "mb")
                for tt in range(4):
                    dma("sp", x1[:, tt, :], xw[NHIST + t0 + tt * 128:NHIST + t0 + (tt + 1) * 128, :], [], [("x1", tt)], ("x1", tt))
                for dblk in range(4):
                    slot = wslot()
                    wload_cols(slot, 0, w_out, dblk * 512, 512)
                    wv = wview(slot)
                    for tt in range(4):
                        qt = blk * 4 + tt
                        bA, bR = (0, 1) if tt % 2 == 0 else (2, 3)
                        for k in range(16):
                            bank = bA if k < 8 else bR
                            S.op("pe", lambda e, k=k, tt=tt, bank=bank, wv=wv: e.matmul(
                                p_acc[bank][:], lhsT=mb[:, k, tt * 128:(tt + 1) * 128], rhs=wv[:, k, :], start=(k % 8 == 0), stop=(k % 8 == 7)),
                                ["mb"] + wkeys(slot), [("acc", bank)])
                        S.op("dve", lambda e, tt=tt, qt=qt, bA=bA, dblk=dblk: e.scalar_tensor_tensor(
                            out=x1[:, tt, dblk * 512:(dblk + 1) * 512], in0=p_acc[bA][:], scalar=rsa[:, qt:qt + 1], in1=x1[:, tt, dblk * 512:(dblk + 1) * 512],
                            op0=ALU.mult, op1=ALU.add), [("acc", bA), "rsa", ("x1", tt)], [("x1", tt)])
                        S.op("dve", lambda e, tt=tt, bR=bR, dblk=dblk: e.tensor_tensor(
                            out=x1[:, tt, dblk * 512:(dblk + 1) * 512], in0=x1[:, tt, dblk * 512:(dblk + 1) * 512], in1=p_acc[bR][:], op=ALU.add),
                            [("acc", bR), ("x1", tt)], [("x1", tt)])
                if debug and "x1" in debug:
                    for tt in range(4):
                        dma("sp", dbg["x1"][t0 + tt * 128:t0 + (tt + 1) * 128, :], x1[:, tt, :], [("x1", tt)], [], ("dbgx1", tt))
                        finals.append(len(S.ops) - 1)
                for tt in range(4):
                    norm_to_T(x1[:, tt, :], [("x1", tt)], h2T, "h2T", tt * 128, g2c, "g2c")
                h2keys = tkeys("h2T", 0, 512)
                for fp in range(NFT // 2):
                    slot = wslot()
                    wload_cols(slot, 0, w_gate, fp * 256, 256)
                    wload_cols(slot, 256, w_up, fp * 256, 256)
                    wv = wview(slot)
                    for fi in range(2):
                        ft = fp * 2 + fi
                        bg = gu_bank()
                        for k in range(16):
                            S.op("pe", lambda e, k=k, fi=fi, bg=bg, wv=wv: e.matmul(p_gu[bg][:], lhsT=wv[:, k, fi * 128:(fi + 1) * 128], rhs=h2T[:, k, :],
                                                                                  start=(k == 0), stop=(k == 15)), [("w", slot, fi)] + h2keys, [("gu", bg)])
                        bu = gu_bank()
                        for k in range(16):
                            S.op("pe", lambda e, k=k, fi=fi, bu=bu, wv=wv: e.matmul(p_gu[bu][:], lhsT=wv[:, k, 256 + fi * 128:256 + (fi + 1) * 128], rhs=h2T[:, k, :],
                                                                                  start=(k == 0), stop=(k == 15)), [("w", slot, 2 + fi)] + h2keys, [("gu", bu)])
                        sl = ft % 2
                        S.op("act", lambda e, bg=bg, sl=sl: e.activation(out=silb[:, sl, :], in_=p_gu[bg][:], func=AF.Silu), [("gu", bg)], [("sil", sl)])
                        S.op("dve", lambda e, bu=bu, sl=sl, ft=ft: e.tensor_tensor(out=ffT[:, ft, :], in0=p_gu[bu][:], in1=silb[:, sl, :], op=ALU.mult),
                             [("gu", bu), ("sil", sl)], [("ffT", ft)])
                for dblk in range(4):
                    for fg, (f0, nf) in enumerate(((0, 16), (16, 16), (32, 12))):
                        slot = wslot()
                        wv = wview(slot)
                        dma("pool", wv[:, 0:nf, :], w_down[f0 * 128:(f0 + nf) * 128, dblk * 512:(dblk + 1) * 512].rearrange("(k p) c -> p k c", p=128),
                            [], wkeys(slot), ("w", slot, 0))
                        for fl in range(nf):
                            fc = f0 + fl
                            for tt in range(4):
                                S.op("pe", lambda e, fl=fl, fc=fc, tt=tt, wv=wv: e.matmul(p_acc[tt][:], lhsT=ffT[:, fc, tt * 128:(tt + 1) * 128], rhs=wv[:, fl, :],
                                                                                        start=(fc == 0), stop=(fc == NFT - 1)), wkeys(slot) + [("ffT", fc)], [("acc", tt)])
                    for tt in range(4):
                        S.op("dve", lambda e, tt=tt, dblk=dblk: e.tensor_tensor(
                            out=x1[:, tt, dblk * 512:(dblk + 1) * 512], in0=x1[:, tt, dblk * 512:(dblk + 1) * 512], in1=p_acc[tt][:], op=ALU.add),
                            [("acc", tt), ("x1", tt)], [("x1", tt)])
                for tt in range(4):
                    i = 32 + (blk * 4 + tt) % 32
                    S.op("act", lambda e, tt=tt, i=i: e.activation(out=junk[:], in_=x1[:, tt, :], func=AF.Square, accum_out=ss[:, i:i + 1]),
                         [("x1", tt)], ["junk", ("ss", i)])
                    S.op("act", lambda e, i=i: e.activation(out=sd[:, i:i + 1], in_=ss[:, i:i + 1], func=AF.Sqrt, scale=1.0 / D, bias=EPS), [("ss", i)], [("sd", i)])
                    S.op("dve", lambda e, i=i: e.reciprocal(out=rstd[:, i:i + 1], in_=sd[:, i:i + 1]), [("sd", i)], [("rstd", i)])
                    S.op("dve", lambda e, tt=tt, i=i: e.scalar_tensor_tensor(out=yo, in0=x1[:, tt, :], scalar=rstd[:, i:i + 1], in1=gfb[:],
                                                                           op0=ALU.mult, op1=ALU.mult), [("x1", tt), ("rstd", i), "gfb"], ["yo"])
                    dma("sp", y[t0 + tt * 128:t0 + (tt + 1) * 128, :], yo, ["yo"], [], "yo")
                    finals.append(len(S.ops) - 1)

        if debug and "md" in debug:
            S.barrier()
            for m in range(16):
                for g in range(4):
                    dma("sp", hb[:, 0:512], md[m, :, g * 512:(g + 1) * 512], [], ["hb"], "hbd")
                    S.op("dve", lambda e: e.tensor_copy(out=gfb[:, 0:512], in_=hb[:, 0:512]), ["hb"], ["gfb"])
                    dma("sp", dbg["md"][m, :, g * 512:(g + 1) * 512], gfb[:, 0:512], ["gfb"], [], "gfbd")
                    finals.append(len(S.ops) - 1)
        stats = S.emit(final_waits=finals)
    return nc, stats


def _consts():
    ident = np.eye(128, dtype=np.float32)
    kk = np.arange(128)[:, None]
    amask = np.zeros((128, 640), np.float32)
    iq = np.arange(128)[None, :]
    amask[:, 0:128] = np.where((iq >= 64) & (kk < 64), -1e5, 0.0)
    amask[:, 512:640] = np.where((iq < 64) & (kk >= 64), -1e5, 0.0)
    s = np.arange(128)[:, None]
    t = np.arange(128)[None, :]
    tri = ((s // 64 == t // 64) & (s <= t)).astype(np.float32)
    cmask = np.tile(tri, (1, 4))
    smask = np.ones((128, 512), np.float32)
    smask[:, 0::64] = 0.0
    return ident, amask, cmask, smask


def _bias_index():
    kt = np.arange(5)[None, :, None]
    kk = np.arange(128)[:, None, None]
    iq = np.arange(128)[None, None, :]
    dist = 512 - 128 * kt + iq - kk
    return (np.clip(dist, -128, 128) + 128).reshape(128, 640)


_PROG = {}


def kernel(x, norm1_gain, w_in, rel_bias, lower_bounds, grn_norm_gain, attn_out_gain, w_out, norm2_gain,
           w_gate, w_up, w_down, final_gain):
    x = np.asarray(x, np.float32)
    if "nc" not in _PROG:
        _PROG["nc"] = build_program()[0]
    nc = _PROG["nc"]
    ident, amask, cmask, smask = _consts()
    idx = _bias_index()
    btab = np.ascontiguousarray(np.asarray(rel_bias, np.float32)[0][:, idx])
    shared = {
        "w_in": np.ascontiguousarray(np.asarray(w_in, np.float32)[0]),
        "w_out": np.ascontiguousarray(np.asarray(w_out, np.float32)[0]),
        "w_gate": np.ascontiguousarray(np.asarray(w_gate, np.float32)[0]),
        "w_up": np.ascontiguousarray(np.asarray(w_up, np.float32)[0]),
        "w_down": np.ascontiguousarray(np.asarray(w_down, np.float32)[0]),
        "g1": np.asarray(norm1_gain, np.float32).reshape(1, D),
        "g2": np.asarray(norm2_gain, np.float32).reshape(1, D),
        "gf": np.asarray(final_gain, np.float32).reshape(1, D),
        "gat": np.asarray(attn_out_gain, np.float32).reshape(1, 1024),
        "ggr": np.asarray(grn_norm_gain, np.float32).reshape(1, 128),
        "lbnd": np.ascontiguousarray(np.asarray(lower_bounds, np.float32)),
        "btab": btab, "c_ident": ident, "c_amask": amask, "c_cmask": cmask, "c_smask": smask,
    }
    in_maps = []
    for c in range(8):
        b, half = c // 2, c % 2
        xwc = np.zeros((NHIST + NT, D), np.float32)
        if half == 1:
            xwc[:] = x[b]
        else:
            xwc[NHIST:] = x[b, :NT]
        m = dict(shared)
        m["xw"] = xwc
        m["c_hneg"] = np.full((128, 1), 0.0 if half == 1 else -30000.0, np.float32)
        in_maps.append(m)
    res = run_bass_kernel_spmd(nc, in_maps, core_ids=list(range(8)))
    out = np.empty((4, 4096, D), np.float32)
    for c in range(8):
        b, half = c // 2, c % 2
        out[b, half * NT:(half + 1) * NT] = res.results[c]["y"]
    return out
```

```python
import contextlib
import os
import numpy as np
import ml_dtypes
import concourse.bass as bass
import concourse.mybir as mybir
from concourse.bass_utils import run_bass_kernel_spmd

F32 = mybir.dt.float32
BF16 = mybir.dt.bfloat16
AF = mybir.ActivationFunctionType
ALU = mybir.AluOpType
AX = mybir.AxisListType

ENGS = ("pe", "act", "dve", "pool", "sp")

D = 2048
NT = 2048
NHIST = 2048
KD = 16
DFF = 5632
NFT = 44
EPS = 1e-6
INCOLS = 7168
TB = 512
SCALE = 128 ** -0.5
SQ128 = 128 ** 0.5


class Sched:
    def __init__(self, nc, same_engine_sync=True):
        self.nc = nc
        self.ops = []
        self.last_writer = {}
        self.readers = {}
        self.same_engine_sync = same_engine_sync
        self.dma_keys = {}

    def op(self, eng, fn, reads=(), writes=(), dma_key=None):
        idx = len(self.ops)
        deps = set()
        for k in reads:
            w = self.last_writer.get(k)
            if w is not None:
                deps.add(w)
        for k in writes:
            w = self.last_writer.get(k)
            if w is not None:
                deps.add(w)
            for r in self.readers.get(k, ()):
                deps.add(r)
        if dma_key is not None:
            prev = self.dma_keys.get(dma_key)
            if prev is not None:
                deps.add(prev)
            self.dma_keys[dma_key] = idx
        deps.discard(idx)
        self.ops.append(dict(eng=eng, fn=fn, deps=deps, dma_key=dma_key, needed=False))
        for k in writes:
            self.last_writer[k] = idx
            self.readers[k] = []
        for k in reads:
            if k not in writes:
                lst = self.readers.setdefault(k, [])
                if dma_key is None:
                    lst[:] = [r for r in lst if not (self.ops[r]["eng"] == eng and self.ops[r]["dma_key"] is None)]
                lst.append(idx)
        return idx

    def barrier(self):
        last = {}
        for i, o in enumerate(self.ops):
            if o["fn"] is None:
                continue
            last[("e", o["eng"])] = i
            if o["dma_key"] is not None:
                last[("d", o["dma_key"])] = i
        deps = set(last.values())
        for e in ENGS:
            self.ops.append(dict(eng=e, fn=None, deps=set(deps), dma_key=None, needed=False))

    def emit(self, final_waits=()):
        nc = self.nc
        ops = self.ops
        for i, o in enumerate(ops):
            nd = set()
            for d in o["deps"]:
                p = ops[d]
                if p["dma_key"] is None and p["eng"] == o["eng"] and not self.same_engine_sync:
                    continue
                if p["dma_key"] is None and p["eng"] == "pe" and o["eng"] == "pe":
                    continue
                nd.add(d)
            o["deps"] = nd
            for d in nd:
                ops[d]["needed"] = True
        for d in final_waits:
            ops[d]["needed"] = True
        tick = {e: 0 for e in ENGS}
        dma_cnt = {}
        for o in ops:
            if o["dma_key"] is not None:
                dma_cnt[o["dma_key"]] = dma_cnt.get(o["dma_key"], 0) + 1
                o["sem"] = ("dma", o["dma_key"])
                o["tick"] = 16 * dma_cnt[o["dma_key"]]
            elif o["needed"]:
                tick[o["eng"]] += 1
                o["sem"] = ("eng", o["eng"])
                o["tick"] = tick[o["eng"]]
        sem_names = [("eng", e) for e in ENGS] + [("dma", k) for k in dma_cnt]
        with contextlib.ExitStack() as st:
            sems = {}
            for i, sn in enumerate(sem_names):
                sems[sn] = st.enter_context(nc.semaphore("s%d" % i))
            block = st.enter_context(nc.Block())
            per_eng = {e: [] for e in ENGS}
            for i, o in enumerate(ops):
                per_eng[o["eng"]].append(i)

            def body(ename, engine):
                waited = {}
                for i in per_eng[ename]:
                    o = ops[i]
                    need = {}
                    for d in o["deps"]:
                        p = ops[d]
                        s = p["sem"]
                        need[s] = max(need.get(s, 0), p["tick"])
                    for s, v in need.items():
                        if waited.get(s, 0) >= v:
                            continue
                        engine.wait_ge(sems[s], v)
                        waited[s] = v
                    if o["fn"] is None:
                        continue
                    ins = o["fn"](engine)
                    if o["dma_key"] is not None:
                        ins.then_inc(sems[o["sem"]], 16)
                    elif o["needed"]:
                        ins.then_inc(sems[o["sem"]], 1)
                if ename == "sp":
                    need = {}
                    for d in final_waits:
                        p = ops[d]
                        need[p["sem"]] = max(need.get(p["sem"], 0), p["tick"])
                    for s, v in need.items():
                        engine.wait_ge(sems[s], v)

            @block.tensor
            def _(e):
                body("pe", e)

            @block.scalar
            def _(e):
                body("act", e)

            @block.vector
            def _(e):
                body("dve", e)

            @block.gpsimd
            def _(e):
                body("pool", e)

            @block.sync
            def _(e):
                body("sp", e)
        return {e: len(per_eng[e]) for e in ENGS}, tick, len(sem_names)


def build_program(debug=None, stop_after=None):
    nc = bass.Bass("TRN2", target_bir_lowering=False)
    dt_in = lambda name, shape: nc.dram_tensor(name, shape, F32, kind="ExternalInput").ap()
    xw = dt_in("xw", [NHIST + NT, D])
    w_in = dt_in("w_in", [D, INCOLS])
    w_out = dt_in("w_out", [D, D])
    w_gate = dt_in("w_gate", [D, DFF])
    w_up = dt_in("w_up", [D, DFF])
    w_down = dt_in("w_down", [DFF, D])
    g1_d = dt_in("g1", [1, D])
    g2_d = dt_in("g2", [1, D])
    gf_d = dt_in("gf", [1, D])
    gat_d = dt_in("gat", [1, 1024])
    ggr_d = dt_in("ggr", [1, 128])
    lbnd_d = dt_in("lbnd", [2, 1024])
    btab_d = dt_in("btab", [8, 128, 640])
    cid_d = dt_in("c_ident", [128, 128])
    cam_d = dt_in("c_amask", [128, 640])
    ccm_d = dt_in("c_cmask", [128, 512])
    csm_d = dt_in("c_smask", [128, 512])
    chn_d = dt_in("c_hneg", [128, 1])
    y = nc.dram_tensor("y", [NT, D], F32, kind="ExternalOutput").ap()
    md = nc.dram_tensor("md", [16, 128, NT], BF16, kind="Internal").ap()
    kvd = nc.dram_tensor("kvd", [8, 128, 1024], BF16, kind="Internal").ap()
    dbg = {}
    if debug:
        for name, shape in debug.items():
            dbg[name] = nc.dram_tensor("dbg_" + name, shape, F32, kind="ExternalOutput").ap()

    S = Sched(nc)
    finals = []
    with contextlib.ExitStack() as st:
        def sb(name, shape, dt=F32):
            return st.enter_context(nc.sbuf_tensor(name, shape, dt))

        def ps(name, shape, dt=F32):
            return st.enter_context(nc.psum_tensor(name, shape, dt))

        R2 = sb("R2", [128, 16, 2048], BF16)
        R1 = sb("R1", [128, 28672], BF16)
        NSLOT = 3
        wpool = sb("wpool", [128, NSLOT, 8192], BF16)
        ident = sb("ident", [128, 128], BF16)
        identF = sb("identF", [128, 128], F32)
        onesF = sb("onesF", [128, 128], F32)
        amask = sb("amask", [128, 640], F32)
        cmask = sb("cmask", [128, 512], F32)
        smask = sb("smask", [128, 512], F32)
        hneg = sb("hneg", [128, 1], F32)
        g1c = sb("g1c", [128, 16], F32)
        g2c = sb("g2c", [128, 16], F32)
        gfb = sb("gfb", [128, 2048], F32)
        gatc = sb("gatc", [128, 8], F32)
        ggrc = sb("ggrc", [128, 1], F32)
        lbc = sb("lbc", [128, 8], F32)
        lb1 = sb("lb1", [128, 8], F32)
        omlc = sb("omlc", [128, 8], F32)
        nomlc = sb("nomlc", [128, 8], F32)
        junk = sb("junk", [128, 2048], BF16)
        hb = sb("hb", [128, 2048], BF16)
        silb = sb("silb", [128, 2, 512], F32)
        ss = sb("ss", [128, 64], F32)
        sd = sb("sd", [128, 64], F32)
        rstd = sb("rstd", [128, 64], F32)
        ssa = sb("ssa", [128, 16, 8], F32)
        rsa = sb("rsa", [128, 16], F32)

        def carve(off, shape, dt):
            n = 1
            for s_ in shape[1:]:
                n *= s_
            esz = 2 if dt == BF16 else 4
            a = R1[:, off // 2: off // 2 + n * esz // 2]
            if dt == F32:
                a = a.bitcast(F32)
            if len(shape) == 3:
                a = a.rearrange("p (a b) -> p a b", a=shape[1])
            elif len(shape) == 4:
                a = a.rearrange("p (a b c) -> p a b c", a=shape[1], b=shape[2])
            return a

        xin = carve(0, [128, 2048], F32)
        Sfin = carve(8192, [128, 8, 128], F32)
        Sall = carve(12288, [128, 8, 128], F32)
        Sb = carve(16384, [128, 8, 128], BF16)
        tab = carve(18432, [128, 8, 640], BF16)
        WK = 28672
        KB2 = 2048
        w_sgn = carve(WK + 0 * KB2, [128, 512], F32)
        w_f = carve(WK + 1 * KB2, [128, 512], F32)
        w_b = carve(WK + 2 * KB2, [128, 512], F32)
        w_e0 = carve(WK + 3 * KB2, [128, 512], F32)
        w_e1 = carve(WK + 4 * KB2, [128, 512], F32)
        w_e = carve(WK + 5 * KB2, [128, 512], F32)
        w_qs = carve(WK + 6 * KB2, [128, 512], F32)
        w_gs0 = carve(WK + 7 * KB2, [128, 512], F32)
        o = WK + 8 * KB2
        w_qd0 = carve(o, [128, 512], BF16); o += 1024
        w_kd0 = carve(o, [128, 512], BF16); o += 1024
        w_kk0 = carve(o, [128, 512], BF16); o += 1024
        w_kk1 = carve(o, [128, 512], BF16); o += 1024
        w_vi = carve(o, [128, 512], BF16); o += 1024
        w_tm0 = carve(o, [128, 12, 128], BF16); o += 3072
        w_am = carve(o, [128, 512], BF16); o += 1024
        mst = carve(o, [128, 2, 512], BF16); o += 2048
        w_eb8 = carve(o, [128, 2, 8], F32); o += 64
        assert o <= 57344, o
        o = 18432
        w_gs1 = carve(o, [128, 512], F32); o += 2048
        w_X = carve(o, [128, 512], F32); o += 2048
        w_qd1 = carve(o, [128, 512], BF16); o += 1024
        w_kd1 = carve(o, [128, 512], BF16); o += 1024
        w_tm1 = carve(o, [128, 12, 128], BF16); o += 3072
        assert o <= 28672, o
        w_gsP = (w_gs0, w_gs1)
        w_qdP = (w_qd0, w_qd1)
        w_kdP = (w_kd0, w_kd1)
        w_tmP = (w_tm0, w_tm1)
        o = WK
        a_q = carve(o, [128, 2048], BF16); o += 4096
        a_k = carve(o, [128, 2560], BF16); o += 5120
        a_vT = carve(o, [128, 2, 512], BF16); o += 2048
        a_v = carve(o, [128, 20, 132], BF16); o += 5280
        a_p = carve(o, [128, 2, 640], BF16); o += 2560
        a_ab = carve(o, [128, 2, 128], BF16); o += 512
        a_jk = carve(o, [128, 128], BF16); o += 256
        a_rd = carve(o, [128, 2], F32); o += 8
        o = (o + 63) // 64 * 64
        mst_a = carve(o, [128, 2, 512], BF16); o += 2048
        assert o <= 57344, o
        mb = carve(0, [128, 16, 512], BF16)
        x1 = carve(16384, [128, 4, 2048], F32)
        yo = carve(49152, [128, 2048], F32)
        r2f = R2[:].rearrange("p k t -> p (k t)")
        ffT = r2f[:, 0:22528].rearrange("p (f t) -> p f t", f=NFT)
        h2T = r2f[:, 22528:30720].rearrange("p (k t) -> p k t", k=16)

        p_acc = [ps("pacc%d" % i, [128, 512]) for i in range(4)]
        p_gu = [ps("pgu%d" % i, [128, 512]) for i in range(3)]
        p_tr = ps("ptr", [128, 8, 128], BF16)

        def dma(eng, out, in_, reads, writes, key, slow=False):
            if slow:
                return S.op(eng, lambda e: e.dma_start(out=out, in_=in_, allow_slow_non_contiguous=True), reads=reads, writes=writes, dma_key=key)
            return S.op(eng, lambda e: e.dma_start(out=out, in_=in_), reads=reads, writes=writes, dma_key=key)

        dma("sp", identF[:], cid_d[:, :], [], ["identF"], "c0")
        dma("sp", amask[:], cam_d[:, :], [], ["amask"], "c1")
        dma("sp", cmask[:], ccm_d[:, :], [], ["cmask"], "c2")
        dma("sp", smask[:], csm_d[:, :], [], ["smask"], "c3")
        dma("sp", hneg[:], chn_d[:, :], [], ["hneg"], "c4")
        dma("sp", gfb[:], gf_d.broadcast_to([128, D]), [], ["gfb"], "c5")
        cvec = lambda d_: d_.rearrange("o (k p) -> p (o k)", p=128)
        dma("sp", g1c[:], cvec(g1_d), [], ["g1c"], "c6", slow=True)
        dma("sp", g2c[:], cvec(g2_d), [], ["g2c"], "c7", slow=True)
        dma("sp", gatc[:], cvec(gat_d), [], ["gatc"], "c8", slow=True)
        dma("sp", ggrc[:], cvec(ggr_d), [], ["ggrc"], "c9", slow=True)
        dma("sp", lbc[:], cvec(lbnd_d[0:1, :]), [], ["lbc"], "c10", slow=True)
        dma("sp", lb1[:], cvec(lbnd_d[1:2, :]), [], ["lb1"], "c11", slow=True)
        S.op("dve", lambda e: e.tensor_copy(out=ident[:], in_=identF[:]), ["identF"], ["ident"])
        S.op("dve", lambda e: e.memset(onesF[:], 1.0), [], ["onesF"])
        S.op("dve", lambda e: e.memset(Sfin, 0.0), [], [("S", h) for h in range(8)])
        S.op("dve", lambda e: e.tensor_tensor(out=lbc[:], in0=lbc[:], in1=lb1[:], op=ALU.subtract), ["lbc", "lb1"], ["lbc"])
        S.op("act", lambda e: e.activation(out=lbc[:], in_=lbc[:], func=AF.Sigmoid), ["lbc"], ["lbc"])
        S.op("dve", lambda e: e.tensor_scalar(out=omlc[:], in0=lbc[:], scalar1=-1.0, scalar2=1.0, op0=ALU.mult, op1=ALU.add), ["lbc"], ["omlc"])
        S.op("dve", lambda e: e.tensor_scalar(out=nomlc[:], in0=lbc[:], scalar1=1.0, scalar2=-1.0, op0=ALU.mult, op1=ALU.add), ["lbc"], ["nomlc"])
        def build_tab():
            for h in range(8):
                dma("sp", xin[:, 0:640], btab_d[h], [], ["xin"], "xin")
                S.op("dve", lambda e, h=h: e.scalar_tensor_tensor(out=tab[:, h, :], in0=xin[:, 0:640], scalar=SQ128, in1=amask[:],
                                                                  op0=ALU.mult, op1=ALU.add), ["xin", "amask"], [("tab", h)])

        wstate = {"n": 0}

        def wslot():
            s = wstate["n"] % NSLOT
            wstate["n"] += 1
            return s

        def wview(slot):
            return wpool[:, slot, :].rearrange("p (k c) -> p k c", k=16)

        def wkeys(slot, q0=0, q1=4):
            return [("w", slot, q) for q in range(q0, q1)]

        def wload_cols(slot, dst_off, src, col0, ncols):
            q0, q1 = dst_off // 128, (dst_off + ncols) // 128
            dma("pool", wview(slot)[:, :, dst_off:dst_off + ncols],
                src[:, col0:col0 + ncols].rearrange("(k p) c -> p k c", p=128),
                [], wkeys(slot, q0, q1), ("w", slot, q0))

        ncount = {"n": 0}

        def norm_to_T(src_ap, src_reads, dstT, dname, col0, gain_c, gkey, hbuf=None, hkey="hb"):
            hbv = hb[:] if hbuf is None else hbuf
            i = ncount["n"] % 32
            ncount["n"] += 1
            S.op("act", lambda e: e.activation(out=junk[:], in_=src_ap, func=AF.Square, accum_out=ss[:, i:i + 1]),
                 src_reads, ["junk", ("ss", i)])
            S.op("act", lambda e: e.activation(out=sd[:, i:i + 1], in_=ss[:, i:i + 1], func=AF.Sqrt, scale=1.0 / D, bias=EPS),
                 [("ss", i)], [("sd", i)])
            S.op("dve", lambda e: e.reciprocal(out=rstd[:, i:i + 1], in_=sd[:, i:i + 1]), [("sd", i)], [("rstd", i)])
            S.op("dve", lambda e: e.tensor_scalar(out=hbv, in0=src_ap, scalar1=rstd[:, i:i + 1], scalar2=None, op0=ALU.mult),
                 src_reads + [("rstd", i)], [hkey])
            for half in range(2):
                for j in range(8):
                    k = half * 8 + j
                    S.op("pe", lambda e, k=k, j=j: e.transpose(out=p_tr[:, j, :], in_=hbv[:, k * 128:(k + 1) * 128], identity=ident[:]),
                         [hkey, "ident"], ["ptr"])
                S.op("dve", lambda e, half=half: e.tensor_tensor(
                    out=dstT[:, half * 8:half * 8 + 8, col0:col0 + 128], in0=p_tr[:],
                    in1=gain_c[:, half * 8:half * 8 + 8].unsqueeze(2).broadcast_to([128, 8, 128]), op=ALU.mult),
                    ["ptr", gkey], [("T", dname, col0, half)])

        def tkeys(dname, tok0, ntok):
            return [("T", dname, c, half) for c in range(tok0, tok0 + ntok, 128) for half in range(2)]

        xin2 = carve(WK, [128, 2048], F32)
        hb2 = carve(WK + 8192, [128, 2048], BF16)

        def load_norm_x(ti, col0):
            if ti % 2 == 0:
                dma("sp", xin, xw[ti * 128:(ti + 1) * 128, :], [], ["xin"], "xin")
                norm_to_T(xin, ["xin"], R2, "R2", col0, g1c, "g1c")
            else:
                dma("sp", xin2, xw[ti * 128:(ti + 1) * 128, :], [], ["xin2"], "xin2")
                norm_to_T(xin2, ["xin2"], R2, "R2", col0, g1c, "g1c", hbuf=hb2, hkey="hb2")

        gu_rr = {"n": 0}

        def gu_bank():
            b = gu_rr["n"] % 3
            gu_rr["n"] += 1
            return b

        def proj(slot, coff, tok0, bank):
            v = wview(slot)
            for k in range(KD):
                S.op("pe", lambda e, k=k: e.matmul(p_gu[bank][:], lhsT=v[:, k, coff:coff + 128], rhs=R2[:, k, tok0:tok0 + 512],
                                                   start=(k == 0), stop=(k == KD - 1)),
                     [("w", slot, coff // 128)] + tkeys("R2", tok0, 512), [("gu", bank)])

        for ti in range(16):
            load_norm_x(ti, ti * 128)
        S.barrier()
        S.op("dve", lambda e: e.memset(w_e0, 0.0), [], ["e0"])
        S.op("dve", lambda e: e.memset(w_e1, 0.0), [], ["e1"])

        def hgrn_front(h, tb, slot, main, p):
            tok0 = tb * 512
            cf, ci = (128, 256) if main else (0, 128)
            w_qd, w_kd, w_tm, w_gs = w_qdP[p], w_kdP[p], w_tmP[p], w_gsP[p]
            if main:
                bq = gu_bank()
                proj(slot, 0, tok0, bq)
                S.op("act", lambda e: e.activation(out=w_qs, in_=p_gu[bq][:], func=AF.Sigmoid), [("gu", bq)], ["qs"])
                S.op("dve", lambda e: e.tensor_tensor(out=w_qs, in0=p_gu[bq][:], in1=w_qs, op=ALU.mult), [("gu", bq), "qs"], ["qs"])
            bf = gu_bank()
            proj(slot, cf, tok0, bf)
            S.op("act", lambda e: e.activation(out=w_sgn, in_=p_gu[bf][:], func=AF.Sigmoid, scale=-1.0), [("gu", bf)], ["sgn"])
            if main:
                bg = gu_bank()
                proj(slot, 384, tok0, bg)
                S.op("act", lambda e: e.activation(out=w_gs, in_=p_gu[bg][:], func=AF.Sigmoid), [("gu", bg)], [("gs", p)])
                S.op("dve", lambda e: e.tensor_tensor(out=w_gs, in0=p_gu[bg][:], in1=w_gs, op=ALU.mult), [("gu", bg), ("gs", p)], [("gs", p)])
            S.op("dve", lambda e: e.tensor_scalar(out=w_f, in0=w_sgn, scalar1=nomlc[:, h:h + 1], scalar2=1.0, op0=ALU.mult, op1=ALU.add),
                 ["sgn", "nomlc"], ["f"])
            S.op("act", lambda e: e.activation(out=w_f, in_=w_f, func=AF.Ln), ["f"], ["f"])
            S.op("dve", lambda e: e.tensor_tensor_scan(out=w_b, data0=smask[:], data1=w_f, initial=0.0, op0=ALU.mult, op1=ALU.add),
                 ["f", "smask"], ["b"])
            bi = gu_bank()
            proj(slot, ci, tok0, bi)
            S.op("act", lambda e: e.copy(out=w_vi, in_=p_gu[bi][:]), [("gu", bi)], ["vi"])
            bview = w_b.rearrange("p (c t) -> p c t", t=64)
            S.op("act", lambda e: e.activation(out=w_eb8[:, p, :].unsqueeze(2), in_=bview[:, :, 63:64], func=AF.Exp), ["b"], [("eb8", p)])
            for c in range(8):
                dst, dk = (w_e0, "e0") if c % 2 == 0 else (w_e1, "e1")
                S.op("act", lambda e, c=c, dst=dst: e.activation(out=dst[:, c * 64:(c + 1) * 64], in_=w_b[:, c * 64:(c + 1) * 64], func=AF.Exp,
                                                                 scale=-1.0, bias=w_b[:, c * 64 + 63:c * 64 + 64]), ["b"], [dk])
            S.op("dve", lambda e: e.scalar_tensor_tensor(out=w_kk0, in0=w_sgn, scalar=omlc[:, h:h + 1], in1=w_e0, op0=ALU.mult, op1=ALU.mult),
                 ["sgn", "e0", "omlc"], ["kk0"])
            S.op("dve", lambda e: e.scalar_tensor_tensor(out=w_kk1, in0=w_sgn, scalar=omlc[:, h:h + 1], in1=w_e1, op0=ALU.mult, op1=ALU.mult),
                 ["sgn", "e1", "omlc"], ["kk1"])
            if main:
                S.op("act", lambda e: e.activation(out=w_e, in_=w_b, func=AF.Exp), ["b"], ["e"])
                S.op("dve", lambda e: e.tensor_tensor(out=w_qd, in0=w_qs, in1=w_e, op=ALU.mult), ["qs", "e"], [("qd", p)])
                S.op("act", lambda e: e.activation(out=w_e, in_=w_b, func=AF.Exp, scale=-1.0), ["b"], ["e"])
                S.op("dve", lambda e: e.scalar_tensor_tensor(out=w_kd, in0=w_sgn, scalar=omlc[:, h:h + 1], in1=w_e, op0=ALU.mult, op1=ALU.mult),
                     ["sgn", "e", "omlc"], [("kd", p)])

        def hgrn_front_b(h, tb, main, p):
            w_tm = w_tmP[p]
            for j in range(4):
                S.op("pe", lambda e, j=j: e.transpose(out=p_tr[:, j, :], in_=w_vi[:, j * 128:(j + 1) * 128], identity=ident[:]), ["vi", "ident"], ["ptr"])
            for j in range(4):
                S.op("pe", lambda e, j=j: e.transpose(out=p_tr[:, 4 + j, :], in_=w_kk0[:, j * 128:(j + 1) * 128], identity=ident[:]), ["kk0", "ident"], ["ptr"])
            S.op("dve", lambda e: e.tensor_copy(out=w_tm[:, 0:8, :], in_=p_tr[:]), ["ptr"], [("tmA", p)])
            for j in range(4):
                S.op("pe", lambda e, j=j: e.transpose(out=p_tr[:, j, :], in_=w_kk1[:, j * 128:(j + 1) * 128], identity=ident[:]), ["kk1", "ident"], ["ptr"])
            S.op("dve", lambda e: e.tensor_copy(out=w_tm[:, 8:12, :], in_=p_tr[:, 0:4, :]), ["ptr"], [("tmB", p)])

        def hgrn_back(h, tb, main, p):
            tok0 = tb * 512
            w_qd, w_kd, w_tm, w_gs = w_qdP[p], w_kdP[p], w_tmP[p], w_gsP[p]
            for c in range(8):
                j, r = c // 2, c % 2
                bank = 1 + c // 4
                S.op("pe", lambda e, c=c, j=j, r=r, bank=bank: e.matmul(
                    p_acc[bank][:, (c % 4) * 128:(c % 4) * 128 + 128], lhsT=w_tm[:, 4 + 4 * r + j, :], rhs=w_tm[:, j, :],
                    start=True, stop=True), [("tmA", p), ("tmB", p)], [("acc", bank)])
            if main:
                for j in range(4):
                    S.op("pe", lambda e, j=j: e.matmul(p_acc[0][:, j * 128:(j + 1) * 128], lhsT=w_kd[:, j * 128:(j + 1) * 128], rhs=w_qd[:, j * 128:(j + 1) * 128],
                                                       start=True, stop=True), [("kd", p), ("qd", p)], [("acc", 0)])
                S.op("dve", lambda e: e.tensor_tensor(out=w_am, in0=p_acc[0][:], in1=cmask[:], op=ALU.mult), [("acc", 0), "cmask"], ["am"])
            if main:
                S.op("pool", lambda e: e.tensor_copy(out=Sb[:, 0, :], in_=Sfin[:, h, :]), [("S", h)], [("Sb", 0)])
            for c in range(8):
                bank = 1 + c // 4
                src, skey = (Sfin[:, h, :], ("S", h)) if c == 0 else (Sall[:, c, :], ("Sall", c))
                dst, dkey = (Sfin[:, h, :], ("S", h)) if c == 7 else (Sall[:, c + 1, :], ("Sall", c + 1))
                S.op("dve", lambda e, c=c, bank=bank, src=src, dst=dst: e.scalar_tensor_tensor(
                    out=dst, in0=src, scalar=w_eb8[:, p, c:c + 1], in1=p_acc[bank][:, (c % 4) * 128:(c % 4) * 128 + 128],
                    op0=ALU.mult, op1=ALU.add), [skey, ("eb8", p), ("acc", bank)], [dkey])
                if main and c == 3:
                    S.op("act", lambda e: e.copy(out=Sb[:, 1:5, :], in_=Sall[:, 1:5, :]), [("Sall", i) for i in range(1, 5)], [("Sb", i) for i in range(1, 5)])
                if main and c == 6:
                    S.op("act", lambda e: e.copy(out=Sb[:, 5:8, :], in_=Sall[:, 5:8, :]), [("Sall", i) for i in range(5, 8)], [("Sb", i) for i in range(5, 8)])

        def hgrn_back_b(h, tb, main, p):
            tok0 = tb * 512
            w_qd, w_kd, w_tm, w_gs = w_qdP[p], w_kdP[p], w_tmP[p], w_gsP[p]
            if not main:
                return
            for j in range(4):
                S.op("pe", lambda e, j=j: e.matmul(p_acc[3][:, j * 128:(j + 1) * 128], lhsT=w_tm[:, j, :], rhs=w_am[:, j * 128:(j + 1) * 128],
                                                   start=True, stop=False), [("tmA", p), "am"], [("acc", 3)])
                for r in range(2):
                    c = 2 * j + r
                    S.op("pe", lambda e, c=c, r=r: e.matmul(p_acc[3][:, c * 64:(c + 1) * 64], lhsT=Sb[:, c, :], rhs=w_qd[:, c * 64:(c + 1) * 64],
                                                            start=False, stop=(r == 1)), [("Sb", c), ("qd", p)], [("acc", 3)])
            S.op("act", lambda e: e.activation(out=w_X, in_=p_acc[3][:], func=AF.Square), [("acc", 3)], ["X"])
            S.op("pe", lambda e: e.matmul(p_acc[0][:], lhsT=onesF[:], rhs=w_X, start=True, stop=True), ["X", "onesF"], [("acc", 0)])
            S.op("act", lambda e: e.activation(out=w_X, in_=p_acc[0][:], func=AF.Ln, scale=1.0 / 128, bias=EPS), [("acc", 0)], ["X"])
            S.op("act", lambda e: e.activation(out=w_X, in_=w_X, func=AF.Exp, scale=-0.5), ["X"], ["X"])
            S.op("dve", lambda e: e.scalar_tensor_tensor(out=w_X, in0=p_acc[3][:], scalar=ggrc[:, 0:1], in1=w_X, op0=ALU.mult, op1=ALU.mult),
                 [("acc", 3), "X", "ggrc"], ["X"])
            ms = (h * 4 + tb) % 2
            S.op("dve", lambda e: e.tensor_tensor(out=mst[:, ms, :], in0=w_X, in1=w_gs, op=ALU.mult), ["X", ("gs", p)], [("mst", ms)])
            dma("sp", md[8 + h, :, tok0:tok0 + 512], mst[:, ms, :], [("mst", ms)], [("md", 8 + h, tb)], ("mst", ms))

        def hist_kv(h, slot):
            bk = gu_bank()
            proj(slot, 256, 1536, bk)
            S.op("act", lambda e: e.copy(out=w_kk0, in_=p_gu[bk][:]), [("gu", bk)], ["kk0"])
            dma("sp", kvd[h, :, 0:512], w_kk0, ["kk0"], [("kvd", h)], "kvd_k")
            bv = gu_bank()
            proj(slot, 384, 1536, bv)
            S.op("act", lambda e: e.copy(out=w_kk1, in_=p_gu[bv][:]), [("gu", bv)], ["kk1"])
            for j in range(4):
                S.op("pe", lambda e, j=j: e.transpose(out=p_tr[:, j, :], in_=w_kk1[:, j * 128:(j + 1) * 128], identity=ident[:]), ["kk1", "ident"], ["ptr"])
            S.op("dve", lambda e: e.tensor_copy(out=w_vi.rearrange("p (a b) -> p a b", a=4), in_=p_tr[:, 0:4, :]), ["ptr"], ["vi"])
            dma("sp", kvd[h, :, 512:1024], w_vi, ["vi"], [("kvd", h)], "kvd_v")

        def hgrn_phase(main):
            slots = {}

            def load_w(h):
                slot = wslot()
                slots[h] = slot
                if main:
                    for gi in range(4):
                        wload_cols(slot, gi * 128, w_in, 3072 + gi * 1024 + h * 128, 128)
                else:
                    wload_cols(slot, 0, w_in, 3072 + 1024 + h * 128, 128)
                    wload_cols(slot, 128, w_in, 3072 + 2048 + h * 128, 128)
                    wload_cols(slot, 256, w_in, 1024 + h * 128, 128)
                    wload_cols(slot, 384, w_in, 2048 + h * 128, 128)
            units = [(h, tb) for h in range(8) for tb in range(4)]
            load_w(0)
            for n, (h, tb) in enumerate(units):
                if tb == 0 and h + 1 < 8:
                    load_w(h + 1)
                if n > 0:
                    ph, ptb = units[n - 1]
                    hgrn_back(ph, ptb, main, (n - 1) % 2)
                hgrn_front(h, tb, slots[h], main, n % 2)
                hgrn_front_b(h, tb, main, n % 2)
                if n > 0:
                    hgrn_back_b(ph, ptb, main, (n - 1) % 2)
                if (not main) and tb == 3:
                    hist_kv(h, slots[h])
            ph, ptb = units[-1]
            hgrn_back(ph, ptb, main, (len(units) - 1) % 2)
            hgrn_back_b(ph, ptb, main, (len(units) - 1) % 2)

        hgrn_phase(main=False)
        if debug and "S" in debug:
            dma("sp", dbg["S"], Sfin, [("S", h) for h in range(8)], [], "dbgS")
            finals.append(len(S.ops) - 1)
        S.barrier()

        build_tab()
        for ti in range(16):
            load_norm_x(16 + ti, ti * 128)
        S.barrier()
        S.op("dve", lambda e: e.memset(a_v, 1.0), [], ["a_v_init"])

        def attention_head(h):
            slot = wslot()
            wload_cols(slot, 0, w_in, h * 128, 128)
            wload_cols(slot, 128, w_in, 1024 + h * 128, 128)
            wload_cols(slot, 256, w_in, 2048 + h * 128, 128)
            dma("sp", a_k[:, 0:512], kvd[h, :, 0:512], [("kvd", h)], [("a_k", 0)], "akh")
            dma("sp", a_v[:, 0:4, 0:128], kvd[h, :, 512:1024].rearrange("p (a b) -> p a b", a=4), [("kvd", h), "a_v_init"], [("a_v", 0)], "avh")
            for tb in range(4):
                b = gu_bank()
                proj(slot, 0, tb * 512, b)
                S.op("act", lambda e, b=b, tb=tb: e.copy(out=a_q[:, tb * 512:(tb + 1) * 512], in_=p_gu[b][:]), [("gu", b)], [("a_q", tb)])
                b = gu_bank()
                proj(slot, 128, tb * 512, b)
                S.op("act", lambda e, b=b, tb=tb: e.copy(out=a_k[:, 512 + tb * 512:512 + (tb + 1) * 512], in_=p_gu[b][:]), [("gu", b)], [("a_k", 1 + tb)])
                b = gu_bank()
                proj(slot, 256, tb * 512, b)
                vs = tb % 2
                S.op("act", lambda e, b=b, vs=vs: e.copy(out=a_vT[:, vs, :], in_=p_gu[b][:]), [("gu", b)], [("a_vT", vs)])
                for j in range(4):
                    S.op("pe", lambda e, j=j, vs=vs: e.transpose(out=p_tr[:, j, :], in_=a_vT[:, vs, j * 128:(j + 1) * 128], identity=ident[:]),
                         [("a_vT", vs), "ident"], ["ptr"])
                S.op("dve", lambda e, tb=tb: e.tensor_copy(out=a_v[:, 4 + tb * 4:8 + tb * 4, 0:128], in_=p_tr[:, 0:4, :]), ["ptr", "a_v_init"], [("a_v", 1 + tb)])
            def sbanks(qt):
                if qt % 2 == 0:
                    return (p_acc[0], ("acc", 0)), (p_acc[1], ("acc", 1)), (p_acc[2], ("acc", 2))
                return (p_gu[0], ("gu", 0)), (p_gu[1], ("gu", 1)), (p_acc[3], ("acc", 3))

            def att_S(qt):
                (bA, kA), (bB, kB), _ = sbanks(qt)
                kreads = sorted(set(("a_k", (qt + kt) // 4) for kt in range(5)))
                for kt in range(5):
                    bank, bkey = (bA, kA) if kt < 4 else (bB, kB)
                    col = (kt % 4) * 128
                    S.op("pe", lambda e, kt=kt, bank=bank, col=col: e.matmul(
                        bank[:, col:col + 128], lhsT=a_k[:, (qt + kt) * 128:(qt + kt + 1) * 128], rhs=a_q[:, qt * 128:(qt + 1) * 128],
                        start=True, stop=False), kreads + [("a_q", qt // 4)], [bkey])
                    S.op("pe", lambda e, kt=kt, bank=bank, col=col: e.matmul(
                        bank[:, col:col + 128], lhsT=ident[:], rhs=tab[:, h, kt * 128:(kt + 1) * 128],
                        start=False, stop=True), [("tab", h), "ident"], [bkey])

            def att_rest(qt):
                pp = qt % 2
                (bA, kA), (bB, kB), (bP, kP) = sbanks(qt)
                vreads = sorted(set(("a_v", (qt + kt) // 4) for kt in range(5)))
                nh = max(0, min(4, 4 - qt))
                if nh > 0:
                    S.op("act", lambda e: e.activation(out=a_p[:, pp, 0:nh * 128], in_=bA[:, 0:nh * 128], func=AF.Exp, scale=SCALE, bias=hneg[:, 0:1]),
                         [kA, "hneg"], [("a_p", pp)])
                if nh < 4:
                    S.op("act", lambda e: e.activation(out=a_p[:, pp, nh * 128:512], in_=bA[:, nh * 128:512], func=AF.Exp, scale=SCALE),
                         [kA], [("a_p", pp)])
                S.op("act", lambda e: e.activation(out=a_p[:, pp, 512:640], in_=bB[:, 0:128], func=AF.Exp, scale=SCALE),
                     [kB], [("a_p", pp)])
                for kt in range(5):
                    S.op("pe", lambda e, kt=kt: e.matmul(bP[:, 0:129], lhsT=a_p[:, pp, kt * 128:(kt + 1) * 128], rhs=a_v[:, qt + kt, 0:129],
                                                         start=(kt == 0), stop=(kt == 4)), [("a_p", pp)] + vreads, [kP])
                S.op("dve", lambda e: e.reciprocal(out=a_rd[:, pp:pp + 1], in_=bP[:, 128:129]), [kP], [("a_rd", pp)])
                S.op("dve", lambda e: e.tensor_scalar(out=a_ab[:, pp, :], in0=bP[:, 0:128], scalar1=a_rd[:, pp:pp + 1], scalar2=None, op0=ALU.mult),
                     [kP, ("a_rd", pp)], [("a_ab", pp)])
                S.op("act", lambda e: e.activation(out=a_jk, in_=a_ab[:, pp, :], func=AF.Square, accum_out=ssa[:, qt, h:h + 1]),
                     [("a_ab", pp)], ["a_jk", ("ssa", qt, h)])
                S.op("pe", lambda e: e.transpose(out=p_tr[:, qt % 4, :], in_=a_ab[:, pp, :], identity=ident[:]), [("a_ab", pp), "ident"], ["ptr"])
                if qt % 4 == 3:
                    g = qt // 4
                    ms = (h * 4 + g) % 2
                    S.op("dve", lambda e: e.tensor_scalar(out=mst_a[:, ms, :].rearrange("p (a b) -> p a b", a=4), in0=p_tr[:, 0:4, :], scalar1=gatc[:, h:h + 1], scalar2=None, op0=ALU.mult),
                         ["ptr", "gatc"], [("mst", ms)])
                    dma("sp", md[h, :, g * 512:(g + 1) * 512], mst_a[:, ms, :], [("mst", ms)], [("md", h, g)], ("mst", ms))

            att_S(0)
            for qt_ in range(16):
                if qt_ + 1 < 16:
                    att_S(qt_ + 1)
                att_rest(qt_)


        if stop_after != "H":
            for h in range(int(os.environ.get('ATT_HEADS', '8'))):
                attention_head(h)
            S.barrier()
        if stop_after not in ("H", "att"):
            S.op("dve", lambda e: e.memset(w_e0, 0.0), [], ["e0"])
            S.op("dve", lambda e: e.memset(w_e1, 0.0), [], ["e1"])
            hgrn_phase(main=True)
            S.barrier()

        if stop_after is None:
            S.op("dve", lambda e: e.tensor_reduce(out=rsa[:], in_=ssa[:], axis=AX.X, op=ALU.add), [], ["rsa"])
            S.op("act", lambda e: e.activation(out=rsa[:], in_=rsa[:], func=AF.Sqrt, scale=1.0 / 1024, bias=EPS), ["rsa"], ["rsa"])
            S.op("dve", lambda e: e.reciprocal(out=rsa[:], in_=rsa[:]), ["rsa"], ["rsa"])
            for blk in range(NT // TB):
                t0 = blk * TB
                dma("sp", mb, md[:, :, t0:t0 + TB].rearrange("k p t -> p k t"), [], ["mb"], "mb")
                for tt in range(4):
                    dma("sp", x1[:, tt, :], xw[NHIST + t0 + tt * 128:NHIST + t0 + (tt + 1) * 128, :], [], [("x1", tt)], ("x1", tt))
                for dblk in range(4):
                    slot = wslot()
                    wload_cols(slot, 0, w_out, dblk * 512, 512)
                    wv = wview(slot)
                    for tt in range(4):
                        qt = blk * 4 + tt
                        bA, bR = (0, 1) if tt % 2 == 0 else (2, 3)
                        for k in range(16):
                            bank = bA if k < 8 else bR
                            S.op("pe", lambda e, k=k, tt=tt, bank=bank, wv=wv: e.matmul(
                                p_acc[bank][:], lhsT=mb[:, k, tt * 128:(tt + 1) * 128], rhs=wv[:, k, :], start=(k % 8 == 0), stop=(k % 8 == 7)),
                                ["mb"] + wkeys(slot), [("acc", bank)])
                        S.op("dve", lambda e, tt=tt, qt=qt, bA=bA, dblk=dblk: e.scalar_tensor_tensor(
                            out=x1[:, tt, dblk * 512:(dblk + 1) * 512], in0=p_acc[bA][:], scalar=rsa[:, qt:qt + 1], in1=x1[:, tt, dblk * 512:(dblk + 1) * 512],
                            op0=ALU.mult, op1=ALU.add), [("acc", bA), "rsa", ("x1", tt)], [("x1", tt)])
                        S.op("dve", lambda e, tt=tt, bR=bR, dblk=dblk: e.tensor_tensor(
                            out=x1[:, tt, dblk * 512:(dblk + 1) * 512], in0=x1[:, tt, dblk * 512:(dblk + 1) * 512], in1=p_acc[bR][:], op=ALU.add),
                            [("acc", bR), ("x1", tt)], [("x1", tt)])
                if debug and "x1" in debug:
                    for tt in range(4):
                        dma("sp", dbg["x1"][t0 + tt * 128:t0 + (tt + 1) * 128, :], x1[:, tt, :], [("x1", tt)], [], ("dbgx1", tt))
                        finals.append(len(S.ops) - 1)
                for tt in range(4):
                    norm_to_T(x1[:, tt, :], [("x1", tt)], h2T, "h2T", tt * 128, g2c, "g2c")
                h2keys = tkeys("h2T", 0, 512)
                for fp in range(NFT // 2):
                    slot = wslot()
                    wload_cols(slot, 0, w_gate, fp * 256, 256)
                    wload_cols(slot, 256, w_up, fp * 256, 256)
                    wv = wview(slot)
                    for fi in range(2):
                        ft = fp * 2 + fi
                        bg = gu_bank()
                        for k in range(16):
                            S.op("pe", lambda e, k=k, fi=fi, bg=bg, wv=wv: e.matmul(p_gu[bg][:], lhsT=wv[:, k, fi * 128:(fi + 1) * 128], rhs=h2T[:, k, :],
                                                                                  start=(k == 0), stop=(k == 15)), [("w", slot, fi)] + h2keys, [("gu", bg)])
                        bu = gu_bank()
                        for k in range(16):
                            S.op("pe", lambda e, k=k, fi=fi, bu=bu, wv=wv: e.matmul(p_gu[bu][:], lhsT=wv[:, k, 256 + fi * 128:256 + (fi + 1) * 128], rhs=h2T[:, k, :],
                                                                                  start=(k == 0), stop=(k == 15)), [("w", slot, 2 + fi)] + h2keys, [("gu", bu)])
                        sl = ft % 2
                        S.op("act", lambda e, bg=bg, sl=sl: e.activation(out=silb[:, sl, :], in_=p_gu[bg][:], func=AF.Silu), [("gu", bg)], [("sil", sl)])
                        S.op("dve", lambda e, bu=bu, sl=sl, ft=ft: e.tensor_tensor(out=ffT[:, ft, :], in0=p_gu[bu][:], in1=silb[:, sl, :], op=ALU.mult),
                             [("gu", bu), ("sil", sl)], [("ffT", ft)])
                for dblk in range(4):
                    for fg, (f0, nf) in enumerate(((0, 16), (16, 16), (32, 12))):
                        slot = wslot()
                        wv = wview(slot)
                        dma("pool", wv[:, 0:nf, :], w_down[f0 * 128:(f0 + nf) * 128, dblk * 512:(dblk + 1) * 512].rearrange("(k p) c -> p k c", p=128),
                            [], wkeys(slot), ("w", slot, 0))
                        for fl in range(nf):
                            fc = f0 + fl
                            for tt in range(4):
                                S.op("pe", lambda e, fl=fl, fc=fc, tt=tt, wv=wv: e.matmul(p_acc[tt][:], lhsT=ffT[:, fc, tt * 128:(tt + 1) * 128], rhs=wv[:, fl, :],
                                                                                        start=(fc == 0), stop=(fc == NFT - 1)), wkeys(slot) + [("ffT", fc)], [("acc", tt)])
                    for tt in range(4):
                        S.op("dve", lambda e, tt=tt, dblk=dblk: e.tensor_tensor(
                            out=x1[:, tt, dblk * 512:(dblk + 1) * 512], in0=x1[:, tt, dblk * 512:(dblk + 1) * 512], in1=p_acc[tt][:], op=ALU.add),
                            [("acc", tt), ("x1", tt)], [("x1", tt)])
                for tt in range(4):
                    i = 32 + (blk * 4 + tt) % 32
                    S.op("act", lambda e, tt=tt, i=i: e.activation(out=junk[:], in_=x1[:, tt, :], func=AF.Square, accum_out=ss[:, i:i + 1]),
                         [("x1", tt)], ["junk", ("ss", i)])
                    S.op("act", lambda e, i=i: e.activation(out=sd[:, i:i + 1], in_=ss[:, i:i + 1], func=AF.Sqrt, scale=1.0 / D, bias=EPS), [("ss", i)], [("sd", i)])
                    S.op("dve", lambda e, i=i: e.reciprocal(out=rstd[:, i:i + 1], in_=sd[:, i:i + 1]), [("sd", i)], [("rstd", i)])
                    S.op("dve", lambda e, tt=tt, i=i: e.scalar_tensor_tensor(out=yo, in0=x1[:, tt, :], scalar=rstd[:, i:i + 1], in1=gfb[:],
                                                                           op0=ALU.mult, op1=ALU.mult), [("x1", tt), ("rstd", i), "gfb"], ["yo"])
                    dma("sp", y[t0 + tt * 128:t0 + (tt + 1) * 128, :], yo, ["yo"], [], "yo")
                    finals.append(len(S.ops) - 1)

        if debug and "md" in debug:
            S.barrier()
            for m in range(16):
                for g in range(4):
                    dma("sp", hb[:, 0:512], md[m, :, g * 512:(g + 1) * 512], [], ["hb"], "hbd")
                    S.op("dve", lambda e: e.tensor_copy(out=gfb[:, 0:512], in_=hb[:, 0:512]), ["hb"], ["gfb"])
                    dma("sp", dbg["md"][m, :, g * 512:(g + 1) * 512], gfb[:, 0:512], ["gfb"], [], "gfbd")
                    finals.append(len(S.ops) - 1)
        stats = S.emit(final_waits=finals)
    return nc, stats


def _consts():
    ident = np.eye(128, dtype=np.float32)
    kk = np.arange(128)[:, None]
    amask = np.zeros((128, 640), np.float32)
    iq = np.arange(128)[None, :]
    amask[:, 0:128] = np.where((iq >= 64) & (kk < 64), -1e5, 0.0)
    amask[:, 512:640] = np.where((iq < 64) & (kk >= 64), -1e5, 0.0)
    s = np.arange(128)[:, None]
    t = np.arange(128)[None, :]
    tri = ((s // 64 == t // 64) & (s <= t)).astype(np.float32)
    cmask = np.tile(tri, (1, 4))
    smask = np.ones((128, 512), np.float32)
    smask[:, 0::64] = 0.0
    return ident, amask, cmask, smask


def _bias_index():
    kt = np.arange(5)[None, :, None]
    kk = np.arange(128)[:, None, None]
    iq = np.arange(128)[None, None, :]
    dist = 512 - 128 * kt + iq - kk
    return (np.clip(dist, -128, 128) + 128).reshape(128, 640)


_PROG = {}


def kernel(x, norm1_gain, w_in, rel_bias, lower_bounds, grn_norm_gain, attn_out_gain, w_out, norm2_gain,
           w_gate, w_up, w_down, final_gain):
    x = np.asarray(x, np.float32)
    if "nc" not in _PROG:
        _PROG["nc"] = build_program()[0]
    nc = _PROG["nc"]
    ident, amask, cmask, smask = _consts()
    idx = _bias_index()
    btab = np.ascontiguousarray(np.asarray(rel_bias, np.float32)[0][:, idx])
    shared = {
        "w_in": np.ascontiguousarray(np.asarray(w_in, np.float32)[0]),
        "w_out": np.ascontiguousarray(np.asarray(w_out, np.float32)[0]),
        "w_gate": np.ascontiguousarray(np.asarray(w_gate, np.float32)[0]),
        "w_up": np.ascontiguousarray(np.asarray(w_up, np.float32)[0]),
        "w_down": np.ascontiguousarray(np.asarray(w_down, np.float32)[0]),
        "g1": np.asarray(norm1_gain, np.float32).reshape(1, D),
        "g2": np.asarray(norm2_gain, np.float32).reshape(1, D),
        "gf": np.asarray(final_gain, np.float32).reshape(1, D),
        "gat": np.asarray(attn_out_gain, np.float32).reshape(1, 1024),
        "ggr": np.asarray(grn_norm_gain, np.float32).reshape(1, 128),
        "lbnd": np.ascontiguousarray(np.asarray(lower_bounds, np.float32)),
        "btab": btab, "c_ident": ident, "c_amask": amask, "c_cmask": cmask, "c_smask": smask,
    }
    in_maps = []
    for c in range(8):
        b, half = c // 2, c % 2
        xwc = np.zeros((NHIST + NT, D), np.float32)
        if half == 1:
            xwc[:] = x[b]
        else:
            xwc[NHIST:] = x[b, :NT]
        m = dict(shared)
        m["xw"] = xwc
        m["c_hneg"] = np.full((128, 1), 0.0 if half == 1 else -30000.0, np.float32)
        in_maps.append(m)
    res = run_bass_kernel_spmd(nc, in_maps, core_ids=list(range(8)))
    out = np.empty((4, 4096, D), np.float32)
    for c in range(8):
        b, half = c // 2, c % 2
        out[b, half * NT:(half + 1) * NT] = res.results[c]["y"]
    return out
```

```python
import contextlib
import os
import numpy as np
import ml_dtypes
import concourse.bass as bass
import concourse.mybir as mybir
from concourse.bass_utils import run_bass_kernel_spmd

F32 = mybir.dt.float32
BF16 = mybir.dt.bfloat16
AF = mybir.ActivationFunctionType
ALU = mybir.AluOpType
AX = mybir.AxisListType

ENGS = ("pe", "act", "dve", "pool", "sp")

D = 2048
NT = 2048
NHIST = 2048
KD = 16
DFF = 5632
NFT = 44
EPS = 1e-6
INCOLS = 7168
TB = 512
SCALE = 128 ** -0.5
SQ128 = 128 ** 0.5


class Sched:
    def __init__(self, nc, same_engine_sync=True):
        self.nc = nc
        self.ops = []
        self.last_writer = {}
        self.readers = {}
        self.same_engine_sync = same_engine_sync
        self.dma_keys = {}

    def op(self, eng, fn, reads=(), writes=(), dma_key=None):
        idx = len(self.ops)
        deps = set()
        for k in reads:
            w = self.last_writer.get(k)
            if w is not None:
                deps.add(w)
        for k in writes:
            w = self.last_writer.get(k)
            if w is not None:
                deps.add(w)
            for r in self.readers.get(k, ()):
                deps.add(r)
        if dma_key is not None:
            prev = self.dma_keys.get(dma_key)
            if prev is not None:
                deps.add(prev)
            self.dma_keys[dma_key] = idx
        deps.discard(idx)
        self.ops.append(dict(eng=eng, fn=fn, deps=deps, dma_key=dma_key, needed=False))
        for k in writes:
            self.last_writer[k] = idx
            self.readers[k] = []
        for k in reads:
            if k not in writes:
                lst = self.readers.setdefault(k, [])
                if dma_key is None:
                    lst[:] = [r for r in lst if not (self.ops[r]["eng"] == eng and self.ops[r]["dma_key"] is None)]
                lst.append(idx)
        return idx

    def barrier(self):
        last = {}
        for i, o in enumerate(self.ops):
            if o["fn"] is None:
                continue
            last[("e", o["eng"])] = i
            if o["dma_key"] is not None:
                last[("d", o["dma_key"])] = i
        deps = set(last.values())
        for e in ENGS:
            self.ops.append(dict(eng=e, fn=None, deps=set(deps), dma_key=None, needed=False))

    def emit(self, final_waits=()):
        nc = self.nc
        ops = self.ops
        for i, o in enumerate(ops):
            nd = set()
            for d in o["deps"]:
                p = ops[d]
                if p["dma_key"] is None and p["eng"] == o["eng"] and not self.same_engine_sync:
                    continue
                if p["dma_key"] is None and p["eng"] == "pe" and o["eng"] == "pe":
                    continue
                nd.add(d)
            o["deps"] = nd
            for d in nd:
                ops[d]["needed"] = True
        for d in final_waits:
            ops[d]["needed"] = True
        tick = {e: 0 for e in ENGS}
        dma_cnt = {}
        for o in ops:
            if o["dma_key"] is not None:
                dma_cnt[o["dma_key"]] = dma_cnt.get(o["dma_key"], 0) + 1
                o["sem"] = ("dma", o["dma_key"])
                o["tick"] = 16 * dma_cnt[o["dma_key"]]
            elif o["needed"]:
                tick[o["eng"]] += 1
                o["sem"] = ("eng", o["eng"])
                o["tick"] = tick[o["eng"]]
        sem_names = [("eng", e) for e in ENGS] + [("dma", k) for k in dma_cnt]
        with contextlib.ExitStack() as st:
            sems = {}
            for i, sn in enumerate(sem_names):
                sems[sn] = st.enter_context(nc.semaphore("s%d" % i))
            block = st.enter_context(nc.Block())
            per_eng = {e: [] for e in ENGS}
            for i, o in enumerate(ops):
                per_eng[o["eng"]].append(i)

            def body(ename, engine):
                waited = {}
                for i in per_eng[ename]:
                    o = ops[i]
                    need = {}
                    for d in o["deps"]:
                        p = ops[d]
                        s = p["sem"]
                        need[s] = max(need.get(s, 0), p["tick"])
                    for s, v in need.items():
                        if waited.get(s, 0) >= v:
                            continue
                        engine.wait_ge(sems[s], v)
                        waited[s] = v
                    if o["fn"] is None:
                        continue
                    ins = o["fn"](engine)
                    if o["dma_key"] is not None:
                        ins.then_inc(sems[o["sem"]], 16)
                    elif o["needed"]:
                        ins.then_inc(sems[o["sem"]], 1)
                if ename == "sp":
                    need = {}
                    for d in final_waits:
                        p = ops[d]
                        need[p["sem"]] = max(need.get(p["sem"], 0), p["tick"])
                    for s, v in need.items():
                        engine.wait_ge(sems[s], v)

            @block.tensor
            def _(e):
                body("pe", e)

            @block.scalar
            def _(e):
                body("act", e)

            @block.vector
            def _(e):
                body("dve", e)

            @block.gpsimd
            def _(e):
                body("pool", e)

            @block.sync
            def _(e):
                body("sp", e)
        return {e: len(per_eng[e]) for e in ENGS}, tick, len(sem_names)


def build_program(debug=None, stop_after=None):
    nc = bass.Bass("TRN2", target_bir_lowering=False)
    dt_in = lambda name, shape: nc.dram_tensor(name, shape, F32, kind="ExternalInput").ap()
    xw = dt_in("xw", [NHIST + NT, D])
    w_in = dt_in("w_in", [D, INCOLS])
    w_out = dt_in("w_out", [D, D])
    w_gate = dt_in("w_gate", [D, DFF])
    w_up = dt_in("w_up", [D, DFF])
    w_down = dt_in("w_down", [DFF, D])
    g1_d = dt_in("g1", [1, D])
    g2_d = dt_in("g2", [1, D])
    gf_d = dt_in("gf", [1, D])
    gat_d = dt_in("gat", [1, 1024])
    ggr_d = dt_in("ggr", [1, 128])
    lbnd_d = dt_in("lbnd", [2, 1024])
    btab_d = dt_in("btab", [8, 128, 640])
    cid_d = dt_in("c_ident", [128, 128])
    cam_d = dt_in("c_amask", [128, 640])
    ccm_d = dt_in("c_cmask", [128, 512])
    csm_d = dt_in("c_smask", [128, 512])
    chn_d = dt_in("c_hneg", [128, 1])
    y = nc.dram_tensor("y", [NT, D], F32, kind="ExternalOutput").ap()
    md = nc.dram_tensor("md", [16, 128, NT], BF16, kind="Internal").ap()
    kvd = nc.dram_tensor("kvd", [8, 128, 1024], BF16, kind="Internal").ap()
    dbg = {}
    if debug:
        for name, shape in debug.items():
            dbg[name] = nc.dram_tensor("dbg_" + name, shape, F32, kind="ExternalOutput").ap()

    S = Sched(nc)
    finals = []
    with contextlib.ExitStack() as st:
        def sb(name, shape, dt=F32):
            return st.enter_context(nc.sbuf_tensor(name, shape, dt))

        def ps(name, shape, dt=F32):
            return st.enter_context(nc.psum_tensor(name, shape, dt))

        R2 = sb("R2", [128, 16, 2048], BF16)
        R1 = sb("R1", [128, 28672], BF16)
        NSLOT = 3
        wpool = sb("wpool", [128, NSLOT, 8192], BF16)
        ident = sb("ident", [128, 128], BF16)
        identF = sb("identF", [128, 128], F32)
        onesF = sb("onesF", [128, 128], F32)
        amask = sb("amask", [128, 640], F32)
        cmask = sb("cmask", [128, 512], F32)
        smask = sb("smask", [128, 512], F32)
        hneg = sb("hneg", [128, 1], F32)
        g1c = sb("g1c", [128, 16], F32)
        g2c = sb("g2c", [128, 16], F32)
        gfb = sb("gfb", [128, 2048], F32)
        gatc = sb("gatc", [128, 8], F32)
        ggrc = sb("ggrc", [128, 1], F32)
        lbc = sb("lbc", [128, 8], F32)
        lb1 = sb("lb1", [128, 8], F32)
        omlc = sb("omlc", [128, 8], F32)
        nomlc = sb("nomlc", [128, 8], F32)
        junk = sb("junk", [128, 2048], BF16)
        hb = sb("hb", [128, 2048], BF16)
        silb = sb("silb", [128, 2, 512], F32)
        ss = sb("ss", [128, 64], F32)
        sd = sb("sd", [128, 64], F32)
        rstd = sb("rstd", [128, 64], F32)
        ssa = sb("ssa", [128, 16, 8], F32)
        rsa = sb("rsa", [128, 16], F32)

        def carve(off, shape, dt):
            n = 1
            for s_ in shape[1:]:
                n *= s_
            esz = 2 if dt == BF16 else 4
            a = R1[:, off // 2: off // 2 + n * esz // 2]
            if dt == F32:
                a = a.bitcast(F32)
            if len(shape) == 3:
                a = a.rearrange("p (a b) -> p a b", a=shape[1])
            elif len(shape) == 4:
                a = a.rearrange("p (a b c) -> p a b c", a=shape[1], b=shape[2])
            return a

        xin = carve(0, [128, 2048], F32)
        Sfin = carve(8192, [128, 8, 128], F32)
        Sall = carve(12288, [128, 8, 128], F32)
        Sb = carve(16384, [128, 8, 128], BF16)
        tab = carve(18432, [128, 8, 640], BF16)
        WK = 28672
        KB2 = 2048
        w_sgn = carve(WK + 0 * KB2, [128, 512], F32)
        w_f = carve(WK + 1 * KB2, [128, 512], F32)
        w_b = carve(WK + 2 * KB2, [128, 512], F32)
        w_e0 = carve(WK + 3 * KB2, [128, 512], F32)
        w_e1 = carve(WK + 4 * KB2, [128, 512], F32)
        w_e = carve(WK + 5 * KB2, [128, 512], F32)
        w_qs = carve(WK + 6 * KB2, [128, 512], F32)
        w_gs0 = carve(WK + 7 * KB2, [128, 512], F32)
        o = WK + 8 * KB2
        w_qd0 = carve(o, [128, 512], BF16); o += 1024
        w_kd0 = carve(o, [128, 512], BF16); o += 1024
        w_kk0 = carve(o, [128, 512], BF16); o += 1024
        w_kk1 = carve(o, [128, 512], BF16); o += 1024
        w_vi = carve(o, [128, 512], BF16); o += 1024
        w_tm0 = carve(o, [128, 12, 128], BF16); o += 3072
        w_am = carve(o, [128, 512], BF16); o += 1024
        mst = carve(o, [128, 2, 512], BF16); o += 2048
        w_eb8 = carve(o, [128, 2, 8], F32); o += 64
        assert o <= 57344, o
        o = 18432
        w_gs1 = carve(o, [128, 512], F32); o += 2048
        w_X = carve(o, [128, 512], F32); o += 2048
        w_qd1 = carve(o, [128, 512], BF16); o += 1024
        w_kd1 = carve(o, [128, 512], BF16); o += 1024
        w_tm1 = carve(o, [128, 12, 128], BF16); o += 3072
        assert o <= 28672, o
        w_gsP = (w_gs0, w_gs1)
        w_qdP = (w_qd0, w_qd1)
        w_kdP = (w_kd0, w_kd1)
        w_tmP = (w_tm0, w_tm1)
        o = WK
        a_q = carve(o, [128, 2048], BF16); o += 4096
        a_k = carve(o, [128, 2560], BF16); o += 5120
        a_vT = carve(o, [128, 2, 512], BF16); o += 2048
        a_v = carve(o, [128, 20, 132], BF16); o += 5280
        a_p = carve(o, [128, 2, 640], BF16); o += 2560
        a_ab = carve(o, [128, 2, 128], BF16); o += 512
        a_jk = carve(o, [128, 128], BF16); o += 256
        a_rd = carve(o, [128, 2], F32); o += 8
        o = (o + 63) // 64 * 64
        mst_a = carve(o, [128, 2, 512], BF16); o += 2048
        assert o <= 57344, o
        mb = carve(0, [128, 16, 512], BF16)
        x1 = carve(16384, [128, 4, 2048], F32)
        yo = carve(49152, [128, 2048], F32)
        r2f = R2[:].rearrange("p k t -> p (k t)")
        ffT = r2f[:, 0:22528].rearrange("p (f t) -> p f t", f=NFT)
        h2T = r2f[:, 22528:30720].rearrange("p (k t) -> p k t", k=16)

        p_acc = [ps("pacc%d" % i, [128, 512]) for i in range(4)]
        p_gu = [ps("pgu%d" % i, [128, 512]) for i in range(3)]
        p_tr = ps("ptr", [128, 8, 128], BF16)

        def dma(eng, out, in_, reads, writes, key, slow=False):
            if slow:
                return S.op(eng, lambda e: e.dma_start(out=out, in_=in_, allow_slow_non_contiguous=True), reads=reads, writes=writes, dma_key=key)
            return S.op(eng, lambda e: e.dma_start(out=out, in_=in_), reads=reads, writes=writes, dma_key=key)

        dma("sp", identF[:], cid_d[:, :], [], ["identF"], "c0")
        dma("sp", amask[:], cam_d[:, :], [], ["amask"], "c1")
        dma("sp", cmask[:], ccm_d[:, :], [], ["cmask"], "c2")
        dma("sp", smask[:], csm_d[:, :], [], ["smask"], "c3")
        dma("sp", hneg[:], chn_d[:, :], [], ["hneg"], "c4")
        dma("sp", gfb[:], gf_d.broadcast_to([128, D]), [], ["gfb"], "c5")
        cvec = lambda d_: d_.rearrange("o (k p) -> p (o k)", p=128)
        dma("sp", g1c[:], cvec(g1_d), [], ["g1c"], "c6", slow=True)
        dma("sp", g2c[:], cvec(g2_d), [], ["g2c"], "c7", slow=True)
        dma("sp", gatc[:], cvec(gat_d), [], ["gatc"], "c8", slow=True)
        dma("sp", ggrc[:], cvec(ggr_d), [], ["ggrc"], "c9", slow=True)
        dma("sp", lbc[:], cvec(lbnd_d[0:1, :]), [], ["lbc"], "c10", slow=True)
        dma("sp", lb1[:], cvec(lbnd_d[1:2, :]), [], ["lb1"], "c11", slow=True)
        S.op("dve", lambda e: e.tensor_copy(out=ident[:], in_=identF[:]), ["identF"], ["ident"])
        S.op("dve", lambda e: e.memset(onesF[:], 1.0), [], ["onesF"])
        S.op("dve", lambda e: e.memset(Sfin, 0.0), [], [("S", h) for h in range(8)])
        S.op("dve", lambda e: e.tensor_tensor(out=lbc[:], in0=lbc[:], in1=lb1[:], op=ALU.subtract), ["lbc", "lb1"], ["lbc"])
        S.op("act", lambda e: e.activation(out=lbc[:], in_=lbc[:], func=AF.Sigmoid), ["lbc"], ["lbc"])
        S.op("dve", lambda e: e.tensor_scalar(out=omlc[:], in0=lbc[:], scalar1=-1.0, scalar2=1.0, op0=ALU.mult, op1=ALU.add), ["lbc"], ["omlc"])
        S.op("dve", lambda e: e.tensor_scalar(out=nomlc[:], in0=lbc[:], scalar1=1.0, scalar2=-1.0, op0=ALU.mult, op1=ALU.add), ["lbc"], ["nomlc"])
        def build_tab():
            for h in range(8):
                dma("sp", xin[:, 0:640], btab_d[h], [], ["xin"], "xin")
                S.op("dve", lambda e, h=h: e.scalar_tensor_tensor(out=tab[:, h, :], in0=xin[:, 0:640], scalar=SQ128, in1=amask[:],
                                                                  op0=ALU.mult, op1=ALU.add), ["xin", "amask"], [("tab", h)])

        wstate = {"n": 0}

        def wslot():
            s = wstate["n"] % NSLOT
            wstate["n"] += 1
            return s

        def wview(slot):
            return wpool[:, slot, :].rearrange("p (k c) -> p k c", k=16)

        def wkeys(slot, q0=0, q1=4):
            return [("w", slot, q) for q in range(q0, q1)]

        def wload_cols(slot, dst_off, src, col0, ncols):
            q0, q1 = dst_off // 128, (dst_off + ncols) // 128
            dma("pool", wview(slot)[:, :, dst_off:dst_off + ncols],
                src[:, col0:col0 + ncols].rearrange("(k p) c -> p k c", p=128),
                [], wkeys(slot, q0, q1), ("w", slot, q0))

        ncount = {"n": 0}

        def norm_to_T(src_ap, src_reads, dstT, dname, col0, gain_c, gkey, hbuf=None, hkey="hb"):
            hbv = hb[:] if hbuf is None else hbuf
            i = ncount["n"] % 32
            ncount["n"] += 1
            S.op("act", lambda e: e.activation(out=junk[:], in_=src_ap, func=AF.Square, accum_out=ss[:, i:i + 1]),
                 src_reads, ["junk", ("ss", i)])
            S.op("act", lambda e: e.activation(out=sd[:, i:i + 1], in_=ss[:, i:i + 1], func=AF.Sqrt, scale=1.0 / D, bias=EPS),
                 [("ss", i)], [("sd", i)])
            S.op("dve", lambda e: e.reciprocal(out=rstd[:, i:i + 1], in_=sd[:, i:i + 1]), [("sd", i)], [("rstd", i)])
            S.op("dve", lambda e: e.tensor_scalar(out=hbv, in0=src_ap, scalar1=rstd[:, i:i + 1], scalar2=None, op0=ALU.mult),
                 src_reads + [("rstd", i)], [hkey])
            for half in range(2):
                for j in range(8):
                    k = half * 8 + j
                    S.op("pe", lambda e, k=k, j=j: e.transpose(out=p_tr[:, j, :], in_=hbv[:, k * 128:(k + 1) * 128], identity=ident[:]),
                         [hkey, "ident"], ["ptr"])
                S.op("dve", lambda e, half=half: e.tensor_tensor(
                    out=dstT[:, half * 8:half * 8 + 8, col0:col0 + 128], in0=p_tr[:],
                    in1=gain_c[:, half * 8:half * 8 + 8].unsqueeze(2).broadcast_to([128, 8, 128]), op=ALU.mult),
                    ["ptr", gkey], [("T", dname, col0, half)])

        def tkeys(dname, tok0, ntok):
            return [("T", dname, c, half) for c in range(tok0, tok0 + ntok, 128) for half in range(2)]

        xin2 = carve(WK, [128, 2048], F32)
        hb2 = carve(WK + 8192, [128, 2048], BF16)

        def load_norm_x(ti, col0):
            if ti % 2 == 0:
                dma("sp", xin, xw[ti * 128:(ti + 1) * 128, :], [], ["xin"], "xin")
                norm_to_T(xin, ["xin"], R2, "R2", col0, g1c, "g1c")
            else:
                dma("sp", xin2, xw[ti * 128:(ti + 1) * 128, :], [], ["xin2"], "xin2")
                norm_to_T(xin2, ["xin2"], R2, "R2", col0, g1c, "g1c", hbuf=hb2, hkey="hb2")

        gu_rr = {"n": 0}

        def gu_bank():
            b = gu_rr["n"] % 3
            gu_rr["n"] += 1
            return b

        def proj(slot, coff, tok0, bank):
            v = wview(slot)
            for k in range(KD):
                S.op("pe", lambda e, k=k: e.matmul(p_gu[bank][:], lhsT=v[:, k, coff:coff + 128], rhs=R2[:, k, tok0:tok0 + 512],
                                                   start=(k == 0), stop=(k == KD - 1)),
                     [("w", slot, coff // 128)] + tkeys("R2", tok0, 512), [("gu", bank)])

        for ti in range(16):
            load_norm_x(ti, ti * 128)
        S.barrier()
        S.op("dve", lambda e: e.memset(w_e0, 0.0), [], ["e0"])
        S.op("dve", lambda e: e.memset(w_e1, 0.0), [], ["e1"])

        def hgrn_front(h, tb, slot, main, p):
            tok0 = tb * 512
            cf, ci = (128, 256) if main else (0, 128)
            w_qd, w_kd, w_tm, w_gs = w_qdP[p], w_kdP[p], w_tmP[p], w_gsP[p]
            if main:
                bq = gu_bank()
                proj(slot, 0, tok0, bq)
                S.op("act", lambda e: e.activation(out=w_qs, in_=p_gu[bq][:], func=AF.Sigmoid), [("gu", bq)], ["qs"])
                S.op("dve", lambda e: e.tensor_tensor(out=w_qs, in0=p_gu[bq][:], in1=w_qs, op=ALU.mult), [("gu", bq), "qs"], ["qs"])
            bf = gu_bank()
            proj(slot, cf, tok0, bf)
            S.op("act", lambda e: e.activation(out=w_sgn, in_=p_gu[bf][:], func=AF.Sigmoid, scale=-1.0), [("gu", bf)], ["sgn"])
            if main:
                bg = gu_bank()
                proj(slot, 384, tok0, bg)
                S.op("act", lambda e: e.activation(out=w_gs, in_=p_gu[bg][:], func=AF.Sigmoid), [("gu", bg)], [("gs", p)])
                S.op("dve", lambda e: e.tensor_tensor(out=w_gs, in0=p_gu[bg][:], in1=w_gs, op=ALU.mult), [("gu", bg), ("gs", p)], [("gs", p)])
            S.op("dve", lambda e: e.tensor_scalar(out=w_f, in0=w_sgn, scalar1=nomlc[:, h:h + 1], scalar2=1.0, op0=ALU.mult, op1=ALU.add),
                 ["sgn", "nomlc"], ["f"])
            S.op("act", lambda e: e.activation(out=w_f, in_=w_f, func=AF.Ln), ["f"], ["f"])
            S.op("dve", lambda e: e.tensor_tensor_scan(out=w_b, data0=smask[:], data1=w_f, initial=0.0, op0=ALU.mult, op1=ALU.add),
                 ["f", "smask"], ["b"])
            bi = gu_bank()
            proj(slot, ci, tok0, bi)
            S.op("act", lambda e: e.copy(out=w_vi, in_=p_gu[bi][:]), [("gu", bi)], ["vi"])
            bview = w_b.rearrange("p (c t) -> p c t", t=64)
            S.op("act", lambda e: e.activation(out=w_eb8[:, p, :].unsqueeze(2), in_=bview[:, :, 63:64], func=AF.Exp), ["b"], [("eb8", p)])
            for c in range(8):
                dst, dk = (w_e0, "e0") if c % 2 == 0 else (w_e1, "e1")
                S.op("act", lambda e, c=c, dst=dst: e.activation(out=dst[:, c * 64:(c + 1) * 64], in_=w_b[:, c * 64:(c + 1) * 64], func=AF.Exp,
                                                                 scale=-1.0, bias=w_b[:, c * 64 + 63:c * 64 + 64]), ["b"], [dk])
            S.op("dve", lambda e: e.scalar_tensor_tensor(out=w_kk0, in0=w_sgn, scalar=omlc[:, h:h + 1], in1=w_e0, op0=ALU.mult, op1=ALU.mult),
                 ["sgn", "e0", "omlc"], ["kk0"])
            S.op("dve", lambda e: e.scalar_tensor_tensor(out=w_kk1, in0=w_sgn, scalar=omlc[:, h:h + 1], in1=w_e1, op0=ALU.mult, op1=ALU.mult),
                 ["sgn", "e1", "omlc"], ["kk1"])
            if main:
                S.op("act", lambda e: e.activation(out=w_e, in_=w_b, func=AF.Exp), ["b"], ["e"])
                S.op("dve", lambda e: e.tensor_tensor(out=w_qd, in0=w_qs, in1=w_e, op=ALU.mult), ["qs", "e"], [("qd", p)])
                S.op("act", lambda e: e.activation(out=w_e, in_=w_b, func=AF.Exp, scale=-1.0), ["b"], ["e"])
                S.op("dve", lambda e: e.scalar_tensor_tensor(out=w_kd, in0=w_sgn, scalar=omlc[:, h:h + 1], in1=w_e, op0=ALU.mult, op1=ALU.mult),
                     ["sgn", "e", "omlc"], [("kd", p)])

        def hgrn_front_b(h, tb, main, p):
            w_tm = w_tmP[p]
            for j in range(4):
                S.op("pe", lambda e, j=j: e.transpose(out=p_tr[:, j, :], in_=w_vi[:, j * 128:(j + 1) * 128], identity=ident[:]), ["vi", "ident"], ["ptr"])
            for j in range(4):
                S.op("pe", lambda e, j=j: e.transpose(out=p_tr[:, 4 + j, :], in_=w_kk0[:, j * 128:(j + 1) * 128], identity=ident[:]), ["kk0", "ident"], ["ptr"])
            S.op("dve", lambda e: e.tensor_copy(out=w_tm[:, 0:8, :], in_=p_tr[:]), ["ptr"], [("tmA", p)])
            for j in range(4):
                S.op("pe", lambda e, j=j: e.transpose(out=p_tr[:, j, :], in_=w_kk1[:, j * 128:(j + 1) * 128], identity=ident[:]), ["kk1", "ident"], ["ptr"])
            S.op("dve", lambda e: e.tensor_copy(out=w_tm[:, 8:12, :], in_=p_tr[:, 0:4, :]), ["ptr"], [("tmB", p)])

        def hgrn_back(h, tb, main, p):
            tok0 = tb * 512
            w_qd, w_kd, w_tm, w_gs = w_qdP[p], w_kdP[p], w_tmP[p], w_gsP[p]
            for c in range(8):
                j, r = c // 2, c % 2
                bank = 1 + c // 4
                S.op("pe", lambda e, c=c, j=j, r=r, bank=bank: e.matmul(
                    p_acc[bank][:, (c % 4) * 128:(c % 4) * 128 + 128], lhsT=w_tm[:, 4 + 4 * r + j, :], rhs=w_tm[:, j, :],
                    start=True, stop=True), [("tmA", p), ("tmB", p)], [("acc", bank)])
            if main:
                for j in range(4):
                    S.op("pe", lambda e, j=j: e.matmul(p_acc[0][:, j * 128:(j + 1) * 128], lhsT=w_kd[:, j * 128:(j + 1) * 128], rhs=w_qd[:, j * 128:(j + 1) * 128],
                                                       start=True, stop=True), [("kd", p), ("qd", p)], [("acc", 0)])
                S.op("dve", lambda e: e.tensor_tensor(out=w_am, in0=p_acc[0][:], in1=cmask[:], op=ALU.mult), [("acc", 0), "cmask"], ["am"])
            if main:
                S.op("pool", lambda e: e.tensor_copy(out=Sb[:, 0, :], in_=Sfin[:, h, :]), [("S", h)], [("Sb", 0)])
            for c in range(8):
                bank = 1 + c // 4
                src, skey = (Sfin[:, h, :], ("S", h)) if c == 0 else (Sall[:, c, :], ("Sall", c))
                dst, dkey = (Sfin[:, h, :], ("S", h)) if c == 7 else (Sall[:, c + 1, :], ("Sall", c + 1))
                S.op("dve", lambda e, c=c, bank=bank, src=src, dst=dst: e.scalar_tensor_tensor(
                    out=dst, in0=src, scalar=w_eb8[:, p, c:c + 1], in1=p_acc[bank][:, (c % 4) * 128:(c % 4) * 128 + 128],
                    op0=ALU.mult, op1=ALU.add), [skey, ("eb8", p), ("acc", bank)], [dkey])
                if main and c == 3:
                    S.op("act", lambda e: e.copy(out=Sb[:, 1:5, :], in_=Sall[:, 1:5, :]), [("Sall", i) for i in range(1, 5)], [("Sb", i) for i in range(1, 5)])
                if main and c == 6:
                    S.op("act", lambda e: e.copy(out=Sb[:, 5:8, :], in_=Sall[:, 5:8, :]), [("Sall", i) for i in range(5, 8)], [("Sb", i) for i in range(5, 8)])

        def hgrn_back_b(h, tb, main, p):
            tok0 = tb * 512
            w_qd, w_kd, w_tm, w_gs = w_qdP[p], w_kdP[p], w_tmP[p], w_gsP[p]
            if not main:
                return
            for j in range(4):
                S.op("pe", lambda e, j=j: e.matmul(p_acc[3][:, j * 128:(j + 1) * 128], lhsT=w_tm[:, j, :], rhs=w_am[:, j * 128:(j + 1) * 128],
                                                   start=True, stop=False), [("tmA", p), "am"], [("acc", 3)])
                for r in range(2):
                    c = 2 * j + r
                    S.op("pe", lambda e, c=c, r=r: e.matmul(p_acc[3][:, c * 64:(c + 1) * 64], lhsT=Sb[:, c, :], rhs=w_qd[:, c * 64:(c + 1) * 64],
                                                            start=False, stop=(r == 1)), [("Sb", c), ("qd", p)], [("acc", 3)])
            S.op("act", lambda e: e.activation(out=w_X, in_=p_acc[3][:], func=AF.Square), [("acc", 3)], ["X"])
            S.op("pe", lambda e: e.matmul(p_acc[0][:], lhsT=onesF[:], rhs=w_X, start=True, stop=True), ["X", "onesF"], [("acc", 0)])
            S.op("act", lambda e: e.activation(out=w_X, in_=p_acc[0][:], func=AF.Ln, scale=1.0 / 128, bias=EPS), [("acc", 0)], ["X"])
            S.op("act", lambda e: e.activation(out=w_X, in_=w_X, func=AF.Exp, scale=-0.5), ["X"], ["X"])
            S.op("dve", lambda e: e.scalar_tensor_tensor(out=w_X, in0=p_acc[3][:], scalar=ggrc[:, 0:1], in1=w_X, op0=ALU.mult, op1=ALU.mult),
                 [("acc", 3), "X", "ggrc"], ["X"])
            ms = (h * 4 + tb) % 2
            S.op("dve", lambda e: e.tensor_tensor(out=mst[:, ms, :], in0=w_X, in1=w_gs, op=ALU.mult), ["X", ("gs", p)], [("mst", ms)])
            dma("sp", md[8 + h, :, tok0:tok0 + 512], mst[:, ms, :], [("mst", ms)], [("md", 8 + h, tb)], ("mst", ms))

        def hist_kv(h, slot):
            bk = gu_bank()
            proj(slot, 256, 1536, bk)
            S.op("act", lambda e: e.copy(out=w_kk0, in_=p_gu[bk][:]), [("gu", bk)], ["kk0"])
            dma("sp", kvd[h, :, 0:512], w_kk0, ["kk0"], [("kvd", h)], "kvd_k")
            bv = gu_bank()
            proj(slot, 384, 1536, bv)
            S.op("act", lambda e: e.copy(out=w_kk1, in_=p_gu[bv][:]), [("gu", bv)], ["kk1"])
            for j in range(4):
                S.op("pe", lambda e, j=j: e.transpose(out=p_tr[:, j, :], in_=w_kk1[:, j * 128:(j + 1) * 128], identity=ident[:]), ["kk1", "ident"], ["ptr"])
            S.op("dve", lambda e: e.tensor_copy(out=w_vi.rearrange("p (a b) -> p a b", a=4), in_=p_tr[:, 0:4, :]), ["ptr"], ["vi"])
            dma("sp", kvd[h, :, 512:1024], w_vi, ["vi"], [("kvd", h)], "kvd_v")

        def hgrn_phase(main):
            slots = {}

            def load_w(h):
                slot = wslot()
                slots[h] = slot
                if main:
                    for gi in range(4):
                        wload_cols(slot, gi * 128, w_in, 3072 + gi * 1024 + h * 128, 128)
                else:
                    wload_cols(slot, 0, w_in, 3072 + 1024 + h * 128, 128)
                    wload_cols(slot, 128, w_in, 3072 + 2048 + h * 128, 128)
                    wload_cols(slot, 256, w_in, 1024 + h * 128, 128)
                    wload_cols(slot, 384, w_in, 2048 + h * 128, 128)
            units = [(h, tb) for h in range(8) for tb in range(4)]
            load_w(0)
            for n, (h, tb) in enumerate(units):
                if tb == 0 and h + 1 < 8:
                    load_w(h + 1)
                if n > 0:
                    ph, ptb = units[n - 1]
                    hgrn_back(ph, ptb, main, (n - 1) % 2)
                hgrn_front(h, tb, slots[h], main, n % 2)
                hgrn_front_b(h, tb, main, n % 2)
                if n > 0:
                    hgrn_back_b(ph, ptb, main, (n - 1) % 2)
                if (not main) and tb == 3:
                    hist_kv(h, slots[h])
            ph, ptb = units[-1]
            hgrn_back(ph, ptb, main, (len(units) - 1) % 2)
            hgrn_back_b(ph, ptb, main, (len(units) - 1) % 2)

        hgrn_phase(main=False)
        if debug and "S" in debug:
            dma("sp", dbg["S"], Sfin, [("S", h) for h in range(8)], [], "dbgS")
            finals.append(len(S.ops) - 1)
        S.barrier()

        build_tab()
        for ti in range(16):
            load_norm_x(16 + ti, ti * 128)
        S.barrier()
        S.op("dve", lambda e: e.memset(a_v, 1.0), [], ["a_v_init"])

        def attention_head(h):
            slot = wslot()
            wload_cols(slot, 0, w_in, h * 128, 128)
            wload_cols(slot, 128, w_in, 1024 + h * 128, 128)
            wload_cols(slot, 256, w_in, 2048 + h * 128, 128)
            dma("sp", a_k[:, 0:512], kvd[h, :, 0:512], [("kvd", h)], [("a_k", 0)], "akh")
            dma("sp", a_v[:, 0:4, 0:128], kvd[h, :, 512:1024].rearrange("p (a b) -> p a b", a=4), [("kvd", h), "a_v_init"], [("a_v", 0)], "avh")
            for tb in range(4):
                b = gu_bank()
                proj(slot, 0, tb * 512, b)
                S.op("act", lambda e, b=b, tb=tb: e.copy(out=a_q[:, tb * 512:(tb + 1) * 512], in_=p_gu[b][:]), [("gu", b)], [("a_q", tb)])
                b = gu_bank()
                proj(slot, 128, tb * 512, b)
                S.op("act", lambda e, b=b, tb=tb: e.copy(out=a_k[:, 512 + tb * 512:512 + (tb + 1) * 512], in_=p_gu[b][:]), [("gu", b)], [("a_k", 1 + tb)])
                b = gu_bank()
                proj(slot, 256, tb * 512, b)
                vs = tb % 2
                S.op("act", lambda e, b=b, vs=vs: e.copy(out=a_vT[:, vs, :], in_=p_gu[b][:]), [("gu", b)], [("a_vT", vs)])
                for j in range(4):
                    S.op("pe", lambda e, j=j, vs=vs: e.transpose(out=p_tr[:, j, :], in_=a_vT[:, vs, j * 128:(j + 1) * 128], identity=ident[:]),
                         [("a_vT", vs), "ident"], ["ptr"])
                S.op("dve", lambda e, tb=tb: e.tensor_copy(out=a_v[:, 4 + tb * 4:8 + tb * 4, 0:128], in_=p_tr[:, 0:4, :]), ["ptr", "a_v_init"], [("a_v", 1 + tb)])
            def sbanks(qt):
                if qt % 2 == 0:
                    return (p_acc[0], ("acc", 0)), (p_acc[1], ("acc", 1)), (p_acc[2], ("acc", 2))
                return (p_gu[0], ("gu", 0)), (p_gu[1], ("gu", 1)), (p_acc[3], ("acc", 3))

            def att_S(qt):
                (bA, kA), (bB, kB), _ = sbanks(qt)
                kreads = sorted(set(("a_k", (qt + kt) // 4) for kt in range(5)))
                for kt in range(5):
                    bank, bkey = (bA, kA) if kt < 4 else (bB, kB)
                    col = (kt % 4) * 128
                    S.op("pe", lambda e, kt=kt, bank=bank, col=col: e.matmul(
                        bank[:, col:col + 128], lhsT=a_k[:, (qt + kt) * 128:(qt + kt + 1) * 128], rhs=a_q[:, qt * 128:(qt + 1) * 128],
                        start=True, stop=False), kreads + [("a_q", qt // 4)], [bkey])
                    S.op("pe", lambda e, kt=kt, bank=bank, col=col: e.matmul(
                        bank[:, col:col + 128], lhsT=ident[:], rhs=tab[:, h, kt * 128:(kt + 1) * 128],
                        start=False, stop=True), [("tab", h), "ident"], [bkey])

            def att_rest(qt):
                pp = qt % 2
                (bA, kA), (bB, kB), (bP, kP) = sbanks(qt)
                vreads = sorted(set(("a_v", (qt + kt) // 4) for kt in range(5)))
                nh = max(0, min(4, 4 - qt))
                if nh > 0:
                    S.op("act", lambda e: e.activation(out=a_p[:, pp, 0:nh * 128], in_=bA[:, 0:nh * 128], func=AF.Exp, scale=SCALE, bias=hneg[:, 0:1]),
                         [kA, "hneg"], [("a_p", pp)])
                if nh < 4:
                    S.op("act", lambda e: e.activation(out=a_p[:, pp, nh * 128:512], in_=bA[:, nh * 128:512], func=AF.Exp, scale=SCALE),
                         [kA], [("a_p", pp)])
                S.op("act", lambda e: e.activation(out=a_p[:, pp, 512:640], in_=bB[:, 0:128], func=AF.Exp, scale=SCALE),
                     [kB], [("a_p", pp)])
                for kt in range(5):
                    S.op("pe", lambda e, kt=kt: e.matmul(bP[:, 0:129], lhsT=a_p[:, pp, kt * 128:(kt + 1) * 128], rhs=a_v[:, qt + kt, 0:129],
                                                         start=(kt == 0), stop=(kt == 4)), [("a_p", pp)] + vreads, [kP])
                S.op("dve", lambda e: e.reciprocal(out=a_rd[:, pp:pp + 1], in_=bP[:, 128:129]), [kP], [("a_rd", pp)])
                S.op("dve", lambda e: e.tensor_scalar(out=a_ab[:, pp, :], in0=bP[:, 0:128], scalar1=a_rd[:, pp:pp + 1], scalar2=None, op0=ALU.mult),
                     [kP, ("a_rd", pp)], [("a_ab", pp)])
                S.op("act", lambda e: e.activation(out=a_jk, in_=a_ab[:, pp, :], func=AF.Square, accum_out=ssa[:, qt, h:h + 1]),
                     [("a_ab", pp)], ["a_jk", ("ssa", qt, h)])
                S.op("pe", lambda e: e.transpose(out=p_tr[:, qt % 4, :], in_=a_ab[:, pp, :], identity=ident[:]), [("a_ab", pp), "ident"], ["ptr"])
                if qt % 4 == 3:
                    g = qt // 4
                    ms = (h * 4 + g) % 2
                    S.op("dve", lambda e: e.tensor_scalar(out=mst_a[:, ms, :].rearrange("p (a b) -> p a b", a=4), in0=p_tr[:, 0:4, :], scalar1=gatc[:, h:h + 1], scalar2=None, op0=ALU.mult),
                         ["ptr", "gatc"], [("mst", ms)])
                    dma("sp", md[h, :, g * 512:(g + 1) * 512], mst_a[:, ms, :], [("mst", ms)], [("md", h, g)], ("mst", ms))

            att_S(0)
            for qt_ in range(16):
                if qt_ + 1 < 16:
                    att_S(qt_ + 1)
                att_rest(qt_)


        if stop_after != "H":
            for h in range(int(os.environ.get('ATT_HEADS', '8'))):
                attention_head(h)
            S.barrier()
        if stop_after not in ("H", "att"):
            S.op("dve", lambda e: e.memset(w_e0, 0.0), [], ["e0"])
            S.op("dve", lambda e: e.memset(w_e1, 0.0), [], ["e1"])
            hgrn_phase(main=True)
            S.barrier()

        if stop_after is None:
            S.op("dve", lambda e: e.tensor_reduce(out=rsa[:], in_=ssa[:], axis=AX.X, op=ALU.add), [], ["rsa"])
            S.op("act", lambda e: e.activation(out=rsa[:], in_=rsa[:], func=AF.Sqrt, scale=1.0 / 1024, bias=EPS), ["rsa"], ["rsa"])
            S.op("dve", lambda e: e.reciprocal(out=rsa[:], in_=rsa[:]), ["rsa"], ["rsa"])
            dma("sp", mb, md[:, :, 0:TB].rearrange("k p t -> p k t"), [], ["mb"], "mb")
            for blk in range(NT // TB):
                t0 = blk * TB
                for tt in range(4):
                    dma("sp", x1[:, tt, :], xw[NHIST + t0 + tt * 128:NHIST + t0 + (tt + 1) * 128, :], [], [("x1", tt)], ("x1", tt))
                for dblk in range(4):
                    slot = wslot()
                    wload_cols(slot, 0, w_out, dblk * 512, 512)
                    wv = wview(slot)
                    for tt in range(4):
                        qt = blk * 4 + tt
                        bA, bR = (0, 1) if tt % 2 == 0 else (2, 3)
                        for k in range(16):
                            bank = bA if k < 8 else bR
                            S.op("pe", lambda e, k=k, tt=tt, bank=bank, wv=wv: e.matmul(
                                p_acc[bank][:], lhsT=mb[:, k, tt * 128:(tt + 1) * 128], rhs=wv[:, k, :], start=(k % 8 == 0), stop=(k % 8 == 7)),
                                ["mb"] + wkeys(slot), [("acc", bank)])
                        S.op("dve", lambda e, tt=tt, qt=qt, bA=bA, dblk=dblk: e.scalar_tensor_tensor(
                            out=x1[:, tt, dblk * 512:(dblk + 1) * 512], in0=p_acc[bA][:], scalar=rsa[:, qt:qt + 1], in1=x1[:, tt, dblk * 512:(dblk + 1) * 512],
                            op0=ALU.mult, op1=ALU.add), [("acc", bA), "rsa", ("x1", tt)], [("x1", tt)])
                        S.op("dve", lambda e, tt=tt, bR=bR, dblk=dblk: e.tensor_tensor(
                            out=x1[:, tt, dblk * 512:(dblk + 1) * 512], in0=x1[:, tt, dblk * 512:(dblk + 1) * 512], in1=p_acc[bR][:], op=ALU.add),
                            [("acc", bR), ("x1", tt)], [("x1", tt)])
                if blk + 1 < NT // TB:
                    dma("sp", mb, md[:, :, t0 + TB:t0 + 2 * TB].rearrange("k p t -> p k t"), [], ["mb"], "mb")
                if debug and "x1" in debug:
                    for tt in range(4):
                        dma("sp", dbg["x1"][t0 + tt * 128:t0 + (tt + 1) * 128, :], x1[:, tt, :], [("x1", tt)], [], ("dbgx1", tt))
                        finals.append(len(S.ops) - 1)
                for tt in range(4):
                    norm_to_T(x1[:, tt, :], [("x1", tt)], h2T, "h2T", tt * 128, g2c, "g2c")
                h2keys = tkeys("h2T", 0, 512)
                for fp in range(NFT // 2):
                    slot = wslot()
                    wload_cols(slot, 0, w_gate, fp * 256, 256)
                    wload_cols(slot, 256, w_up, fp * 256, 256)
                    wv = wview(slot)
                    for fi in range(2):
                        ft = fp * 2 + fi
                        bg = gu_bank()
                        for k in range(16):
                            S.op("pe", lambda e, k=k, fi=fi, bg=bg, wv=wv: e.matmul(p_gu[bg][:], lhsT=wv[:, k, fi * 128:(fi + 1) * 128], rhs=h2T[:, k, :],
                                                                                  start=(k == 0), stop=(k == 15)), [("w", slot, fi)] + h2keys, [("gu", bg)])
                        bu = gu_bank()
                        for k in range(16):
                            S.op("pe", lambda e, k=k, fi=fi, bu=bu, wv=wv: e.matmul(p_gu[bu][:], lhsT=wv[:, k, 256 + fi * 128:256 + (fi + 1) * 128], rhs=h2T[:, k, :],
                                                                                  start=(k == 0), stop=(k == 15)), [("w", slot, 2 + fi)] + h2keys, [("gu", bu)])
                        sl = ft % 2
                        S.op("act", lambda e, bg=bg, sl=sl: e.activation(out=silb[:, sl, :], in_=p_gu[bg][:], func=AF.Silu), [("gu", bg)], [("sil", sl)])
                        S.op("dve", lambda e, bu=bu, sl=sl, ft=ft: e.tensor_tensor(out=ffT[:, ft, :], in0=p_gu[bu][:], in1=silb[:, sl, :], op=ALU.mult),
                             [("gu", bu), ("sil", sl)], [("ffT", ft)])
                for dblk in range(4):
                    for fg, (f0, nf) in enumerate(((0, 16), (16, 16), (32, 12))):
                        slot = wslot()
                        wv = wview(slot)
                        dma("pool", wv[:, 0:nf, :], w_down[f0 * 128:(f0 + nf) * 128, dblk * 512:(dblk + 1) * 512].rearrange("(k p) c -> p k c", p=128),
                            [], wkeys(slot), ("w", slot, 0))
                        for fl in range(nf):
                            fc = f0 + fl
                            for tt in range(4):
                                S.op("pe", lambda e, fl=fl, fc=fc, tt=tt, wv=wv: e.matmul(p_acc[tt][:], lhsT=ffT[:, fc, tt * 128:(tt + 1) * 128], rhs=wv[:, fl, :],
                                                                                        start=(fc == 0), stop=(fc == NFT - 1)), wkeys(slot) + [("ffT", fc)], [("acc", tt)])
                    for tt in range(4):
                        S.op("dve", lambda e, tt=tt, dblk=dblk: e.tensor_tensor(
                            out=x1[:, tt, dblk * 512:(dblk + 1) * 512], in0=x1[:, tt, dblk * 512:(dblk + 1) * 512], in1=p_acc[tt][:], op=ALU.add),
                            [("acc", tt), ("x1", tt)], [("x1", tt)])
                for tt in range(4):
                    i = 32 + (blk * 4 + tt) % 32
                    S.op("act", lambda e, tt=tt, i=i: e.activation(out=junk[:], in_=x1[:, tt, :], func=AF.Square, accum_out=ss[:, i:i + 1]),
                         [("x1", tt)], ["junk", ("ss", i)])
                    S.op("act", lambda e, i=i: e.activation(out=sd[:, i:i + 1], in_=ss[:, i:i + 1], func=AF.Sqrt, scale=1.0 / D, bias=EPS), [("ss", i)], [("sd", i)])
                    S.op("dve", lambda e, i=i: e.reciprocal(out=rstd[:, i:i + 1], in_=sd[:, i:i + 1]), [("sd", i)], [("rstd", i)])
                    S.op("dve", lambda e, tt=tt, i=i: e.scalar_tensor_tensor(out=yo, in0=x1[:, tt, :], scalar=rstd[:, i:i + 1], in1=gfb[:],
                                                                           op0=ALU.mult, op1=ALU.mult), [("x1", tt), ("rstd", i), "gfb"], ["yo"])
                    dma("sp", y[t0 + tt * 128:t0 + (tt + 1) * 128, :], yo, ["yo"], [], "yo")
                    finals.append(len(S.ops) - 1)

        if debug and "md" in debug:
            S.barrier()
            for m in range(16):
                for g in range(4):
                    dma("sp", hb[:, 0:512], md[m, :, g * 512:(g + 1) * 512], [], ["hb"], "hbd")
                    S.op("dve", lambda e: e.tensor_copy(out=gfb[:, 0:512], in_=hb[:, 0:512]), ["hb"], ["gfb"])
                    dma("sp", dbg["md"][m, :, g * 512:(g + 1) * 512], gfb[:, 0:512], ["gfb"], [], "gfbd")
                    finals.append(len(S.ops) - 1)
        stats = S.emit(final_waits=finals)
    return nc, stats


def _consts():
    ident = np.eye(128, dtype=np.float32)
    kk = np.arange(128)[:, None]
    amask = np.zeros((128, 640), np.float32)
    iq = np.arange(128)[None, :]
    amask[:, 0:128] = np.where((iq >= 64) & (kk < 64), -1e5, 0.0)
    amask[:, 512:640] = np.where((iq < 64) & (kk >= 64), -1e5, 0.0)
    s = np.arange(128)[:, None]
    t = np.arange(128)[None, :]
    tri = ((s // 64 == t // 64) & (s <= t)).astype(np.float32)
    cmask = np.tile(tri, (1, 4))
    smask = np.ones((128, 512), np.float32)
    smask[:, 0::64] = 0.0
    return ident, amask, cmask, smask


def _bias_index():
    kt = np.arange(5)[None, :, None]
    kk = np.arange(128)[:, None, None]
    iq = np.arange(128)[None, None, :]
    dist = 512 - 128 * kt + iq - kk
    return (np.clip(dist, -128, 128) + 128).reshape(128, 640)


_PROG = {}


def kernel(x, norm1_gain, w_in, rel_bias, lower_bounds, grn_norm_gain, attn_out_gain, w_out, norm2_gain,
           w_gate, w_up, w_down, final_gain):
    x = np.asarray(x, np.float32)
    if "nc" not in _PROG:
        _PROG["nc"] = build_program()[0]
    nc = _PROG["nc"]
    ident, amask, cmask, smask = _consts()
    idx = _bias_index()
    btab = np.ascontiguousarray(np.asarray(rel_bias, np.float32)[0][:, idx])
    shared = {
        "w_in": np.ascontiguousarray(np.asarray(w_in, np.float32)[0]),
        "w_out": np.ascontiguousarray(np.asarray(w_out, np.float32)[0]),
        "w_gate": np.ascontiguousarray(np.asarray(w_gate, np.float32)[0]),
        "w_up": np.ascontiguousarray(np.asarray(w_up, np.float32)[0]),
        "w_down": np.ascontiguousarray(np.asarray(w_down, np.float32)[0]),
        "g1": np.asarray(norm1_gain, np.float32).reshape(1, D),
        "g2": np.asarray(norm2_gain, np.float32).reshape(1, D),
        "gf": np.asarray(final_gain, np.float32).reshape(1, D),
        "gat": np.asarray(attn_out_gain, np.float32).reshape(1, 1024),
        "ggr": np.asarray(grn_norm_gain, np.float32).reshape(1, 128),
        "lbnd": np.ascontiguousarray(np.asarray(lower_bounds, np.float32)),
        "btab": btab, "c_ident": ident, "c_amask": amask, "c_cmask": cmask, "c_smask": smask,
    }
    in_maps = []
    for c in range(8):
        b, half = c // 2, c % 2
        xwc = np.zeros((NHIST + NT, D), np.float32)
        if half == 1:
            xwc[:] = x[b]
        else:
            xwc[NHIST:] = x[b, :NT]
        m = dict(shared)
        m["xw"] = xwc
        m["c_hneg"] = np.full((128, 1), 0.0 if half == 1 else -30000.0, np.float32)
        in_maps.append(m)
    res = run_bass_kernel_spmd(nc, in_maps, core_ids=list(range(8)))
    out = np.empty((4, 4096, D), np.float32)
    for c in range(8):
        b, half = c // 2, c % 2
        out[b, half * NT:(half + 1) * NT] = res.results[c]["y"]
    return out
```

```python
import contextlib
import os
import numpy as np
import ml_dtypes
import concourse.bass as bass
import concourse.mybir as mybir
from concourse.bass_utils import run_bass_kernel_spmd

F32 = mybir.dt.float32
BF16 = mybir.dt.bfloat16
AF = mybir.ActivationFunctionType
ALU = mybir.AluOpType
AX = mybir.AxisListType

ENGS = ("pe", "act", "dve", "pool", "sp")

D = 2048
NT = 2048
NHIST = 2048
KD = 16
DFF = 5632
NFT = 44
EPS = 1e-6
INCOLS = 7168
TB = 512
SCALE = 128 ** -0.5
SQ128 = 128 ** 0.5


class Sched:
    def __init__(self, nc, same_engine_sync=True):
        self.nc = nc
        self.ops = []
        self.last_writer = {}
        self.readers = {}
        self.same_engine_sync = same_engine_sync
        self.dma_keys = {}

    def op(self, eng, fn, reads=(), writes=(), dma_key=None):
        idx = len(self.ops)
        deps = set()
        for k in reads:
            w = self.last_writer.get(k)
            if w is not None:
                deps.add(w)
        for k in writes:
            w = self.last_writer.get(k)
            if w is not None:
                deps.add(w)
            for r in self.readers.get(k, ()):
                deps.add(r)
        if dma_key is not None:
            prev = self.dma_keys.get(dma_key)
            if prev is not None:
                deps.add(prev)
            self.dma_keys[dma_key] = idx
        deps.discard(idx)
        self.ops.append(dict(eng=eng, fn=fn, deps=deps, dma_key=dma_key, needed=False))
        for k in writes:
            self.last_writer[k] = idx
            self.readers[k] = []
        for k in reads:
            if k not in writes:
                lst = self.readers.setdefault(k, [])
                if dma_key is None:
                    lst[:] = [r for r in lst if not (self.ops[r]["eng"] == eng and self.ops[r]["dma_key"] is None)]
                lst.append(idx)
        return idx

    def barrier(self):
        last = {}
        for i, o in enumerate(self.ops):
            if o["fn"] is None:
                continue
            last[("e", o["eng"])] = i
            if o["dma_key"] is not None:
                last[("d", o["dma_key"])] = i
        deps = set(last.values())
        for e in ENGS:
            self.ops.append(dict(eng=e, fn=None, deps=set(deps), dma_key=None, needed=False))

    def emit(self, final_waits=()):
        nc = self.nc
        ops = self.ops
        for i, o in enumerate(ops):
            nd = set()
            for d in o["deps"]:
                p = ops[d]
                if p["dma_key"] is None and p["eng"] == o["eng"] and not self.same_engine_sync:
                    continue
                if p["dma_key"] is None and p["eng"] == "pe" and o["eng"] == "pe":
                    continue
                nd.add(d)
            o["deps"] = nd
            for d in nd:
                ops[d]["needed"] = True
        for d in final_waits:
            ops[d]["needed"] = True
        tick = {e: 0 for e in ENGS}
        dma_cnt = {}
        for o in ops:
            if o["dma_key"] is not None:
                dma_cnt[o["dma_key"]] = dma_cnt.get(o["dma_key"], 0) + 1
                o["sem"] = ("dma", o["dma_key"])
                o["tick"] = 16 * dma_cnt[o["dma_key"]]
            elif o["needed"]:
                tick[o["eng"]] += 1
                o["sem"] = ("eng", o["eng"])
                o["tick"] = tick[o["eng"]]
        sem_names = [("eng", e) for e in ENGS] + [("dma", k) for k in dma_cnt]
        with contextlib.ExitStack() as st:
            sems = {}
            for i, sn in enumerate(sem_names):
                sems[sn] = st.enter_context(nc.semaphore("s%d" % i))
            block = st.enter_context(nc.Block())
            per_eng = {e: [] for e in ENGS}
            for i, o in enumerate(ops):
                per_eng[o["eng"]].append(i)

            def body(ename, engine):
                waited = {}
                for i in per_eng[ename]:
                    o = ops[i]
                    need = {}
                    for d in o["deps"]:
                        p = ops[d]
                        s = p["sem"]
                        need[s] = max(need.get(s, 0), p["tick"])
                    for s, v in need.items():
                        if waited.get(s, 0) >= v:
                            continue
                        engine.wait_ge(sems[s], v)
                        waited[s] = v
                    if o["fn"] is None:
                        continue
                    ins = o["fn"](engine)
                    if o["dma_key"] is not None:
                        ins.then_inc(sems[o["sem"]], 16)
                    elif o["needed"]:
                        ins.then_inc(sems[o["sem"]], 1)
                if ename == "sp":
                    need = {}
                    for d in final_waits:
                        p = ops[d]
                        need[p["sem"]] = max(need.get(p["sem"], 0), p["tick"])
                    for s, v in need.items():
                        engine.wait_ge(sems[s], v)

            @block.tensor
            def _(e):
                body("pe", e)

            @block.scalar
            def _(e):
                body("act", e)

            @block.vector
            def _(e):
                body("dve", e)

            @block.gpsimd
            def _(e):
                body("pool", e)

            @block.sync
            def _(e):
                body("sp", e)
        return {e: len(per_eng[e]) for e in ENGS}, tick, len(sem_names)


def build_program(debug=None, stop_after=None):
    nc = bass.Bass("TRN2", target_bir_lowering=False)
    dt_in = lambda name, shape: nc.dram_tensor(name, shape, F32, kind="ExternalInput").ap()
    xw = dt_in("xw", [NHIST + NT, D])
    w_in = dt_in("w_in", [D, INCOLS])
    w_out = dt_in("w_out", [D, D])
    w_gate = dt_in("w_gate", [D, DFF])
    w_up = dt_in("w_up", [D, DFF])
    w_down = dt_in("w_down", [DFF, D])
    g1_d = dt_in("g1", [1, D])
    g2_d = dt_in("g2", [1, D])
    gf_d = dt_in("gf", [1, D])
    gat_d = dt_in("gat", [1, 1024])
    ggr_d = dt_in("ggr", [1, 128])
    lbnd_d = dt_in("lbnd", [2, 1024])
    btab_d = dt_in("btab", [8, 128, 640])
    cid_d = dt_in("c_ident", [128, 128])
    cam_d = dt_in("c_amask", [128, 640])
    ccm_d = dt_in("c_cmask", [128, 512])
    csm_d = dt_in("c_smask", [128, 512])
    chn_d = dt_in("c_hneg", [128, 1])
    y = nc.dram_tensor("y", [NT, D], F32, kind="ExternalOutput").ap()
    md = nc.dram_tensor("md", [16, 128, NT], BF16, kind="Internal").ap()
    kvd = nc.dram_tensor("kvd", [8, 128, 1024], BF16, kind="Internal").ap()
    dbg = {}
    if debug:
        for name, shape in debug.items():
            dbg[name] = nc.dram_tensor("dbg_" + name, shape, F32, kind="ExternalOutput").ap()

    S = Sched(nc)
    finals = []
    with contextlib.ExitStack() as st:
        def sb(name, shape, dt=F32):
            return st.enter_context(nc.sbuf_tensor(name, shape, dt))

        def ps(name, shape, dt=F32):
            return st.enter_context(nc.psum_tensor(name, shape, dt))

        R2 = sb("R2", [128, 16, 2048], BF16)
        R1 = sb("R1", [128, 28672], BF16)
        NSLOT = 3
        wpool = sb("wpool", [128, NSLOT, 8192], BF16)
        ident = sb("ident", [128, 128], BF16)
        identF = sb("identF", [128, 128], F32)
        onesF = sb("onesF", [128, 128], F32)
        onesB = sb("onesB", [128, 128], BF16)
        amask = sb("amask", [128, 640], F32)
        cmask = sb("cmask", [128, 512], F32)
        smask = sb("smask", [128, 512], F32)
        hneg = sb("hneg", [128, 1], F32)
        g1c = sb("g1c", [128, 16], F32)
        g2c = sb("g2c", [128, 16], F32)
        gfb = sb("gfb", [128, 2048], F32)
        gatc = sb("gatc", [128, 8], F32)
        ggrc = sb("ggrc", [128, 1], F32)
        lbc = sb("lbc", [128, 8], F32)
        lb1 = sb("lb1", [128, 8], F32)
        omlc = sb("omlc", [128, 8], F32)
        nomlc = sb("nomlc", [128, 8], F32)
        junk = sb("junk", [128, 2048], BF16)
        hb = sb("hb", [128, 2048], BF16)
        silb = sb("silb", [128, 2, 512], F32)
        ss = sb("ss", [128, 64], F32)
        sd = sb("sd", [128, 64], F32)
        rstd = sb("rstd", [128, 64], F32)
        ssa = sb("ssa", [128, 16, 8], F32)
        rsa = sb("rsa", [128, 16], F32)

        def carve(off, shape, dt):
            n = 1
            for s_ in shape[1:]:
                n *= s_
            esz = 2 if dt == BF16 else 4
            a = R1[:, off // 2: off // 2 + n * esz // 2]
            if dt == F32:
                a = a.bitcast(F32)
            if len(shape) == 3:
                a = a.rearrange("p (a b) -> p a b", a=shape[1])
            elif len(shape) == 4:
                a = a.rearrange("p (a b c) -> p a b c", a=shape[1], b=shape[2])
            return a

        xin = carve(0, [128, 2048], F32)
        Sfin = carve(8192, [128, 8, 128], F32)
        Sall = carve(12288, [128, 8, 128], F32)
        Sb = carve(16384, [128, 8, 128], BF16)
        tab = carve(18432, [128, 8, 640], BF16)
        WK = 28672
        KB2 = 2048
        w_sgn = carve(WK + 0 * KB2, [128, 512], F32)
        w_f = carve(WK + 1 * KB2, [128, 512], F32)
        w_b = carve(WK + 2 * KB2, [128, 512], F32)
        w_e0 = carve(WK + 3 * KB2, [128, 512], F32)
        w_e1 = carve(WK + 4 * KB2, [128, 512], F32)
        w_e = carve(WK + 5 * KB2, [128, 512], F32)
        w_qs = carve(WK + 6 * KB2, [128, 512], F32)
        w_gs0 = carve(WK + 7 * KB2, [128, 512], F32)
        o = WK + 8 * KB2
        w_qd0 = carve(o, [128, 512], BF16); o += 1024
        w_kd0 = carve(o, [128, 512], BF16); o += 1024
        w_kk0 = carve(o, [128, 512], BF16); o += 1024
        w_kk1 = carve(o, [128, 512], BF16); o += 1024
        w_vi = carve(o, [128, 512], BF16); o += 1024
        w_tm0 = carve(o, [128, 12, 128], BF16); o += 3072
        w_am = carve(o, [128, 512], BF16); o += 1024
        mst = carve(o, [128, 2, 512], BF16); o += 2048
        w_eb8 = carve(o, [128, 2, 8], F32); o += 64
        assert o <= 57344, o
        o = 18432
        w_gs1 = carve(o, [128, 512], F32); o += 2048
        w_X = carve(o, [128, 512], F32); o += 2048
        w_qd1 = carve(o, [128, 512], BF16); o += 1024
        w_kd1 = carve(o, [128, 512], BF16); o += 1024
        w_tm1 = carve(o, [128, 12, 128], BF16); o += 3072
        assert o <= 28672, o
        w_gsP = (w_gs0, w_gs1)
        w_qdP = (w_qd0, w_qd1)
        w_kdP = (w_kd0, w_kd1)
        w_tmP = (w_tm0, w_tm1)
        o = WK
        a_q = carve(o, [128, 2048], BF16); o += 4096
        a_k = carve(o, [128, 2560], BF16); o += 5120
        a_vT = carve(o, [128, 2, 512], BF16); o += 2048
        a_v = carve(o, [128, 20, 132], BF16); o += 5280
        a_p = carve(o, [128, 2, 640], BF16); o += 2560
        a_ab = carve(o, [128, 2, 128], BF16); o += 512
        a_jk = carve(o, [128, 128], BF16); o += 256
        a_rd = carve(o, [128, 2], F32); o += 8
        o = (o + 63) // 64 * 64
        mst_a = carve(o, [128, 2, 512], BF16); o += 2048
        assert o <= 57344, o
        mb = carve(0, [128, 16, 512], BF16)
        x1 = carve(16384, [128, 4, 2048], F32)
        yo = carve(49152, [128, 2048], F32)
        r2f = R2[:].rearrange("p k t -> p (k t)")
        ffT = r2f[:, 0:22528].rearrange("p (f t) -> p f t", f=NFT)
        h2T = r2f[:, 22528:30720].rearrange("p (k t) -> p k t", k=16)

        p_acc = [ps("pacc%d" % i, [128, 512]) for i in range(4)]
        p_gu = [ps("pgu%d" % i, [128, 512]) for i in range(3)]
        p_tr = ps("ptr", [128, 8, 128], BF16)

        def dma(eng, out, in_, reads, writes, key, slow=False):
            if slow:
                return S.op(eng, lambda e: e.dma_start(out=out, in_=in_, allow_slow_non_contiguous=True), reads=reads, writes=writes, dma_key=key)
            return S.op(eng, lambda e: e.dma_start(out=out, in_=in_), reads=reads, writes=writes, dma_key=key)

        dma("sp", identF[:], cid_d[:, :], [], ["identF"], "c0")
        dma("sp", amask[:], cam_d[:, :], [], ["amask"], "c1")
        dma("sp", cmask[:], ccm_d[:, :], [], ["cmask"], "c2")
        dma("sp", smask[:], csm_d[:, :], [], ["smask"], "c3")
        dma("sp", hneg[:], chn_d[:, :], [], ["hneg"], "c4")
        dma("sp", gfb[:], gf_d.broadcast_to([128, D]), [], ["gfb"], "c5")
        cvec = lambda d_: d_.rearrange("o (k p) -> p (o k)", p=128)
        dma("sp", g1c[:], cvec(g1_d), [], ["g1c"], "c6", slow=True)
        dma("sp", g2c[:], cvec(g2_d), [], ["g2c"], "c7", slow=True)
        dma("sp", gatc[:], cvec(gat_d), [], ["gatc"], "c8", slow=True)
        dma("sp", ggrc[:], cvec(ggr_d), [], ["ggrc"], "c9", slow=True)
        dma("sp", lbc[:], cvec(lbnd_d[0:1, :]), [], ["lbc"], "c10", slow=True)
        dma("sp", lb1[:], cvec(lbnd_d[1:2, :]), [], ["lb1"], "c11", slow=True)
        S.op("dve", lambda e: e.tensor_copy(out=ident[:], in_=identF[:]), ["identF"], ["ident"])
        S.op("dve", lambda e: e.memset(onesF[:], 1.0), [], ["onesF"])
        S.op("dve", lambda e: e.memset(onesB[:], 1.0), [], ["onesB"])
        S.op("dve", lambda e: e.memset(Sfin, 0.0), [], [("S", h) for h in range(8)])
        S.op("dve", lambda e: e.tensor_tensor(out=lbc[:], in0=lbc[:], in1=lb1[:], op=ALU.subtract), ["lbc", "lb1"], ["lbc"])
        S.op("act", lambda e: e.activation(out=lbc[:], in_=lbc[:], func=AF.Sigmoid), ["lbc"], ["lbc"])
        S.op("dve", lambda e: e.tensor_scalar(out=omlc[:], in0=lbc[:], scalar1=-1.0, scalar2=1.0, op0=ALU.mult, op1=ALU.add), ["lbc"], ["omlc"])
        S.op("dve", lambda e: e.tensor_scalar(out=nomlc[:], in0=lbc[:], scalar1=1.0, scalar2=-1.0, op0=ALU.mult, op1=ALU.add), ["lbc"], ["nomlc"])
        def build_tab():
            for h in range(8):
                dma("sp", xin[:, 0:640], btab_d[h], [], ["xin"], "xin")
                S.op("dve", lambda e, h=h: e.scalar_tensor_tensor(out=tab[:, h, :], in0=xin[:, 0:640], scalar=SQ128, in1=amask[:],
                                                                  op0=ALU.mult, op1=ALU.add), ["xin", "amask"], [("tab", h)])

        wstate = {"n": 0}

        def wslot():
            s = wstate["n"] % NSLOT
            wstate["n"] += 1
            return s

        def wview(slot):
            return wpool[:, slot, :].rearrange("p (k c) -> p k c", k=16)

        def wkeys(slot, q0=0, q1=4):
            return [("w", slot, q) for q in range(q0, q1)]

        def wload_cols(slot, dst_off, src, col0, ncols):
            q0, q1 = dst_off // 128, (dst_off + ncols) // 128
            dma("pool", wview(slot)[:, :, dst_off:dst_off + ncols],
                src[:, col0:col0 + ncols].rearrange("(k p) c -> p k c", p=128),
                [], wkeys(slot, q0, q1), ("w", slot, q0))

        ncount = {"n": 0}

        def norm_to_T(src_ap, src_reads, dstT, dname, col0, gain_c, gkey, hbuf=None, hkey="hb"):
            hbv = hb[:] if hbuf is None else hbuf
            i = ncount["n"] % 32
            ncount["n"] += 1
            S.op("act", lambda e: e.activation(out=junk[:], in_=src_ap, func=AF.Square, accum_out=ss[:, i:i + 1]),
                 src_reads, ["junk", ("ss", i)])
            S.op("act", lambda e: e.activation(out=sd[:, i:i + 1], in_=ss[:, i:i + 1], func=AF.Sqrt, scale=1.0 / D, bias=EPS),
                 [("ss", i)], [("sd", i)])
            S.op("dve", lambda e: e.reciprocal(out=rstd[:, i:i + 1], in_=sd[:, i:i + 1]), [("sd", i)], [("rstd", i)])
            S.op("dve", lambda e: e.tensor_scalar(out=hbv, in0=src_ap, scalar1=rstd[:, i:i + 1], scalar2=None, op0=ALU.mult),
                 src_reads + [("rstd", i)], [hkey])
            for half in range(2):
                for j in range(8):
                    k = half * 8 + j
                    S.op("pe", lambda e, k=k, j=j: e.transpose(out=p_tr[:, j, :], in_=hbv[:, k * 128:(k + 1) * 128], identity=ident[:]),
                         [hkey, "ident"], ["ptr"])
                S.op("dve", lambda e, half=half: e.tensor_tensor(
                    out=dstT[:, half * 8:half * 8 + 8, col0:col0 + 128], in0=p_tr[:],
                    in1=gain_c[:, half * 8:half * 8 + 8].unsqueeze(2).broadcast_to([128, 8, 128]), op=ALU.mult),
                    ["ptr", gkey], [("T", dname, col0, half)])

        def tkeys(dname, tok0, ntok):
            return [("T", dname, c, half) for c in range(tok0, tok0 + ntok, 128) for half in range(2)]

        xin2 = carve(WK, [128, 2048], F32)
        hb2 = carve(WK + 8192, [128, 2048], BF16)

        def load_norm_x(ti, col0):
            if ti % 2 == 0:
                dma("sp", xin, xw[ti * 128:(ti + 1) * 128, :], [], ["xin"], "xin")
                norm_to_T(xin, ["xin"], R2, "R2", col0, g1c, "g1c")
            else:
                dma("sp", xin2, xw[ti * 128:(ti + 1) * 128, :], [], ["xin2"], "xin2")
                norm_to_T(xin2, ["xin2"], R2, "R2", col0, g1c, "g1c", hbuf=hb2, hkey="hb2")

        gu_rr = {"n": 0}

        def gu_bank():
            b = gu_rr["n"] % 3
            gu_rr["n"] += 1
            return b

        def proj(slot, coff, tok0, bank):
            v = wview(slot)
            for k in range(KD):
                S.op("pe", lambda e, k=k: e.matmul(p_gu[bank][:], lhsT=v[:, k, coff:coff + 128], rhs=R2[:, k, tok0:tok0 + 512],
                                                   start=(k == 0), stop=(k == KD - 1)),
                     [("w", slot, coff // 128)] + tkeys("R2", tok0, 512), [("gu", bank)])

        for ti in range(16):
            load_norm_x(ti, ti * 128)
        S.barrier()
        S.op("dve", lambda e: e.memset(w_e0, 0.0), [], ["e0"])
        S.op("dve", lambda e: e.memset(w_e1, 0.0), [], ["e1"])

        def hgrn_front(h, tb, slot, main, p):
            tok0 = tb * 512
            cf, ci = (128, 256) if main else (0, 128)
            w_qd, w_kd, w_tm, w_gs = w_qdP[p], w_kdP[p], w_tmP[p], w_gsP[p]
            if main:
                bq = gu_bank()
                proj(slot, 0, tok0, bq)
                S.op("act", lambda e: e.activation(out=w_qs, in_=p_gu[bq][:], func=AF.Sigmoid), [("gu", bq)], ["qs"])
                S.op("dve", lambda e: e.tensor_tensor(out=w_qs, in0=p_gu[bq][:], in1=w_qs, op=ALU.mult), [("gu", bq), "qs"], ["qs"])
            bf = gu_bank()
            proj(slot, cf, tok0, bf)
            S.op("act", lambda e: e.activation(out=w_sgn, in_=p_gu[bf][:], func=AF.Sigmoid, scale=-1.0), [("gu", bf)], ["sgn"])
            if main:
                bg = gu_bank()
                proj(slot, 384, tok0, bg)
                S.op("act", lambda e: e.activation(out=w_gs, in_=p_gu[bg][:], func=AF.Sigmoid), [("gu", bg)], [("gs", p)])
                S.op("dve", lambda e: e.tensor_tensor(out=w_gs, in0=p_gu[bg][:], in1=w_gs, op=ALU.mult), [("gu", bg), ("gs", p)], [("gs", p)])
            S.op("dve", lambda e: e.tensor_scalar(out=w_f, in0=w_sgn, scalar1=nomlc[:, h:h + 1], scalar2=1.0, op0=ALU.mult, op1=ALU.add),
                 ["sgn", "nomlc"], ["f"])
            S.op("act", lambda e: e.activation(out=w_f, in_=w_f, func=AF.Ln), ["f"], ["f"])
            S.op("dve", lambda e: e.tensor_tensor_scan(out=w_b, data0=smask[:], data1=w_f, initial=0.0, op0=ALU.mult, op1=ALU.add),
                 ["f", "smask"], ["b"])
            bi = gu_bank()
            proj(slot, ci, tok0, bi)
            S.op("act", lambda e: e.copy(out=w_vi, in_=p_gu[bi][:]), [("gu", bi)], ["vi"])
            bview = w_b.rearrange("p (c t) -> p c t", t=64)
            S.op("act", lambda e: e.activation(out=w_eb8[:, p, :].unsqueeze(2), in_=bview[:, :, 63:64], func=AF.Exp), ["b"], [("eb8", p)])
            for c in range(8):
                dst, dk = (w_e0, "e0") if c % 2 == 0 else (w_e1, "e1")
                S.op("act", lambda e, c=c, dst=dst: e.activation(out=dst[:, c * 64:(c + 1) * 64], in_=w_b[:, c * 64:(c + 1) * 64], func=AF.Exp,
                                                                 scale=-1.0, bias=w_b[:, c * 64 + 63:c * 64 + 64]), ["b"], [dk])
            S.op("dve", lambda e: e.scalar_tensor_tensor(out=w_kk0, in0=w_sgn, scalar=omlc[:, h:h + 1], in1=w_e0, op0=ALU.mult, op1=ALU.mult),
                 ["sgn", "e0", "omlc"], ["kk0"])
            S.op("dve", lambda e: e.scalar_tensor_tensor(out=w_kk1, in0=w_sgn, scalar=omlc[:, h:h + 1], in1=w_e1, op0=ALU.mult, op1=ALU.mult),
                 ["sgn", "e1", "omlc"], ["kk1"])
            if main:
                S.op("act", lambda e: e.activation(out=w_e, in_=w_b, func=AF.Exp), ["b"], ["e"])
                S.op("dve", lambda e: e.tensor_tensor(out=w_qd, in0=w_qs, in1=w_e, op=ALU.mult), ["qs", "e"], [("qd", p)])
                S.op("act", lambda e: e.activation(out=w_e, in_=w_b, func=AF.Exp, scale=-1.0), ["b"], ["e"])
                S.op("dve", lambda e: e.scalar_tensor_tensor(out=w_kd, in0=w_sgn, scalar=omlc[:, h:h + 1], in1=w_e, op0=ALU.mult, op1=ALU.mult),
                     ["sgn", "e", "omlc"], [("kd", p)])

        def hgrn_front_b(h, tb, main, p):
            w_tm = w_tmP[p]
            for j in range(4):
                S.op("pe", lambda e, j=j: e.transpose(out=p_tr[:, j, :], in_=w_vi[:, j * 128:(j + 1) * 128], identity=ident[:]), ["vi", "ident"], ["ptr"])
            for j in range(4):
                S.op("pe", lambda e, j=j: e.transpose(out=p_tr[:, 4 + j, :], in_=w_kk0[:, j * 128:(j + 1) * 128], identity=ident[:]), ["kk0", "ident"], ["ptr"])
            S.op("dve", lambda e: e.tensor_copy(out=w_tm[:, 0:8, :], in_=p_tr[:]), ["ptr"], [("tmA", p)])
            for j in range(4):
                S.op("pe", lambda e, j=j: e.transpose(out=p_tr[:, j, :], in_=w_kk1[:, j * 128:(j + 1) * 128], identity=ident[:]), ["kk1", "ident"], ["ptr"])
            S.op("dve", lambda e: e.tensor_copy(out=w_tm[:, 8:12, :], in_=p_tr[:, 0:4, :]), ["ptr"], [("tmB", p)])

        def hgrn_back(h, tb, main, p):
            tok0 = tb * 512
            w_qd, w_kd, w_tm, w_gs = w_qdP[p], w_kdP[p], w_tmP[p], w_gsP[p]
            for c in range(8):
                j, r = c // 2, c % 2
                bank = 1 + c // 4
                S.op("pe", lambda e, c=c, j=j, r=r, bank=bank: e.matmul(
                    p_acc[bank][:, (c % 4) * 128:(c % 4) * 128 + 128], lhsT=w_tm[:, 4 + 4 * r + j, :], rhs=w_tm[:, j, :],
                    start=True, stop=True), [("tmA", p), ("tmB", p)], [("acc", bank)])
            if main:
                for j in range(4):
                    S.op("pe", lambda e, j=j: e.matmul(p_acc[0][:, j * 128:(j + 1) * 128], lhsT=w_kd[:, j * 128:(j + 1) * 128], rhs=w_qd[:, j * 128:(j + 1) * 128],
                                                       start=True, stop=True), [("kd", p), ("qd", p)], [("acc", 0)])
                S.op("dve", lambda e: e.tensor_tensor(out=w_am, in0=p_acc[0][:], in1=cmask[:], op=ALU.mult), [("acc", 0), "cmask"], ["am"])
            if main:
                S.op("pool", lambda e: e.tensor_copy(out=Sb[:, 0, :], in_=Sfin[:, h, :]), [("S", h)], [("Sb", 0)])
            for c in range(8):
                bank = 1 + c // 4
                src, skey = (Sfin[:, h, :], ("S", h)) if c == 0 else (Sall[:, c, :], ("Sall", c))
                dst, dkey = (Sfin[:, h, :], ("S", h)) if c == 7 else (Sall[:, c + 1, :], ("Sall", c + 1))
                S.op("dve", lambda e, c=c, bank=bank, src=src, dst=dst: e.scalar_tensor_tensor(
                    out=dst, in0=src, scalar=w_eb8[:, p, c:c + 1], in1=p_acc[bank][:, (c % 4) * 128:(c % 4) * 128 + 128],
                    op0=ALU.mult, op1=ALU.add), [skey, ("eb8", p), ("acc", bank)], [dkey])
                if main and c == 3:
                    S.op("act", lambda e: e.copy(out=Sb[:, 1:5, :], in_=Sall[:, 1:5, :]), [("Sall", i) for i in range(1, 5)], [("Sb", i) for i in range(1, 5)])
                if main and c == 6:
                    S.op("act", lambda e: e.copy(out=Sb[:, 5:8, :], in_=Sall[:, 5:8, :]), [("Sall", i) for i in range(5, 8)], [("Sb", i) for i in range(5, 8)])

        def hgrn_back_b(h, tb, main, p):
            tok0 = tb * 512
            w_qd, w_kd, w_tm, w_gs = w_qdP[p], w_kdP[p], w_tmP[p], w_gsP[p]
            if not main:
                return
            for j in range(4):
                S.op("pe", lambda e, j=j: e.matmul(p_acc[3][:, j * 128:(j + 1) * 128], lhsT=w_tm[:, j, :], rhs=w_am[:, j * 128:(j + 1) * 128],
                                                   start=True, stop=False), [("tmA", p), "am"], [("acc", 3)])
                for r in range(2):
                    c = 2 * j + r
                    S.op("pe", lambda e, c=c, r=r: e.matmul(p_acc[3][:, c * 64:(c + 1) * 64], lhsT=Sb[:, c, :], rhs=w_qd[:, c * 64:(c + 1) * 64],
                                                            start=False, stop=(r == 1)), [("Sb", c), ("qd", p)], [("acc", 3)])
            S.op("act", lambda e: e.activation(out=w_am, in_=p_acc[3][:], func=AF.Square), [("acc", 3)], ["am"])
            S.op("pe", lambda e: e.matmul(p_acc[0][:], lhsT=onesB[:], rhs=w_am, start=True, stop=True), ["am", "onesB"], [("acc", 0)])
            S.op("act", lambda e: e.activation(out=w_X, in_=p_acc[0][:], func=AF.Ln, scale=1.0 / 128, bias=EPS), [("acc", 0)], ["X"])
            S.op("act", lambda e: e.activation(out=w_X, in_=w_X, func=AF.Exp, scale=-0.5), ["X"], ["X"])
            S.op("dve", lambda e: e.scalar_tensor_tensor(out=w_X, in0=p_acc[3][:], scalar=ggrc[:, 0:1], in1=w_X, op0=ALU.mult, op1=ALU.mult),
                 [("acc", 3), "X", "ggrc"], ["X"])
            ms = (h * 4 + tb) % 2
            S.op("dve", lambda e: e.tensor_tensor(out=mst[:, ms, :], in0=w_X, in1=w_gs, op=ALU.mult), ["X", ("gs", p)], [("mst", ms)])
            dma("sp", md[8 + h, :, tok0:tok0 + 512], mst[:, ms, :], [("mst", ms)], [("md", 8 + h, tb)], ("mst", ms))

        def hist_kv(h, slot):
            bk = gu_bank()
            proj(slot, 256, 1536, bk)
            S.op("act", lambda e: e.copy(out=w_kk0, in_=p_gu[bk][:]), [("gu", bk)], ["kk0"])
            dma("sp", kvd[h, :, 0:512], w_kk0, ["kk0"], [("kvd", h)], "kvd_k")
            bv = gu_bank()
            proj(slot, 384, 1536, bv)
            S.op("act", lambda e: e.copy(out=w_kk1, in_=p_gu[bv][:]), [("gu", bv)], ["kk1"])
            for j in range(4):
                S.op("pe", lambda e, j=j: e.transpose(out=p_tr[:, j, :], in_=w_kk1[:, j * 128:(j + 1) * 128], identity=ident[:]), ["kk1", "ident"], ["ptr"])
            S.op("dve", lambda e: e.tensor_copy(out=w_vi.rearrange("p (a b) -> p a b", a=4), in_=p_tr[:, 0:4, :]), ["ptr"], ["vi"])
            dma("sp", kvd[h, :, 512:1024], w_vi, ["vi"], [("kvd", h)], "kvd_v")

        def hgrn_phase(main):
            slots = {}

            def load_w(h):
                slot = wslot()
                slots[h] = slot
                if main:
                    for gi in range(4):
                        wload_cols(slot, gi * 128, w_in, 3072 + gi * 1024 + h * 128, 128)
                else:
                    wload_cols(slot, 0, w_in, 3072 + 1024 + h * 128, 128)
                    wload_cols(slot, 128, w_in, 3072 + 2048 + h * 128, 128)
                    wload_cols(slot, 256, w_in, 1024 + h * 128, 128)
                    wload_cols(slot, 384, w_in, 2048 + h * 128, 128)
            units = [(h, tb) for h in range(8) for tb in range(4)]
            load_w(0)
            for n, (h, tb) in enumerate(units):
                if tb == 0 and h + 1 < 8:
                    load_w(h + 1)
                if n > 0:
                    ph, ptb = units[n - 1]
                    hgrn_back(ph, ptb, main, (n - 1) % 2)
                hgrn_front(h, tb, slots[h], main, n % 2)
                hgrn_front_b(h, tb, main, n % 2)
                if n > 0:
                    hgrn_back_b(ph, ptb, main, (n - 1) % 2)
                if (not main) and tb == 3:
                    hist_kv(h, slots[h])
            ph, ptb = units[-1]
            hgrn_back(ph, ptb, main, (len(units) - 1) % 2)
            hgrn_back_b(ph, ptb, main, (len(units) - 1) % 2)

        hgrn_phase(main=False)
        if debug and "S" in debug:
            dma("sp", dbg["S"], Sfin, [("S", h) for h in range(8)], [], "dbgS")
            finals.append(len(S.ops) - 1)
        S.barrier()

        build_tab()
        for ti in range(16):
            load_norm_x(16 + ti, ti * 128)
        S.barrier()
        S.op("dve", lambda e: e.memset(a_v, 1.0), [], ["a_v_init"])

        def attention_head(h):
            slot = wslot()
            wload_cols(slot, 0, w_in, h * 128, 128)
            wload_cols(slot, 128, w_in, 1024 + h * 128, 128)
            wload_cols(slot, 256, w_in, 2048 + h * 128, 128)
            dma("sp", a_k[:, 0:512], kvd[h, :, 0:512], [("kvd", h)], [("a_k", 0)], "akh")
            dma("sp", a_v[:, 0:4, 0:128], kvd[h, :, 512:1024].rearrange("p (a b) -> p a b", a=4), [("kvd", h), "a_v_init"], [("a_v", 0)], "avh")
            for tb in range(4):
                b = gu_bank()
                proj(slot, 0, tb * 512, b)
                S.op("act", lambda e, b=b, tb=tb: e.copy(out=a_q[:, tb * 512:(tb + 1) * 512], in_=p_gu[b][:]), [("gu", b)], [("a_q", tb)])
                b = gu_bank()
                proj(slot, 128, tb * 512, b)
                S.op("act", lambda e, b=b, tb=tb: e.copy(out=a_k[:, 512 + tb * 512:512 + (tb + 1) * 512], in_=p_gu[b][:]), [("gu", b)], [("a_k", 1 + tb)])
                b = gu_bank()
                proj(slot, 256, tb * 512, b)
                vs = tb % 2
                S.op("act", lambda e, b=b, vs=vs: e.copy(out=a_vT[:, vs, :], in_=p_gu[b][:]), [("gu", b)], [("a_vT", vs)])
                for j in range(4):
                    S.op("pe", lambda e, j=j, vs=vs: e.transpose(out=p_tr[:, j, :], in_=a_vT[:, vs, j * 128:(j + 1) * 128], identity=ident[:]),
                         [("a_vT", vs), "ident"], ["ptr"])
                S.op("dve", lambda e, tb=tb: e.tensor_copy(out=a_v[:, 4 + tb * 4:8 + tb * 4, 0:128], in_=p_tr[:, 0:4, :]), ["ptr", "a_v_init"], [("a_v", 1 + tb)])
            def sbanks(qt):
                if qt % 2 == 0:
                    return (p_acc[0], ("acc", 0)), (p_acc[1], ("acc", 1)), (p_acc[2], ("acc", 2))
                return (p_gu[0], ("gu", 0)), (p_gu[1], ("gu", 1)), (p_acc[3], ("acc", 3))

            def att_S(qt):
                (bA, kA), (bB, kB), _ = sbanks(qt)
                kreads = sorted(set(("a_k", (qt + kt) // 4) for kt in range(5)))
                for kt in range(5):
                    bank, bkey = (bA, kA) if kt < 4 else (bB, kB)
                    col = (kt % 4) * 128
                    S.op("pe", lambda e, kt=kt, bank=bank, col=col: e.matmul(
                        bank[:, col:col + 128], lhsT=a_k[:, (qt + kt) * 128:(qt + kt + 1) * 128], rhs=a_q[:, qt * 128:(qt + 1) * 128],
                        start=True, stop=False), kreads + [("a_q", qt // 4)], [bkey])
                    S.op("pe", lambda e, kt=kt, bank=bank, col=col: e.matmul(
                        bank[:, col:col + 128], lhsT=ident[:], rhs=tab[:, h, kt * 128:(kt + 1) * 128],
                        start=False, stop=True), [("tab", h), "ident"], [bkey])

            def att_rest(qt):
                pp = qt % 2
                (bA, kA), (bB, kB), (bP, kP) = sbanks(qt)
                vreads = sorted(set(("a_v", (qt + kt) // 4) for kt in range(5)))
                nh = max(0, min(4, 4 - qt))
                if nh > 0:
                    S.op("act", lambda e: e.activation(out=a_p[:, pp, 0:nh * 128], in_=bA[:, 0:nh * 128], func=AF.Exp, scale=SCALE, bias=hneg[:, 0:1]),
                         [kA, "hneg"], [("a_p", pp)])
                if nh < 4:
                    S.op("act", lambda e: e.activation(out=a_p[:, pp, nh * 128:512], in_=bA[:, nh * 128:512], func=AF.Exp, scale=SCALE),
                         [kA], [("a_p", pp)])
                S.op("act", lambda e: e.activation(out=a_p[:, pp, 512:640], in_=bB[:, 0:128], func=AF.Exp, scale=SCALE),
                     [kB], [("a_p", pp)])
                for kt in range(5):
                    S.op("pe", lambda e, kt=kt: e.matmul(bP[:, 0:129], lhsT=a_p[:, pp, kt * 128:(kt + 1) * 128], rhs=a_v[:, qt + kt, 0:129],
                                                         start=(kt == 0), stop=(kt == 4)), [("a_p", pp)] + vreads, [kP])
                S.op("dve", lambda e: e.reciprocal(out=a_rd[:, pp:pp + 1], in_=bP[:, 128:129]), [kP], [("a_rd", pp)])
                S.op("dve", lambda e: e.tensor_scalar(out=a_ab[:, pp, :], in0=bP[:, 0:128], scalar1=a_rd[:, pp:pp + 1], scalar2=None, op0=ALU.mult),
                     [kP, ("a_rd", pp)], [("a_ab", pp)])
                S.op("act", lambda e: e.activation(out=a_jk, in_=a_ab[:, pp, :], func=AF.Square, accum_out=ssa[:, qt, h:h + 1]),
                     [("a_ab", pp)], ["a_jk", ("ssa", qt, h)])
                S.op("pe", lambda e: e.transpose(out=p_tr[:, qt % 4, :], in_=a_ab[:, pp, :], identity=ident[:]), [("a_ab", pp), "ident"], ["ptr"])
                if qt % 4 == 3:
                    g = qt // 4
                    ms = (h * 4 + g) % 2
                    S.op("dve", lambda e: e.tensor_scalar(out=mst_a[:, ms, :].rearrange("p (a b) -> p a b", a=4), in0=p_tr[:, 0:4, :], scalar1=gatc[:, h:h + 1], scalar2=None, op0=ALU.mult),
                         ["ptr", "gatc"], [("mst", ms)])
                    dma("sp", md[h, :, g * 512:(g + 1) * 512], mst_a[:, ms, :], [("mst", ms)], [("md", h, g)], ("mst", ms))

            att_S(0)
            for qt_ in range(16):
                if qt_ + 1 < 16:
                    att_S(qt_ + 1)
                att_rest(qt_)


        if stop_after != "H":
            for h in range(int(os.environ.get('ATT_HEADS', '8'))):
                attention_head(h)
            S.barrier()
        if stop_after not in ("H", "att"):
            S.op("dve", lambda e: e.memset(w_e0, 0.0), [], ["e0"])
            S.op("dve", lambda e: e.memset(w_e1, 0.0), [], ["e1"])
            hgrn_phase(main=True)
            S.barrier()

        if stop_after is None:
            S.op("dve", lambda e: e.tensor_reduce(out=rsa[:], in_=ssa[:], axis=AX.X, op=ALU.add), [], ["rsa"])
            S.op("act", lambda e: e.activation(out=rsa[:], in_=rsa[:], func=AF.Sqrt, scale=1.0 / 1024, bias=EPS), ["rsa"], ["rsa"])
            S.op("dve", lambda e: e.reciprocal(out=rsa[:], in_=rsa[:]), ["rsa"], ["rsa"])
            dma("sp", mb, md[:, :, 0:TB].rearrange("k p t -> p k t"), [], ["mb"], "mb")
            for blk in range(NT // TB):
                t0 = blk * TB
                for tt in range(4):
                    dma("sp", x1[:, tt, :], xw[NHIST + t0 + tt * 128:NHIST + t0 + (tt + 1) * 128, :], [], [("x1", tt)], ("x1", tt))
                for dblk in range(4):
                    slot = wslot()
                    wload_cols(slot, 0, w_out, dblk * 512, 512)
                    wv = wview(slot)
                    for tt in range(4):
                        qt = blk * 4 + tt
                        bA, bR = (0, 1) if tt % 2 == 0 else (2, 3)
                        for k in range(16):
                            bank = bA if k < 8 else bR
                            S.op("pe", lambda e, k=k, tt=tt, bank=bank, wv=wv: e.matmul(
                                p_acc[bank][:], lhsT=mb[:, k, tt * 128:(tt + 1) * 128], rhs=wv[:, k, :], start=(k % 8 == 0), stop=(k % 8 == 7)),
                                ["mb"] + wkeys(slot), [("acc", bank)])
                        S.op("dve", lambda e, tt=tt, qt=qt, bA=bA, dblk=dblk: e.scalar_tensor_tensor(
                            out=x1[:, tt, dblk * 512:(dblk + 1) * 512], in0=p_acc[bA][:], scalar=rsa[:, qt:qt + 1], in1=x1[:, tt, dblk * 512:(dblk + 1) * 512],
                            op0=ALU.mult, op1=ALU.add), [("acc", bA), "rsa", ("x1", tt)], [("x1", tt)])
                        S.op("dve", lambda e, tt=tt, bR=bR, dblk=dblk: e.tensor_tensor(
                            out=x1[:, tt, dblk * 512:(dblk + 1) * 512], in0=x1[:, tt, dblk * 512:(dblk + 1) * 512], in1=p_acc[bR][:], op=ALU.add),
                            [("acc", bR), ("x1", tt)], [("x1", tt)])
                if blk + 1 < NT // TB:
                    dma("sp", mb, md[:, :, t0 + TB:t0 + 2 * TB].rearrange("k p t -> p k t"), [], ["mb"], "mb")
                if debug and "x1" in debug:
                    for tt in range(4):
                        dma("sp", dbg["x1"][t0 + tt * 128:t0 + (tt + 1) * 128, :], x1[:, tt, :], [("x1", tt)], [], ("dbgx1", tt))
                        finals.append(len(S.ops) - 1)
                for tt in range(4):
                    norm_to_T(x1[:, tt, :], [("x1", tt)], h2T, "h2T", tt * 128, g2c, "g2c")
                h2keys = tkeys("h2T", 0, 512)
                for fp in range(NFT // 2):
                    slot = wslot()
                    wload_cols(slot, 0, w_gate, fp * 256, 256)
                    wload_cols(slot, 256, w_up, fp * 256, 256)
                    wv = wview(slot)
                    for fi in range(2):
                        ft = fp * 2 + fi
                        bg = gu_bank()
                        for k in range(16):
                            S.op("pe", lambda e, k=k, fi=fi, bg=bg, wv=wv: e.matmul(p_gu[bg][:], lhsT=wv[:, k, fi * 128:(fi + 1) * 128], rhs=h2T[:, k, :],
                                                                                  start=(k == 0), stop=(k == 15)), [("w", slot, fi)] + h2keys, [("gu", bg)])
                        bu = gu_bank()
                        for k in range(16):
                            S.op("pe", lambda e, k=k, fi=fi, bu=bu, wv=wv: e.matmul(p_gu[bu][:], lhsT=wv[:, k, 256 + fi * 128:256 + (fi + 1) * 128], rhs=h2T[:, k, :],
                                                                                  start=(k == 0), stop=(k == 15)), [("w", slot, 2 + fi)] + h2keys, [("gu", bu)])
                        sl = ft % 2
                        S.op("act", lambda e, bg=bg, sl=sl: e.activation(out=silb[:, sl, :], in_=p_gu[bg][:], func=AF.Silu), [("gu", bg)], [("sil", sl)])
                        S.op("dve", lambda e, bu=bu, sl=sl, ft=ft: e.tensor_tensor(out=ffT[:, ft, :], in0=p_gu[bu][:], in1=silb[:, sl, :], op=ALU.mult),
                             [("gu", bu), ("sil", sl)], [("ffT", ft)])
                for dblk in range(4):
                    for fg, (f0, nf) in enumerate(((0, 16), (16, 16), (32, 12))):
                        slot = wslot()
                        wv = wview(slot)
                        dma("pool", wv[:, 0:nf, :], w_down[f0 * 128:(f0 + nf) * 128, dblk * 512:(dblk + 1) * 512].rearrange("(k p) c -> p k c", p=128),
                            [], wkeys(slot), ("w", slot, 0))
                        for fl in range(nf):
                            fc = f0 + fl
                            for tt in range(4):
                                S.op("pe", lambda e, fl=fl, fc=fc, tt=tt, wv=wv: e.matmul(p_acc[tt][:], lhsT=ffT[:, fc, tt * 128:(tt + 1) * 128], rhs=wv[:, fl, :],
                                                                                        start=(fc == 0), stop=(fc == NFT - 1)), wkeys(slot) + [("ffT", fc)], [("acc", tt)])
                    for tt in range(4):
                        S.op("dve", lambda e, tt=tt, dblk=dblk: e.tensor_tensor(
                            out=x1[:, tt, dblk * 512:(dblk + 1) * 512], in0=x1[:, tt, dblk * 512:(dblk + 1) * 512], in1=p_acc[tt][:], op=ALU.add),
                            [("acc", tt), ("x1", tt)], [("x1", tt)])
                for tt in range(4):
                    i = 32 + (blk * 4 + tt) % 32
                    S.op("act", lambda e, tt=tt, i=i: e.activation(out=junk[:], in_=x1[:, tt, :], func=AF.Square, accum_out=ss[:, i:i + 1]),
                         [("x1", tt)], ["junk", ("ss", i)])
                    S.op("act", lambda e, i=i: e.activation(out=sd[:, i:i + 1], in_=ss[:, i:i + 1], func=AF.Sqrt, scale=1.0 / D, bias=EPS), [("ss", i)], [("sd", i)])
                    S.op("dve", lambda e, i=i: e.reciprocal(out=rstd[:, i:i + 1], in_=sd[:, i:i + 1]), [("sd", i)], [("rstd", i)])
                    S.op("dve", lambda e, tt=tt, i=i: e.scalar_tensor_tensor(out=yo, in0=x1[:, tt, :], scalar=rstd[:, i:i + 1], in1=gfb[:],
                                                                           op0=ALU.mult, op1=ALU.mult), [("x1", tt), ("rstd", i), "gfb"], ["yo"])
                    dma("sp", y[t0 + tt * 128:t0 + (tt + 1) * 128, :], yo, ["yo"], [], "yo")
                    finals.append(len(S.ops) - 1)

        if debug and "md" in debug:
            S.barrier()
            for m in range(16):
                for g in range(4):
                    dma("sp", hb[:, 0:512], md[m, :, g * 512:(g + 1) * 512], [], ["hb"], "hbd")
                    S.op("dve", lambda e: e.tensor_copy(out=gfb[:, 0:512], in_=hb[:, 0:512]), ["hb"], ["gfb"])
                    dma("sp", dbg["md"][m, :, g * 512:(g + 1) * 512], gfb[:, 0:512], ["gfb"], [], "gfbd")
                    finals.append(len(S.ops) - 1)
        stats = S.emit(final_waits=finals)
    return nc, stats


def _consts():
    ident = np.eye(128, dtype=np.float32)
    kk = np.arange(128)[:, None]
    amask = np.zeros((128, 640), np.float32)
    iq = np.arange(128)[None, :]
    amask[:, 0:128] = np.where((iq >= 64) & (kk < 64), -1e5, 0.0)
    amask[:, 512:640] = np.where((iq < 64) & (kk >= 64), -1e5, 0.0)
    s = np.arange(128)[:, None]
    t = np.arange(128)[None, :]
    tri = ((s // 64 == t // 64) & (s <= t)).astype(np.float32)
    cmask = np.tile(tri, (1, 4))
    smask = np.ones((128, 512), np.float32)
    smask[:, 0::64] = 0.0
    return ident, amask, cmask, smask


def _bias_index():
    kt = np.arange(5)[None, :, None]
    kk = np.arange(128)[:, None, None]
    iq = np.arange(128)[None, None, :]
    dist = 512 - 128 * kt + iq - kk
    return (np.clip(dist, -128, 128) + 128).reshape(128, 640)


_PROG = {}


def kernel(x, norm1_gain, w_in, rel_bias, lower_bounds, grn_norm_gain, attn_out_gain, w_out, norm2_gain,
           w_gate, w_up, w_down, final_gain):
    x = np.asarray(x, np.float32)
    if "nc" not in _PROG:
        _PROG["nc"] = build_program()[0]
    nc = _PROG["nc"]
    ident, amask, cmask, smask = _consts()
    idx = _bias_index()
    btab = np.ascontiguousarray(np.asarray(rel_bias, np.float32)[0][:, idx])
    shared = {
        "w_in": np.ascontiguousarray(np.asarray(w_in, np.float32)[0]),
        "w_out": np.ascontiguousarray(np.asarray(w_out, np.float32)[0]),
        "w_gate": np.ascontiguousarray(np.asarray(w_gate, np.float32)[0]),
        "w_up": np.ascontiguousarray(np.asarray(w_up, np.float32)[0]),
        "w_down": np.ascontiguousarray(np.asarray(w_down, np.float32)[0]),
        "g1": np.asarray(norm1_gain, np.float32).reshape(1, D),
        "g2": np.asarray(norm2_gain, np.float32).reshape(1, D),
        "gf": np.asarray(final_gain, np.float32).reshape(1, D),
        "gat": np.asarray(attn_out_gain, np.float32).reshape(1, 1024),
        "ggr": np.asarray(grn_norm_gain, np.float32).reshape(1, 128),
        "lbnd": np.ascontiguousarray(np.asarray(lower_bounds, np.float32)),
        "btab": btab, "c_ident": ident, "c_amask": amask, "c_cmask": cmask, "c_smask": smask,
    }
    in_maps = []
    for c in range(8):
        b, half = c // 2, c % 2
        xwc = np.zeros((NHIST + NT, D), np.float32)
        if half == 1:
            xwc[:] = x[b]
        else:
            xwc[NHIST:] = x[b, :NT]
        m = dict(shared)
        m["xw"] = xwc
        m["c_hneg"] = np.full((128, 1), 0.0 if half == 1 else -30000.0, np.float32)
        in_maps.append(m)
    res = run_bass_kernel_spmd(nc, in_maps, core_ids=list(range(8)))
    out = np.empty((4, 4096, D), np.float32)
    for c in range(8):
        b, half = c // 2, c % 2
        out[b, half * NT:(half + 1) * NT] = res.results[c]["y"]
    return out
```
